# Optimizing a Trainium2 kernel written in Bass

```python
import math
import jax
import jax.numpy as jnp
from jax import lax
import numpy as np

D_MODEL = 1024
BATCH = 16
SEQ = 256
DEPTH = 4
DEC_BATCH = 4
DEC_SEQ = 1024
PAST_LEN = 512

GRID_W = 64
HEAD_DIM = 64
Q_BLOCK = 128
ROPE_THETA = 10000.0
EPS = 1e-6
NEG_INF = -1e30
A_HEADS = 4
A_QK = 2 * HEAD_DIM
A_V = 2 * HEAD_DIM
B_HEADS = 8
B_KV = 2
C_HEADS = 8
C_KV = 2
WINDOW = 128
N_BRANCH = 3
A_WIDTH = A_HEADS * A_V
B_WIDTH = B_HEADS * HEAD_DIM
C_WIDTH = C_HEADS * HEAD_DIM
D_FF = -(-8 * D_MODEL // (3 * 256)) * 256
N_MOD = 6
IN_SIZES = (A_HEADS * A_QK, A_HEADS * A_QK, A_HEADS * A_V,
            B_HEADS * HEAD_DIM, B_KV * HEAD_DIM, B_KV * HEAD_DIM,
            C_HEADS * HEAD_DIM, C_KV * HEAD_DIM, C_KV * HEAD_DIM,
            N_BRANCH * D_MODEL)
D_IN = sum(IN_SIZES)

kernel_name = 'hybrid_diffusion_prefix_trunk_step'


def rmsnorm(x, g):
    xf = x.astype(jnp.float32)
    y = xf * lax.rsqrt(jnp.mean(xf * xf, axis=-1, keepdims=True) + EPS)
    return (y * g.astype(jnp.float32)).astype(x.dtype)


def axial_rope(L, head_dim):
    rows = L // GRID_W
    row = jnp.repeat(jnp.arange(rows), GRID_W).astype(jnp.float32)
    col = jnp.tile(jnp.arange(GRID_W), rows).astype(jnp.float32)
    n = head_dim // 4
    inv = ROPE_THETA ** (-jnp.arange(n, dtype=jnp.float32) / n)
    ang = jnp.concatenate([row[:, None] * inv, col[:, None] * inv], axis=-1)
    return jnp.cos(ang), jnp.sin(ang)


def apply_rope(x, cos, sin):
    x1, x2 = jnp.split(x, 2, axis=-1)
    c = cos[None, :, None, :].astype(x.dtype)
    s = sin[None, :, None, :].astype(x.dtype)
    return jnp.concatenate([x1 * c - x2 * s, x2 * c + x1 * s], axis=-1)


def rope_halves(x, cos, sin):
    x1, x2 = jnp.split(x, 2, axis=-1)
    return jnp.concatenate([apply_rope(x1, cos, sin), apply_rope(x2, cos, sin)], axis=-1)


def to_groups(q, n_kv):
    B, L, H, D = q.shape
    return q.reshape(B, L, n_kv, H // n_kv, D)


def attend(q, k, v, bias=None, sink=None):
    s = jnp.einsum('bqhgd,bkhd->bhgqk', q, k).astype(jnp.float32) * (q.shape[-1] ** -0.5)
    if bias is not None:
        s = s + bias
    if sink is not None:
        sk = jnp.broadcast_to(sink.astype(jnp.float32)[None, :, :, None, None], s.shape[:-1] + (1,))
        p = jax.nn.softmax(jnp.concatenate([s, sk], axis=-1), axis=-1)[..., :-1]
    else:
        p = jax.nn.softmax(s, axis=-1)
    return jnp.einsum('bhgqk,bkhd->bqhgd', p.astype(v.dtype), v)


def diff_attend(q, k, v, lam):
    q1, q2 = jnp.split(q, 2, axis=-1)
    k1, k2 = jnp.split(k, 2, axis=-1)
    return attend(q1, k1, v) - lam.astype(v.dtype) * attend(q2, k2, v)


def sweep_queries(fn, q):
    B, L = q.shape[:2]
    nb = L // Q_BLOCK
    qb = jnp.moveaxis(q.reshape((B, nb, Q_BLOCK) + q.shape[2:]), 1, 0)
    out = lax.map(lambda a: fn(a[0], a[1]), (jnp.arange(nb), qb))
    return jnp.moveaxis(out, 0, 1).reshape((B, L) + out.shape[3:])


def project_heads(h, w_in):
    B, L, _ = h.shape
    idx = np.cumsum(IN_SIZES)[:-1].tolist()
    qa, ka, va, qb, kb, vb, qc, kc, vc, gates = jnp.split(h @ w_in, idx, axis=-1)
    return (qa.reshape(B, L, A_HEADS, A_QK), ka.reshape(B, L, A_HEADS, A_QK), va.reshape(B, L, A_HEADS, A_V),
            qb.reshape(B, L, B_HEADS, HEAD_DIM), kb.reshape(B, L, B_KV, HEAD_DIM), vb.reshape(B, L, B_KV, HEAD_DIM),
            qc.reshape(B, L, C_HEADS, HEAD_DIM), kc.reshape(B, L, C_KV, HEAD_DIM), vc.reshape(B, L, C_KV, HEAD_DIM),
            gates)


def merge_branches(out_a, out_b, out_c, gates, lp, lam_init):
    B, L = gates.shape[:2]
    oa = rmsnorm(out_a.reshape(B, L, A_HEADS, A_V), lp['a_subln_g']) * (1.0 - lam_init)
    ya = oa.reshape(B, L, A_WIDTH) @ lp['w_br_a']
    yb = out_b.reshape(B, L, B_WIDTH) @ lp['w_br_b']
    yc = out_c.reshape(B, L, C_WIDTH) @ lp['w_br_c']
    ga, gb, gc = jnp.split(jax.nn.sigmoid(gates), N_BRANCH, axis=-1)
    return (ga * ya + gb * yb + gc * yc) @ lp['w_out']


def context_mixers(h, lp, lam, lam_init):
    qa, ka, va, qb, kb, vb, qc, kc, vc, gates = project_heads(h, lp['w_in'])
    qb = rmsnorm(qb, lp['b_qnorm_g'])
    kb = rmsnorm(kb, lp['b_knorm_g'])
    sink = lp['c_sink'].reshape(C_KV, C_HEADS // C_KV)
    out_a = sweep_queries(lambda i, q: diff_attend(q, ka, va, lam), to_groups(qa, A_HEADS))
    out_b = sweep_queries(lambda i, q: attend(q, kb, vb), to_groups(qb, B_KV))
    out_c = sweep_queries(lambda i, q: attend(q, kc, vc, sink=sink), to_groups(qc, C_KV))
    merged = merge_branches(out_a, out_b, out_c, gates, lp, lam_init)
    return merged, (ka, va, kb, vb, kc, vc)


def latent_mixers(h, lp, lam, lam_init, cache, cos, sin):
    ctx_ka, ctx_va, ctx_kb, ctx_vb, ctx_kc, ctx_vc = cache
    qa, ka, va, qb, kb, vb, qc, kc, vc, gates = project_heads(h, lp['w_in'])
    L = h.shape[1]
    P = ctx_kc.shape[1]
    qa = rope_halves(qa, cos, sin)
    ka = rope_halves(ka, cos, sin)
    qb = apply_rope(rmsnorm(qb, lp['b_qnorm_g']), cos, sin)
    kb = apply_rope(rmsnorm(kb, lp['b_knorm_g']), cos, sin)
    qc = apply_rope(qc, cos, sin)
    kc = apply_rope(kc, cos, sin)
    sink = lp['c_sink'].reshape(C_KV, C_HEADS // C_KV)
    ka_all = jnp.concatenate([ctx_ka, ka], axis=1)
    va_all = jnp.concatenate([ctx_va, va], axis=1)
    kb_all = jnp.concatenate([ctx_kb, kb], axis=1)
    vb_all = jnp.concatenate([ctx_vb, vb], axis=1)
    out_a = sweep_queries(lambda i, q: diff_attend(q, ka_all, va_all, lam), to_groups(qa, A_HEADS))
    out_b = sweep_queries(lambda i, q: attend(q, kb_all, vb_all), to_groups(qb, B_KV))
    pad = ((0, 0), (WINDOW, WINDOW), (0, 0), (0, 0))
    kc_pad = jnp.pad(kc, pad)
    vc_pad = jnp.pad(vc, pad)
    KW = Q_BLOCK + 2 * WINDOW
    qi = jnp.arange(Q_BLOCK)[:, None]
    kj = jnp.arange(KW)[None, :]
    ctx_bias = jnp.zeros((Q_BLOCK, P), jnp.float32)

    def c_block(i, q):
        start = i * Q_BLOCK
        kw = lax.dynamic_slice_in_dim(kc_pad, start, KW, axis=1)
        vw = lax.dynamic_slice_in_dim(vc_pad, start, KW, axis=1)
        kpos = start - WINDOW + kj
        valid = (kj - qi >= 0) & (kj - qi <= 2 * WINDOW) & (kpos >= 0) & (kpos < L)
        bias = jnp.concatenate([ctx_bias, jnp.where(valid, 0.0, NEG_INF).astype(jnp.float32)], axis=-1)
        return attend(q, jnp.concatenate([ctx_kc, kw], axis=1), jnp.concatenate([ctx_vc, vw], axis=1),
                      bias=bias, sink=sink)

    out_c = sweep_queries(c_block, to_groups(qc, C_KV))
    merged = merge_branches(out_a, out_b, out_c, gates, lp, lam_init)
    return merged, ()


def modulation(cvec, w_mod, b_mod):
    m = jax.nn.silu(cvec) @ w_mod + b_mod
    return [t[:, None, :] for t in jnp.split(m, N_MOD, axis=-1)]


def swiglu(h, w_ffn_in, w_ffn_out):
    a, b = jnp.split(h @ w_ffn_in, 2, axis=-1)
    return (jax.nn.silu(a) * b) @ w_ffn_out


def trunk_layer(x, cvec, lp, mix_fn):
    sh1, sc1, g1, sh2, sc2, g2 = modulation(cvec, lp['w_mod'], lp['b_mod'])
    mixed, kv = mix_fn(rmsnorm(x, lp['norm1_g']) * (1.0 + sc1) + sh1)
    x = x + g1 * mixed
    x = x + g2 * swiglu(rmsnorm(x, lp['norm2_g']) * (1.0 + sc2) + sh2, lp['w_ffn_in'], lp['w_ffn_out'])
    return x, kv


def setup_inputs(seed: int = 0) -> dict:
    key = jax.random.key(seed)
    ks = iter(jax.random.split(key, 40))

    def nrm(shape, scale=1.0):
        return jax.random.normal(next(ks), shape, jnp.float32) * scale

    fd = D_MODEL ** -0.5
    return {
        'x_prompt': nrm((BATCH, SEQ, D_MODEL)),
        'x_sample': nrm((DEC_BATCH, DEC_SEQ, D_MODEL)),
        'cache_a_k': nrm((DEC_BATCH, DEPTH, PAST_LEN, A_HEADS, A_QK)),
        'cache_a_v': nrm((DEC_BATCH, DEPTH, PAST_LEN, A_HEADS, A_V)),
        'cache_b_k': nrm((DEC_BATCH, DEPTH, PAST_LEN, B_KV, HEAD_DIM)),
        'cache_b_v': nrm((DEC_BATCH, DEPTH, PAST_LEN, B_KV, HEAD_DIM)),
        'cache_c_k': nrm((DEC_BATCH, DEPTH, PAST_LEN, C_KV, HEAD_DIM)),
        'cache_c_v': nrm((DEC_BATCH, DEPTH, PAST_LEN, C_KV, HEAD_DIM)),
        'c': nrm((DEC_BATCH, D_MODEL)),
        'c_ctx': nrm((D_MODEL,)),
        'w_mod': nrm((DEPTH, D_MODEL, N_MOD * D_MODEL), fd),
        'b_mod': nrm((DEPTH, N_MOD * D_MODEL), 0.02),
        'norm1_g': 1.0 + nrm((DEPTH, D_MODEL), 0.02),
        'norm2_g': 1.0 + nrm((DEPTH, D_MODEL), 0.02),
        'w_in': nrm((DEPTH, D_MODEL, D_IN), fd),
        'a_lam_q1': nrm((DEPTH, HEAD_DIM), 0.1),
        'a_lam_k1': nrm((DEPTH, HEAD_DIM), 0.1),
        'a_lam_q2': nrm((DEPTH, HEAD_DIM), 0.1),
        'a_lam_k2': nrm((DEPTH, HEAD_DIM), 0.1),
        'a_subln_g': 1.0 + nrm((DEPTH, A_V), 0.02),
        'b_qnorm_g': 1.0 + nrm((DEPTH, HEAD_DIM), 0.02),
        'b_knorm_g': 1.0 + nrm((DEPTH, HEAD_DIM), 0.02),
        'c_sink': nrm((DEPTH, C_HEADS), 0.5),
        'w_br_a': nrm((DEPTH, A_WIDTH, D_MODEL), A_WIDTH ** -0.5),
        'w_br_b': nrm((DEPTH, B_WIDTH, D_MODEL), B_WIDTH ** -0.5),
        'w_br_c': nrm((DEPTH, C_WIDTH, D_MODEL), C_WIDTH ** -0.5),
        'w_out': nrm((DEPTH, D_MODEL, D_MODEL), fd),
        'w_ffn_in': nrm((DEPTH, D_MODEL, 2 * D_FF), fd),
        'w_ffn_out': nrm((DEPTH, D_FF, D_MODEL), D_FF ** -0.5),
        'final_g': 1.0 + nrm((D_MODEL,), 0.02),
    }


def reference(x_prompt, x_sample, cache_a_k, cache_a_v, cache_b_k, cache_b_v, cache_c_k, cache_c_v,
              c, c_ctx, w_mod, b_mod, norm1_g, norm2_g, w_in, a_lam_q1, a_lam_k1, a_lam_q2, a_lam_k2,
              a_subln_g, b_qnorm_g, b_knorm_g, c_sink, w_br_a, w_br_b, w_br_c, w_out, w_ffn_in,
              w_ffn_out, final_g):
    cos, sin = axial_rope(x_sample.shape[1], HEAD_DIM)
    xp = x_prompt
    xs = x_sample
    ctx_cvec = c_ctx[None, :]
    st_ak, st_av, st_bk, st_bv, st_ck, st_cv = [], [], [], [], [], []
    for l in range(DEPTH):
        lp = {'w_mod': w_mod[l], 'b_mod': b_mod[l], 'norm1_g': norm1_g[l], 'norm2_g': norm2_g[l],
              'w_in': w_in[l], 'a_subln_g': a_subln_g[l], 'b_qnorm_g': b_qnorm_g[l],
              'b_knorm_g': b_knorm_g[l], 'c_sink': c_sink[l], 'w_br_a': w_br_a[l], 'w_br_b': w_br_b[l],
              'w_br_c': w_br_c[l], 'w_out': w_out[l], 'w_ffn_in': w_ffn_in[l], 'w_ffn_out': w_ffn_out[l]}
        lam_init = 0.8 - 0.6 * math.exp(-0.3 * l)
        lam = (jnp.exp(jnp.sum(a_lam_q1[l].astype(jnp.float32) * a_lam_k1[l].astype(jnp.float32)))
               - jnp.exp(jnp.sum(a_lam_q2[l].astype(jnp.float32) * a_lam_k2[l].astype(jnp.float32)))
               + lam_init)
        xp, kv = trunk_layer(xp, ctx_cvec, lp, lambda h: context_mixers(h, lp, lam, lam_init))
        st_ak.append(kv[0]); st_av.append(kv[1]); st_bk.append(kv[2])
        st_bv.append(kv[3]); st_ck.append(kv[4]); st_cv.append(kv[5])
        cache = (cache_a_k[:, l], cache_a_v[:, l], cache_b_k[:, l], cache_b_v[:, l],
                 cache_c_k[:, l], cache_c_v[:, l])
        xs, _ = trunk_layer(xs, c, lp, lambda h: latent_mixers(h, lp, lam, lam_init, cache, cos, sin))
    y_prompt = rmsnorm(xp, final_g)
    y_sample = rmsnorm(xs, final_g)
    new_a_k = jnp.stack(st_ak, axis=1)
    new_a_v = jnp.stack(st_av, axis=1)
    new_b_k = jnp.stack(st_bk, axis=1)
    new_b_v = jnp.stack(st_bv, axis=1)
    new_c_k = jnp.stack(st_ck, axis=1)
    new_c_v = jnp.stack(st_cv, axis=1)
    return (y_prompt, y_sample, new_a_k, new_a_v, new_b_k, new_b_v, new_c_k, new_c_v)
```

```python
import math
import os
import contextlib
import numpy as np
import concourse.bass as bass
import concourse.mybir as mybir
from concourse.bass_utils import run_bass_kernel_spmd

F32 = mybir.dt.float32
BF16 = mybir.dt.bfloat16
ALU = mybir.AluOpType
AF = mybir.ActivationFunctionType
AX = mybir.AxisListType

D_MODEL = 1024
DEPTH = 4
T = 1024
PAST = 512
D_FF = 2816
NFF = 22
EPS = 1e-6
NEGBIG = -30000.0
WSLOT = 4608
NSLOT = 3

O_QA, O_KA, O_VA, O_QB, O_KB, O_VB, O_QC, O_KC, O_VC, O_G = 0, 512, 1024, 1536, 2048, 2176, 2304, 2816, 2944, 3072


def _pair_perm():
    idx = []
    for j in range(4):
        idx += list(range(j * 64, j * 64 + 64)) + list(range((j + 4) * 64, (j + 4) * 64 + 64))
    return np.array(idx)


def qkv_group_cols():
    pp = _pair_perm()
    g = []
    g.append(np.arange(O_QA, O_QA + 512))
    g.append(np.arange(O_KA, O_KA + 512))
    g.append(np.arange(O_VA, O_VA + 512))
    g.append(O_QB + pp)
    g.append(O_QC + pp)
    g.append(np.concatenate([np.arange(O_KB, O_KB + 128), np.arange(O_KC, O_KC + 128),
                             np.arange(O_VB, O_VB + 128), np.arange(O_VC, O_VC + 128)]))
    return g


def weight_plan(depth):
    plan = []
    for l in range(depth):
        for jg in range(12):
            plan.append(('mod', l, jg, 4096))
        for g in (0, 1, 2, 3, 5, 4):
            plan.append(('qkv', l, g, 4096))
        for oc in range(8):
            plan.append(('mrg', l, oc, 4608))
        for og in range(2):
            plan.append(('out', l, og, 4096))
        for jb in range(11):
            plan.append(('ffi', l, jb, 4096))
        for oc in range(8):
            plan.append(('ffo', l, oc, 2816))
    return plan


def _kc(w, cols):
    K = w.shape[0] // 128
    return w[:, cols].reshape(K, 128, len(cols)).transpose(1, 0, 2)


def pack_weights(inp, depth):
    plan = weight_plan(depth)
    tot = sum(b[3] for b in plan)
    W = np.empty((128, tot), np.float32)
    gcols = qkv_group_cols()
    pp = _pair_perm()
    off = 0
    for (kind, l, i, E) in plan:
        if kind == 'mod':
            blk = _kc(inp['w_mod'][l], np.arange(i * 512, i * 512 + 512))
        elif kind == 'qkv':
            blk = _kc(inp['w_in'][l], gcols[i])
        elif kind == 'mrg':
            cs = np.arange(i * 128, i * 128 + 128)
            parts = [_kc(inp['w_br_a'][l], cs),
                     _kc(inp['w_br_b'][l][pp], cs),
                     _kc(inp['w_br_c'][l][pp], cs),
                     _kc(inp['w_in'][l], O_G + cs),
                     _kc(inp['w_in'][l], O_G + 1024 + cs),
                     _kc(inp['w_in'][l], O_G + 2048 + cs)]
            blk = np.concatenate(parts, axis=1)
        elif kind == 'out':
            parts = [_kc(inp['w_out'][l], np.arange((i * 4 + o) * 128, (i * 4 + o) * 128 + 128)) for o in range(4)]
            blk = np.concatenate(parts, axis=1)
        elif kind == 'ffi':
            parts = []
            for jj in range(2):
                j = i * 2 + jj
                parts.append(_kc(inp['w_ffn_in'][l], np.arange(j * 128, j * 128 + 128)))
                parts.append(_kc(inp['w_ffn_in'][l], D_FF + np.arange(j * 128, j * 128 + 128)))
            blk = np.concatenate(parts, axis=1)
        elif kind == 'ffo':
            blk = _kc(inp['w_ffn_out'][l], np.arange(i * 128, i * 128 + 128))
        W[:, off:off + E] = blk.reshape(128, E)
        off += E
    return W


KSTOP = os.environ.get('KSTOP', '')


class _Stop(Exception):
    pass


def _chk(name):
    if KSTOP == name:
        raise _Stop()


class _Rec:
    def __init__(self):
        self.call = None

    def __getattr__(self, name):
        def f(*a, **k):
            self.call = (name, a, k)
            return self
        return f


class Prog:
    ENGS = ['tensor', 'vector', 'scalar', 'gpsimd', 'sync']

    def __init__(self, nc, stack):
        self.nc = nc
        self.stack = stack
        self.q = {e: [] for e in self.ENGS}
        self.epoch = 0
        self.esem = {}
        self.ecnt = {}
        self.dsem = {}
        self.buf = {}
        self.waited = {}
        self.pend = {e: ([], []) for e in self.ENGS}
        self.out_evs = {}

    def _deps(self, eng, reads, writes):
        best = {}

        def add(ev):
            sk, v, prod = ev
            if prod == 'tensor' and eng == 'tensor':
                return
            if best.get(sk, -1) < v:
                best[sk] = v
        for b in reads:
            st = self.buf.get(b)
            if st and st[0] is not None:
                add(st[0])
        for b in writes:
            st = self.buf.get(b)
            if st:
                if st[0] is not None:
                    add(st[0])
                for ev in st[1]:
                    add(ev)
        out = []
        for sk, v in best.items():
            key = (eng, sk)
            if self.waited.get(key, -1) >= v:
                continue
            self.waited[key] = v
            out.append((sk, v))
        return out

    def _commit(self, ev, reads, writes):
        for b in writes:
            self.buf[b] = [ev, []]
        for b in reads:
            st = self.buf.get(b)
            if st is None:
                st = self.buf[b] = [None, []]
            st[1].append(ev)

    def op(self, eng, fn, reads=(), writes=(), sig=True):
        reads = list(reads)
        writes = list(writes)
        waits = self._deps(eng, reads, writes)
        rec = _Rec()
        fn(rec)
        fn = rec.call
        assert fn is not None
        if not sig:
            self.q[eng].append((fn, waits, None))
            self.pend[eng][0].extend(reads)
            self.pend[eng][1].extend(writes)
            return None
        ek = 'E%s@%d' % (eng, self.epoch)
        if ek not in self.esem:
            self.esem[ek] = self.stack.enter_context(self.nc.semaphore("es_%s_%d" % (eng, self.epoch)))
            self.ecnt[ek] = 0
        self.ecnt[ek] += 1
        ev = (ek, self.ecnt[ek], eng)
        self.q[eng].append((fn, waits, ('E', ek)))
        pr, pw = self.pend[eng]
        self._commit(ev, reads + pr, writes + pw)
        self.pend[eng] = ([], [])
        return ev

    def dma(self, eng, out, in_, sem, reads=(), writes=(), is_out=False):
        waits = self._deps(eng, reads, writes)
        sem = '%s@%d' % (sem, self.epoch)
        if sem not in self.dsem:
            self.dsem[sem] = [self.stack.enter_context(self.nc.semaphore("ds_" + sem.replace('@', '_'))), 0]
        s = self.dsem[sem]
        s[1] += 16
        ev = ('D' + sem, s[1], 'dma')
        self.q[eng].append((('dma_start', (), {'out': out, 'in_': in_}), waits, ('D', sem)))
        self._commit(ev, list(reads), list(writes))
        if is_out:
            self.out_evs[sem] = ev
        return ev

    def _semh(self, sk):
        if sk[0] == 'E':
            return self.esem[sk]
        return self.dsem[sk[1:]][0]

    def emit(self):
        nc = self.nc
        self.q['sync'].append((None, [(ev[0], ev[1]) for ev in self.out_evs.values()], None))
        with nc.Block() as block:
            for ename in self.ENGS:
                ops = self.q[ename]

                def body(eng, ops=ops):
                    for (fn, waits, inc) in ops:
                        for (sk, v) in waits:
                            eng.wait_ge(self._semh(sk), v)
                        if fn is None:
                            continue
                        ins = getattr(eng, fn[0])(*fn[1], **fn[2])
                        if inc is None:
                            continue
                        if inc[0] == 'E':
                            ins.then_inc(self.esem[inc[1]], 1)
                        else:
                            ins.then_inc(self.dsem[inc[1]][0], 16)
                getattr(block, ename)(body)


RB = 512
R_QT, R_KT, R_V, R_OT = 0, 4096, 13312, 25600
R_TOT = 37888
VW = 1024


def rid(lo, hi):
    return ['R%d' % i for i in range(lo // RB, (hi - 1) // RB + 1)]


def build_program(depth=DEPTH):
    nc = bass.Bass("TRN2", target_bir_lowering=False)
    plan = weight_plan(depth)
    wtot = sum(b[3] for b in plan)

    def din(name, shape):
        return nc.dram_tensor(name, list(shape), F32, kind="ExternalInput").ap()

    def dout(name, shape):
        return nc.dram_tensor(name, list(shape), F32, kind="ExternalOutput").ap()

    W = din("W", [128, wtot])
    xT_in = din("xT_in", [1024, T])
    cvec_d = din("cvec", [128, 8])
    cKT = din("cKT", [depth, 768, PAST])
    cV = din("cV", [depth, PAST, VW])
    m01_d = din("m01", [128, 32])
    cos_d = din("cos_t", [T, 32])
    sin_d = din("sin_t", [T, 32])
    maskb_d = din("maskb", [128, 48])
    cmask_d = din("cmask", [128, 8 * 2 * 128])
    ident_d = din("ident", [128, 128])
    bmod_d = din("bmodT", [128, depth * 48])
    g1_d = din("g1T", [128, depth * 8])
    g2_d = din("g2T", [128, depth * 8])
    gfin_d = din("gfinT", [128, 8])
    subln_d = din("sublnT", [128, depth])
    gq_d = din("gqB", [128, depth * 64])
    gk_d = din("gkB", [128, depth * 64])
    lam_d = din("lamB", [128, 4 * depth * 64])
    sink_d = din("sinkB", [128, depth * 8])

    yT_out = dout("yT", [1024, T])
    nk_a = dout("nk_a", [depth, T, 512])
    nv_a = dout("nv_a", [depth, T, 512])
    nk_b = dout("nk_b", [depth, T, 128])
    nv_b = dout("nv_b", [depth, T, 128])
    nk_c = dout("nk_c", [depth, T, 128])
    nv_c = dout("nv_c", [depth, T, 128])

    with contextlib.ExitStack() as st:
        def sb(name, shape, dt):
            return st.enter_context(nc.sbuf_tensor(name, list(shape), dt))

        def psum(name, shape, dt):
            return st.enter_context(nc.psum_tensor(name, list(shape), dt))

        xT = sb("xT", [128, 8, T], F32)
        hT = sb("hT", [128, 8, T], BF16)
        R = sb("R", [128, R_TOT], BF16)
        wsl = [sb("wsl%d" % i, [128, WSLOT], BF16) for i in range(NSLOT)]
        cos_s = sb("cos_s", [128, 8, 32], F32)
        sin_s = sb("sin_s", [128, 8, 32], F32)
        maskb = sb("maskb_s", [128, 48], F32)
        cmask = sb("cmask_s", [128, 8, 2, 128], BF16)
        ident = sb("ident_s", [128, 128], BF16)
        ones_f = sb("ones_f", [128, 128], F32)
        ones_b = sb("ones_b", [128, 128], BF16)
        bmodT = sb("bmodT_s", [128, depth * 48], F32)
        g1T = sb("g1T_s", [128, depth * 8], F32)
        g2T = sb("g2T_s", [128, depth * 8], F32)
        gfinT = sb("gfinT_s", [128, 8], F32)
        sublnT = sb("sublnT_s", [128, depth], F32)
        gqB = sb("gqB_s", [128, depth * 64], F32)
        gkB = sb("gkB_s", [128, depth * 64], F32)
        lamS = sb("lamS", [128, 2 * depth], F32)
        neglam = sb("neglam", [128, depth], F32)
        esink = sb("esink", [128, depth * 8], F32)
        cvec = sb("cvec_s", [128, 8], F32)
        scb = sb("scb", [128, 8], BF16)
        modT = sb("modT", [128, 48], F32)
        a1 = sb("a1", [128, 8], F32)
        a2 = sb("a2", [128, 8], F32)
        zero8 = sb("zero8", [128, 8], F32)
        rstd = sb("rstd", [128, 512], F32)
        lnv = sb("lnv", [128, 512], F32)
        sq = [sb("sq%d" % i, [128, 512], F32) for i in range(2)]
        tmpf = [sb("tmpf%d" % i, [128, 512], F32) for i in range(2)]
        rf = [sb("rf%d" % i, [128, 512], F32) for i in range(2)]
        rbt = [sb("rb%d" % i, [128, 512], BF16) for i in range(2)]
        vst = [sb("vst%d" % i, [128, 512], F32) for i in range(2)]
        xn = sb("xn", [128, 512], F32)
        rt = [sb("rt%d" % i, [128, 256], F32) for i in range(4)]
        ss8 = sb("ss8", [128, 16], F32)
        l8 = sb("l8", [128, 16], F32)
        r8 = sb("r8", [128, 16], F32)
        PP = sb("PP", [128, 2, 1024], BF16)
        m01 = sb("m01_s", [128, 32], F32)
        ar = sb("ar", [128, 512], F32)
        ao1 = sb("ao1", [128, 512], F32)
        ao2 = sb("ao2", [128, 512], F32)
        sg = sb("sg", [128, 512], F32)
        tm2 = sb("tm2", [128, 512], F32)

        acc = ao1
        yst = vst
        psS = [psum("psS%d" % i, [128, 1024], F32) for i in range(2)]
        psA = [psum("ps%d" % i, [128, 512], F32) for i in range(2, 6)]
        ps = [psS[0][:, 0:512], psS[0][:, 512:1024]] + psA + [psS[1][:, 0:512], psS[1][:, 512:1024]]
        psB = ps[7][:].bitcast(BF16)

        p = Prog(nc, st)

        qT = R[:, R_QT:R_QT + 4096].rearrange("p (c t) -> p c t", t=1024)
        kT = R[:, R_KT:R_KT + 9216].rearrange("p (c t) -> p c t", t=1536)
        Vv = R[:, R_V:R_V + 12 * VW].rearrange("p (k f) -> p k f", f=VW)
        OT = R[:, R_OT:R_OT + 12288].rearrange("p (c t) -> p c t", t=1024)
        mT = R[:, 0:8192].rearrange("p (c t) -> p c t", t=1024)
        uT = R[:, 0:22528].rearrange("p (c t) -> p c t", t=1024)

        def id_qT(c, lo=0, hi=1024):
            return rid(R_QT + c * 1024 + lo, R_QT + c * 1024 + hi)

        def id_kT(c, lo, hi):
            return rid(R_KT + c * 1536 + lo, R_KT + c * 1536 + hi)

        def id_V(kt, lo, hi):
            return rid(R_V + kt * VW + lo, R_V + kt * VW + hi)

        def id_OT(c, lo, hi):
            return rid(R_OT + c * 1024 + lo, R_OT + c * 1024 + hi)

        def id_mT(c, lo, hi):
            return rid(c * 1024 + lo, c * 1024 + hi)

        id_uT = id_mT

        for c in range(8):
            p.dma('sync', xT[:, c, :], xT_in[c * 128:(c + 1) * 128, :], 'ldx%d' % c, writes=['xT%d.0' % c, 'xT%d.1' % c])
        small = [(cvec, cvec_d, 'cvec'), (maskb, maskb_d, 'maskb'), (bmodT, bmod_d, 'bmodT'), (g1T, g1_d, 'g1T'),
                 (g2T, g2_d, 'g2T'), (gfinT, gfin_d, 'gfinT'), (sublnT, subln_d, 'sublnT'), (gqB, gq_d, 'gqB'),
                 (gkB, gk_d, 'gkB'), (esink, sink_d, 'esink')]
        for (t_, d_, n_) in small:
            p.dma('sync', t_[:], d_[:], 'lds_' + n_, writes=[n_])
        p.dma('sync', cos_s[:], cos_d.rearrange("(t p) d -> p t d", p=128), 'lds_cos', writes=['cos'])
        p.dma('sync', sin_s[:], sin_d.rearrange("(t p) d -> p t d", p=128), 'lds_sin', writes=['sin'])
        p.dma('gpsimd', ident[:], ident_d[:], 'ldc_i', writes=['ident'])
        p.dma('gpsimd', cmask[:].rearrange("p a b c -> p (a b c)"), cmask_d[:], 'ldc_m', writes=['cmask'])
        p.op('vector', lambda e: e.memset(ones_f[:], 1.0), writes=['ones_f'])
        p.op('vector', lambda e: e.memset(ones_b[:], 1.0), writes=['ones_b'])
        p.op('vector', lambda e: e.memset(zero8[:], 0.0), writes=['zero8'])
        p.op('scalar', lambda e: e.activation(scb[:], cvec[:], AF.Silu), reads=['cvec'], writes=['scb'])
        p.op('scalar', lambda e: e.activation(esink[:], esink[:], AF.Exp), reads=['esink'], writes=['esink'])
        p.op('vector', lambda e: e.memset(esink[0:64, :], 0.0), reads=['esink'], writes=['esink'])
        n64 = depth * 64
        p.dma('sync', tmpf[0][:, 0:2 * n64], lam_d[:, 0:2 * n64], 'lds_lam1', writes=['tmpf0'])
        p.dma('sync', tmpf[1][:, 0:2 * n64], lam_d[:, 2 * n64:4 * n64], 'lds_lam2', writes=['tmpf1'])
        p.dma('sync', m01[:], m01_d[:], 'lds_m01', writes=['m01'])
        p.op('vector', lambda e: e.tensor_tensor(sq[0][:, 0:n64], tmpf[0][:, 0:n64], tmpf[0][:, n64:2 * n64], ALU.mult),
             reads=['tmpf0'], writes=['sq0'])
        p.op('vector', lambda e: e.tensor_tensor(sq[0][:, n64:2 * n64], tmpf[1][:, 0:n64], tmpf[1][:, n64:2 * n64], ALU.mult),
             reads=['tmpf1', 'sq0'], writes=['sq0'])
        p.op('vector', lambda e: e.tensor_reduce(lamS[:], sq[0][:, 0:2 * n64].rearrange("p (a d) -> p a d", d=64), AX.X, ALU.add),
             reads=['sq0'], writes=['lamS'])
        p.op('scalar', lambda e: e.activation(lamS[:], lamS[:], AF.Exp), reads=['lamS'], writes=['lamS'])
        p.op('vector', lambda e: e.tensor_tensor(neglam[:], lamS[:, depth:2 * depth], lamS[:, 0:depth], ALU.subtract),
             reads=['lamS'], writes=['neglam'])
        for l in range(depth):
            li = 0.8 - 0.6 * math.exp(-0.3 * l)
            p.op('vector', lambda e, l=l, li=li: e.tensor_scalar(neglam[:, l:l + 1], neglam[:, l:l + 1], -li, None, ALU.add),
                 reads=['neglam'], writes=['neglam'])

        wstate = {'next': 0, 'off': 0}
        wslot_of = {}

        def issue_loads(upto):
            while wstate['next'] < min(upto, len(plan)):
                i = wstate['next']
                E = plan[i][3]
                s = i % NSLOT
                p.dma('gpsimd', wsl[s][:, 0:E], W[:, wstate['off']:wstate['off'] + E], 'w%d' % s, writes=['W%d' % s])
                wslot_of[i] = s
                wstate['off'] += E
                wstate['next'] += 1

        bidx = {'i': 0}

        def next_block(kind, l, i):
            b = bidx['i']
            assert plan[b][:3] == (kind, l, i), (plan[b], kind, l, i)
            issue_loads(b + NSLOT)
            bidx['i'] += 1
            s = wslot_of[b]
            return wsl[s], 'W%d' % s

        def norm_phase(a_ap, sh_ap, a_ids, out_fn):
            rs = [rstd, lnv]
            accs = [ar, ao2]
            banks = [(ps[6], 'ps6'), (ps[7], 'ps7')]
            for th in range(2):
                tsl = slice(th * 512, (th + 1) * 512)
                acc_ = accs[th]
                for c in range(8):
                    s_ = sq[c % 2]
                    p.op('scalar', lambda e: e.activation(s_[:], xT[:, c, tsl], AF.Square),
                         reads=['xT%d.%d' % (c, th)], writes=[s_.name])
                    if c == 1:
                        p.op('vector', lambda e: e.tensor_tensor(acc_[:], sq[0][:], sq[1][:], ALU.add),
                             reads=['sq0', 'sq1'], writes=[acc_.name])
                    elif c >= 2:
                        p.op('vector', lambda e: e.tensor_tensor(acc_[:], acc_[:], s_[:], ALU.add),
                             reads=[acc_.name, s_.name], writes=[acc_.name])
                bk, bid = banks[th]
                p.op('tensor', lambda e: e.matmul(bk[:], ones_f[:], acc_[:], start=True, stop=True),
                     reads=[acc_.name, 'ones_f'], writes=[bid])
            for th in range(2):
                bk, bid = banks[th]
                r_ = rs[th]
                p.op('scalar', lambda e: e.activation(r_[:], bk[:], AF.Ln, bias=EPS, scale=1.0 / 1024),
                     reads=[bid], writes=[r_.name])
                p.op('scalar', lambda e: e.activation(r_[:], r_[:], AF.Exp, scale=-0.5), reads=[r_.name], writes=[r_.name])
            for th in range(2):
                tsl = slice(th * 512, (th + 1) * 512)
                r_ = rs[th]
                for c in range(8):
                    t_ = tmpf[c % 2]
                    p.op('vector', lambda e: e.tensor_tensor(t_[:], xT[:, c, tsl], r_[:], ALU.mult),
                         reads=['xT%d.%d' % (c, th), r_.name], writes=[t_.name])
                    out_fn(c, th, tsl, t_, a_ap, sh_ap, a_ids)

        def h_out(c, th, tsl, t_, a_ap, sh_ap, a_ids):
            p.op('scalar', lambda e: e.activation(hT[:, c, tsl], t_[:], AF.Identity, bias=sh_ap[:, c:c + 1], scale=a_ap[:, c:c + 1]),
                 reads=[t_.name] + a_ids, writes=['hT%d.%d' % (c, th)])

        def y_out(c, th, tsl, t_, a_ap, sh_ap, a_ids):
            y_ = yst[c % 2]
            p.op('scalar', lambda e: e.activation(y_[:], t_[:], AF.Identity, bias=sh_ap[:, c:c + 1], scale=a_ap[:, c:c + 1]),
                 reads=[t_.name] + a_ids, writes=[y_.name])
            p.dma('sync', yT_out[c * 128:(c + 1) * 128, tsl], y_[:], 'o_' + y_.name, reads=[y_.name], is_out=True)

        rope_n = {'i': 0}

        def rope_block(src, src_ids, U, tt):
            i = rope_n['i'] % 2
            rope_n['i'] += 1
            rf_, rb_ = rf[i], rbt[i]
            n = U * 64
            xs = src.rearrange("p (u two d) -> p u two d", two=2, d=32)
            x1, x2 = xs[:, :, 0, :], xs[:, :, 1, :]
            ds_ = rf_[:, 0:n].rearrange("p (u two d) -> p u two d", two=2, d=32)
            d1, d2 = ds_[:, :, 0, :], ds_[:, :, 1, :]
            cB = cos_s[:, tt, :].unsqueeze(1).broadcast_to([128, U, 32])
            sB = sin_s[:, tt, :].unsqueeze(1).broadcast_to([128, U, 32])
            tv = [rt[k][:, 0:U * 32].rearrange("p (u d) -> p u d", d=32) for k in range(4)]
            p.op('vector', lambda e: e.tensor_tensor(tv[0], x1, cB, ALU.mult), reads=src_ids + ['cos'], writes=['rt0'])
            p.op('vector', lambda e: e.tensor_tensor(tv[1], x2, sB, ALU.mult), reads=src_ids + ['sin'], writes=['rt1'])
            p.op('vector', lambda e: e.tensor_tensor(tv[2], x2, cB, ALU.mult), reads=src_ids + ['cos'], writes=['rt2'])
            p.op('vector', lambda e: e.tensor_tensor(tv[3], x1, sB, ALU.mult), reads=src_ids + ['sin'], writes=['rt3'])
            p.op('vector', lambda e: e.tensor_tensor(d1, tv[0], tv[1], ALU.subtract), reads=['rt0', 'rt1'], writes=[rf_.name + 'a'])
            p.op('vector', lambda e: e.tensor_tensor(d2, tv[2], tv[3], ALU.add), reads=['rt2', 'rt3'], writes=[rf_.name + 'b'])
            p.op('scalar', lambda e: e.activation(rb_[:, 0:n], rf_[:, 0:n], AF.Copy),
                 reads=[rf_.name + 'a', rf_.name + 'b'], writes=[rb_.name])
            return rf_, rb_

        tr_n = {'i': 0}

        def transposes(rb_, nblk, dst_view, dst_ids):
            h = tr_n['i'] % 2
            tr_n['i'] += 1
            pb = psB[:, h * 512:h * 512 + nblk * 128]
            for b in range(nblk):
                p.op('tensor', lambda e, b=b: e.transpose(psB[:, h * 512 + b * 128:h * 512 + (b + 1) * 128],
                                                           rb_[:, b * 128:(b + 1) * 128], ident[:]),
                     reads=[rb_.name, 'ident'], writes=['psB%d' % h, 'ps7'], sig=(b == nblk - 1))
            p.op('vector', lambda e: e.tensor_copy(dst_view, pb.rearrange("p (n t) -> p n t", t=128)),
                 reads=['psB%d' % h, 'ps7'], writes=dst_ids)

        def qknorm(src_ps, src_ids, nh, g_ap, g_id, dst, par):
            n = nh * 64
            tq = (tm2, sq[1])[par]
            o8 = par * 8
            sid, lid, rid_ = 'ss8_%d' % par, 'l8_%d' % par, 'r8_%d' % par
            p.op('scalar', lambda e: e.activation(tq[:, 0:n], src_ps, AF.Square), reads=src_ids, writes=[tq.name])
            p.op('vector', lambda e: e.tensor_reduce(ss8[:, o8:o8 + nh], tq[:, 0:n].rearrange("p (h d) -> p h d", d=64), AX.X, ALU.add),
                 reads=[tq.name], writes=[sid])
            p.op('scalar', lambda e: e.activation(l8[:, o8:o8 + nh], ss8[:, o8:o8 + nh], AF.Ln, bias=EPS, scale=1.0 / 64),
                 reads=[sid], writes=[lid])
            p.op('scalar', lambda e: e.activation(r8[:, o8:o8 + nh], l8[:, o8:o8 + nh], AF.Exp, scale=-0.5), reads=[lid], writes=[rid_])
            for h in range(nh):
                p.op('scalar', lambda e, h=h: e.activation(dst[:, h * 64:(h + 1) * 64], src_ps[:, h * 64:(h + 1) * 64], AF.Identity,
                                                           scale=r8[:, o8 + h:o8 + h + 1]),
                     reads=src_ids + [rid_], writes=[dst.name], sig=(h == nh - 1))
            dv = dst[:, 0:n].rearrange("p (h d) -> p h d", d=64)
            p.op('vector', lambda e: e.tensor_tensor(dv, dv, g_ap.unsqueeze(1).broadcast_to([128, nh, 64]), ALU.mult),
                 reads=[dst.name, g_id], writes=[dst.name])

        class _PV:
            def __init__(self, ap, name):
                self.ap, self.name = ap, name

            def __getitem__(self, k):
                return self.ap[k]
        SB = [[ps[0], ps[6]], [ps[1], ps[7]]]
        KT_ORDER = [[0, 1, 2, 3, 8, 9, 10, 11, 4, 5, 6, 7], list(range(12))]
        SBN = [['ps0', 'ps6'], ['ps1', 'ps7']]
        PT = [[_PV(PP[:, par, u * 512:(u + 1) * 512], 'PP%d' % par) for par in range(2)] for u in range(2)]

        def attn_pair(kc, qc, qh, kts, v_fn, acc_fn, d_fn, hooks=None):
            hooks = hooks or {}
            n = len(kts)
            qs = slice(qh * 512, (qh + 1) * 512)
            qids = id_qT(qc, qh * 512, (qh + 1) * 512)

            def do_S(i):
                kt = kts[i]
                for u in range(2):
                    rows = slice(u * 64, (u + 1) * 64)
                    p.op('tensor', lambda e: e.matmul(SB[u][i % 2][:], kT[rows, kc, kt * 128:(kt + 1) * 128], qT[rows, qc, qs],
                                                      start=True, stop=True),
                         reads=id_kT(kc, kt * 128, (kt + 1) * 128) + qids, writes=[SBN[u][i % 2]])
            do_S(0)
            if n > 1:
                do_S(1)
            for i in range(n):
                kt = kts[i]
                par = i % 2
                if kt < 4 or (kt - 4) // 4 != qh:
                    bc = kt * 4 + 2 * qh
                    p.op('scalar', lambda e: e.activation(PP[:, par, :], psS[par][:, :], AF.Exp, bias=maskb[:, bc:bc + 1], scale=0.125),
                         reads=[SBN[0][par], SBN[1][par], 'maskb'], writes=['PP%d' % par])
                else:
                    p.op('scalar', lambda e: e.activation(PP[:, par, :], psS[par][:, :], AF.Exp, scale=0.125),
                         reads=[SBN[0][par], SBN[1][par]], writes=['PP%d' % par])
                if ('exp', i) in hooks:
                    hooks[('exp', i)](SB[0][i % 2], SBN[0][i % 2])
                if i + 2 < n:
                    do_S(i + 2)
                if kt >= 4 and (kt - 4) // 4 == qh:
                    c0 = (kt - 4) * 4 + 2 * qh
                    for u in range(2):
                        P_ = PT[u][i % 2]
                        for sq_ in range(2):
                            p.op('vector', lambda e: e.tensor_scalar(P_[:, sq_ * 256:(sq_ + 1) * 256], P_[:, sq_ * 256:(sq_ + 1) * 256],
                                                                     m01[:, c0 + sq_:c0 + sq_ + 1], None, ALU.mult),
                                 reads=[P_.name, 'm01'], writes=[P_.name])
                for u in range(2):
                    P_ = PT[u][i % 2]
                    acc, aid = acc_fn(u)
                    v_ap, v_ids = v_fn(u, kt)
                    dd = d_fn(u) if d_fn is not None else None
                    p.op('tensor', lambda e: e.matmul(acc, v_ap, P_[:], start=(i == 0), stop=(i == n - 1), skip_group_check=True),
                         reads=v_ids + [P_.name], writes=[aid], sig=(dd is None))
                    if dd is not None:
                        p.op('tensor', lambda e: e.matmul(dd[0], ones_b[:], P_[:], start=(i == 0), stop=(i == n - 1), skip_group_check=True),
                             reads=['ones_b', P_.name], writes=[dd[1]], sig=True)
                if ('pv', i) in hooks:
                    hooks[('pv', i)]()

        def load_cache(l):
            p.dma('gpsimd', kT[:, 0:6, 0:PAST], cKT[l].rearrange("(c p) k -> p c k", p=128), 'ldk',
                  writes=sum([id_kT(c, 0, PAST) for c in range(6)], []))
            p.dma('gpsimd', Vv[:, 0:4, :], cV[l].rearrange("(t p) f -> p t f", p=128), 'ldv',
                  writes=rid(R_V, R_V + 4 * VW))
            for kt in range(4, 12):
                ov = Vv[:, kt, 512:1024].rearrange("p (a b d) -> p a b d", a=4, b=2, d=64)[:, :, 1, :]
                p.op('vector', lambda e: e.memset(ov, 1.0), writes=id_V(kt, 512, 1024))

        def attn_A(l):
            li = 0.8 - 0.6 * math.exp(-0.3 * l)
            d1s, d2s, o1c, o2c = tmpf[0], tmpf[1], sq[0], sq[1]
            pend = {}

            def part1(a, qh):
                p.op('scalar', lambda e: e.activation(d1s[:], ps[3][:], AF.Copy), reads=['ps3'], writes=['tmpf0'])
                p.op('scalar', lambda e: e.activation(d2s[:], ps[5][:], AF.Copy), reads=['ps5'], writes=['tmpf1'])
                p.op('vector', lambda e: e.tensor_copy(o1c[:], ps[2][:]), reads=['ps2'], writes=['sq0'])
                p.op('vector', lambda e: e.tensor_copy(o2c[:], ps[4][:]), reads=['ps4'], writes=['sq1'])
                p.op('vector', lambda e: e.reciprocal(ar[:], d1s[:]), reads=['tmpf0'], writes=['ar'])
                p.op('vector', lambda e: e.tensor_tensor(ao1[:], o1c[:], ar[:], ALU.mult), reads=['sq0', 'ar'], writes=['ao1'])
                p.op('vector', lambda e: e.reciprocal(ar[:], d2s[:]), reads=['tmpf1', 'ar'], writes=['ar'])
                p.op('vector', lambda e: e.tensor_tensor(ao2[:], o2c[:], ar[:], ALU.mult), reads=['sq1', 'ar'], writes=['ao2'])
                p.op('vector', lambda e: e.scalar_tensor_tensor(ao1[:], ao2[:], neglam[:, l:l + 1], ao1[:], ALU.mult, ALU.add),
                     reads=['ao1', 'ao2', 'neglam'], writes=['ao1'])

            def part2a(bank, bid):
                p.op('scalar', lambda e: e.activation(sg[:], ao1[:], AF.Square), reads=['ao1'], writes=['sg'])
                p.op('tensor', lambda e: e.matmul(bank[:], ones_f[:], sg[:], start=True, stop=True),
                     reads=['ones_f', 'sg'], writes=[bid])
                p.op('scalar', lambda e: e.activation(lnv[:], bank[:], AF.Ln, bias=EPS, scale=1.0 / 128),
                     reads=[bid], writes=['lnv'])
                p.op('scalar', lambda e: e.activation(rstd[:], lnv[:], AF.Exp, scale=-0.5), reads=['lnv'], writes=['rstd'])

            def part2b(a, qh):
                qs = slice(qh * 512, (qh + 1) * 512)
                p.op('vector', lambda e: e.tensor_tensor(ao2[:], ao1[:], rstd[:], ALU.mult), reads=['ao1', 'rstd'], writes=['ao2'])
                p.op('vector', lambda e: e.tensor_scalar(OT[:, a, qs], ao2[:], sublnT[:, l:l + 1], 1.0 - li, ALU.mult, ALU.mult),
                     reads=['ao2', 'sublnT'], writes=id_OT(a, qh * 512, (qh + 1) * 512))

            prev = None
            for a in range(4):
                for qh in range(2):
                    hooks = {}
                    if prev is not None:
                        hooks[('exp', 5)] = part2a
                        hooks[('pv', 6)] = (lambda pa=prev: part2b(*pa))
                    attn_pair(a, a, qh, KT_ORDER[qh],
                              lambda u, kt: (Vv[:, kt, a * 128:(a + 1) * 128], id_V(kt, a * 128, (a + 1) * 128)),
                              lambda u: (ps[2 + 2 * u][:], 'ps%d' % (2 + 2 * u)),
                              lambda u: (ps[3 + 2 * u][:], 'ps%d' % (3 + 2 * u)), hooks)
                    part1(a, qh)
                    prev = (a, qh)
            part2a(ps[1], 'ps1')
            part2b(*prev)

        def attn_B(l):
            n = 0
            prev = None

            def post(j, qh, ab):
                qs = slice(qh * 512, (qh + 1) * 512)
                for u in range(2):
                    rows = slice(u * 64, (u + 1) * 64)
                    p.op('vector', lambda e: e.reciprocal(ar[0:64, :], ps[ab + u][64:128, :]), reads=['ps%d' % (ab + u)], writes=['ar'])
                    p.op('vector', lambda e: e.tensor_tensor(OT[rows, 4 + j, qs], ps[ab + u][0:64, :], ar[0:64, :], ALU.mult),
                         reads=['ps%d' % (ab + u), 'ar'], writes=id_OT(4 + j, qh * 512, (qh + 1) * 512))
            for j in range(4):
                for qh in range(2):
                    ab = 2 + 2 * (n % 2)
                    n += 1
                    hooks = {}
                    if prev is not None:
                        hooks[('pv', 1)] = (lambda pa=prev: post(*pa))
                    attn_pair(4, j, qh, KT_ORDER[qh],
                              lambda u, kt: (Vv[:, kt, 512 + u * 128:512 + (u + 1) * 128], id_V(kt, 512 + u * 128, 512 + (u + 1) * 128)),
                              lambda u: (ps[ab + u][:], 'ps%d' % (ab + u)), None, hooks)
                    prev = (j, qh, ab)
            post(*prev)

        def attn_C(l):
            cbuf = {(0, 0): tmpf[0], (0, 1): tmpf[1], (1, 0): sq[0], (1, 1): sq[1]}

            def evac(j):
                for qh in range(2):
                    for u in range(2):
                        h = j + 4 * u
                        bk = 2 + 2 * qh + u
                        c_ = cbuf[(qh, u)]
                        col = l * 8 + h
                        if u == 0:
                            p.op('scalar', lambda e: e.activation(c_[:], ps[bk][:], AF.Identity, bias=esink[:, col:col + 1], scale=1.0),
                                 reads=['ps%d' % bk, 'esink'], writes=[c_.name])
                        else:
                            p.op('vector', lambda e: e.tensor_scalar(c_[:], ps[bk][:], esink[:, col:col + 1], None, ALU.add),
                                 reads=['ps%d' % bk, 'esink'], writes=[c_.name])

            def post(j):
                for qh in range(2):
                    qs = slice(qh * 512, (qh + 1) * 512)
                    for u in range(2):
                        rows = slice(u * 64, (u + 1) * 64)
                        c_ = cbuf[(qh, u)]
                        p.op('vector', lambda e: e.reciprocal(ar[0:64, :], c_[64:128, :]), reads=[c_.name], writes=['ar'])
                        p.op('vector', lambda e: e.tensor_tensor(OT[rows, 8 + j, qs], c_[0:64, :], ar[0:64, :], ALU.mult),
                             reads=[c_.name, 'ar'], writes=id_OT(8 + j, qh * 512, (qh + 1) * 512))

            prev = None
            for j in range(4):
                v_fn = lambda u, kt: (Vv[:, kt, 768 + u * 128:768 + (u + 1) * 128], id_V(kt, 768 + u * 128, 768 + (u + 1) * 128))
                for qh in range(2):
                    hooks = {}
                    if qh == 0 and prev is not None:
                        hooks[('pv', 0)] = (lambda pj=prev: post(pj))
                    attn_pair(5, j, qh, list(range(4)), v_fn,
                              lambda u: (ps[2 + 2 * qh + u][:], 'ps%d' % (2 + 2 * qh + u)), None, hooks)

                def c_S(jt):
                    qlo, qhi = max(0, jt - 1) * 128, min(8, jt + 2) * 128
                    N = qhi - qlo
                    for u in range(2):
                        rows = slice(u * 64, (u + 1) * 64)
                        p.op('tensor', lambda e: e.matmul(SB[u][jt % 2][:, 0:N], kT[rows, 5, PAST + jt * 128:PAST + (jt + 1) * 128],
                                                          qT[rows, j, qlo:qhi], start=True, stop=True),
                             reads=id_kT(5, PAST + jt * 128, PAST + (jt + 1) * 128) + id_qT(j, qlo, qhi), writes=[SBN[u][jt % 2]])
                    return (qlo, qhi, N)
                cinfo = {0: c_S(0), 1: c_S(1)}
                for jt in range(8):
                    qlo, qhi, N = cinfo[jt]
                    par = jt % 2
                    pin = psS[par][:, :].rearrange("p (u n) -> p u n", u=2)[:, :, 0:N]
                    pout = PP[:, par, :].rearrange("p (u n) -> p u n", u=2)[:, :, 0:N]
                    p.op('scalar', lambda e: e.activation(pout, pin, AF.Exp, scale=0.125),
                         reads=[SBN[0][par], SBN[1][par]], writes=['PP%d' % par])
                    if jt + 2 < 8:
                        cinfo[jt + 2] = c_S(jt + 2)
                    for u in range(2):
                        P_ = PT[u][jt % 2]
                        if jt >= 1:
                            p.op('vector', lambda e: e.tensor_tensor(P_[:, 0:128], P_[:, 0:128], cmask[:, jt, 0, :], ALU.mult),
                                 reads=[P_.name, 'cmask'], writes=[P_.name])
                        if jt <= 6:
                            p.op('vector', lambda e: e.tensor_tensor(P_[:, N - 128:N], P_[:, N - 128:N], cmask[:, jt, 1, :], ALU.mult),
                                 reads=[P_.name, 'cmask'], writes=[P_.name])
                    for u in range(2):
                        P_ = PT[u][jt % 2]
                        for qh in range(2):
                            lo, hi = max(qlo, qh * 512), min(qhi, (qh + 1) * 512)
                            if lo >= hi:
                                continue
                            osl = slice(lo - qh * 512, hi - qh * 512)
                            psl = slice(lo - qlo, hi - qlo)
                            bk = 2 + 2 * qh + u
                            p.op('tensor', lambda e: e.matmul(ps[bk][:, osl], Vv[:, 4 + jt, 768 + u * 128:768 + (u + 1) * 128], P_[:, psl],
                                                              start=False, stop=False, skip_group_check=True),
                                 reads=id_V(4 + jt, 768 + u * 128, 768 + (u + 1) * 128) + [P_.name], writes=['ps%d' % bk], sig=True)
                evac(j)
                prev = j
            post(prev)

        def qkv_group(l, g):
            Wt, wid = next_block('qkv', l, g)
            pend = []
            for tt in range(8):
                b = 2 + (tt % 4)
                pb = ps[b]
                pid = 'ps%d' % b
                for k in range(8):
                    p.op('tensor', lambda e, k=k, tt=tt, pb=pb: e.matmul(pb[:], hT[:, k, tt * 128:(tt + 1) * 128], Wt[:, k * 512:(k + 1) * 512],
                                                                         start=(k == 0), stop=(k == 7)),
                         reads=['hT%d.%d' % (k, tt // 4), wid], writes=[pid], sig=(k == 7))
                def post(tt=tt, pb=pb, pid=pid):
                    tsl = slice(tt * 128, (tt + 1) * 128)
                    if g == 0:
                        rf_, rb_ = rope_block(pb[:], [pid], 8, tt)
                        transposes(rb_, 4, qT[:, 0:4, tsl], sum([id_qT(c, tt * 128, (tt + 1) * 128) for c in range(4)], []))
                    elif g == 1:
                        rf_, rb_ = rope_block(pb[:], [pid], 8, tt)
                        p.dma('sync', nk_a[l, tsl, :], rf_[:], 'o_' + rf_.name, reads=[rf_.name + 'a', rf_.name + 'b'], is_out=True)
                        transposes(rb_, 4, kT[:, 0:4, PAST + tt * 128:PAST + (tt + 1) * 128],
                                   sum([id_kT(c, PAST + tt * 128, PAST + (tt + 1) * 128) for c in range(4)], []))
                    elif g == 2:
                        v_ = vst[tt % 2]
                        KV = os.environ.get('KVAR', '')
                        if 'a' not in KV:
                            p.op('scalar', lambda e, v_=v_, pb=pb: e.activation(v_[:], pb[:], AF.Copy), reads=[pid], writes=[v_.name])
                        if 'b' not in KV:
                            p.dma('sync', nv_a[l, tsl, :], v_[:], 'o_' + v_.name, reads=[v_.name], is_out=True)
                        if 'c' not in KV:
                            p.op('vector', lambda e, v_=v_, tt=tt: e.tensor_copy(Vv[:, 4 + tt, 0:512], v_[:]), reads=[v_.name], writes=id_V(4 + tt, 0, 512))
                    elif g == 3:
                        xq = (xn, sq[0])[tt % 2]
                        qknorm(pb[:], [pid], 8, gqB[:, l * 64:(l + 1) * 64], 'gqB', xq, tt % 2)
                        rf_, rb_ = rope_block(xq[:], [xq.name], 8, tt)
                        transposes(rb_, 4, qT[:, 0:4, tsl], sum([id_qT(c, tt * 128, (tt + 1) * 128) for c in range(4)], []))
                    elif g == 4:
                        rf_, rb_ = rope_block(pb[:], [pid], 8, tt)
                        transposes(rb_, 4, qT[:, 0:4, tsl], sum([id_qT(c, tt * 128, (tt + 1) * 128) for c in range(4)], []))
                    else:
                        v_ = vst[tt % 2]
                        xq = (xn, sq[0])[tt % 2]
                        p.op('scalar', lambda e, pb=pb: e.activation(xq[:, 128:256], pb[:, 128:256], AF.Copy), reads=[pid], writes=[xq.name])
                        p.op('scalar', lambda e, v_=v_, pb=pb: e.activation(v_[:, 0:256], pb[:, 256:512], AF.Copy), reads=[pid], writes=[v_.name])
                        qknorm(pb[:, 0:128], [pid], 2, gkB[:, l * 64:(l + 1) * 64], 'gkB', xq, tt % 2)
                        rf_, rb_ = rope_block(xq[:, 0:256], [xq.name], 4, tt)
                        p.dma('sync', nk_b[l, tsl, :], rf_[:, 0:128], 'o_' + rf_.name, reads=[rf_.name + 'a', rf_.name + 'b'], is_out=True)
                        p.dma('sync', nk_c[l, tsl, :], rf_[:, 128:256], 'o_' + rf_.name, reads=[rf_.name + 'a', rf_.name + 'b'], is_out=True)
                        transposes(rb_, 2, kT[:, 4:6, PAST + tt * 128:PAST + (tt + 1) * 128],
                                   sum([id_kT(c, PAST + tt * 128, PAST + (tt + 1) * 128) for c in (4, 5)], []))
                        p.dma('sync', nv_b[l, tsl, :], v_[:, 0:128], 'o_' + v_.name, reads=[v_.name], is_out=True)
                        p.dma('sync', nv_c[l, tsl, :], v_[:, 128:256], 'o_' + v_.name, reads=[v_.name], is_out=True)
                        p.op('vector', lambda e, v_=v_, tt=tt: e.tensor_copy(
                            Vv[:, 4 + tt, 512:1024].rearrange("p (a b d) -> p a b d", a=4, b=2, d=64)[:, :, 0, :],
                            v_[:, 0:256].rearrange("p (a d) -> p a d", d=64)), reads=[v_.name],
                             writes=id_V(4 + tt, 512, 1024))
                pend.append(post)
                if len(pend) > 2:
                    pend.pop(0)()
            while pend:
                pend.pop(0)()

        for l in range(depth if not KSTOP else 1):
          try:
            p.epoch = l + 1
            for jg in range(12):
                Wt, wid = next_block('mod', l, jg)
                for j in range(4):
                    col = jg * 4 + j
                    for k in range(8):
                        p.op('tensor', lambda e, Wt=Wt, j=j, k=k, col=col: e.matmul(
                            ps[6][:, col:col + 1], Wt[:, k * 512 + j * 128:k * 512 + (j + 1) * 128], scb[:, k:k + 1],
                            start=(k == 0), stop=(k == 7)),
                            reads=[wid, 'scb'], writes=['ps6'], sig=(k == 7))
            p.op('vector', lambda e, l=l: e.tensor_tensor(modT[:], ps[6][:, 0:48], bmodT[:, l * 48:(l + 1) * 48], ALU.add),
                 reads=['ps6', 'bmodT'], writes=['modT'])
            p.op('vector', lambda e, l=l: e.scalar_tensor_tensor(a1[:], modT[:, 8:16], 1.0, g1T[:, l * 8:(l + 1) * 8], ALU.add, ALU.mult),
                 reads=['modT', 'g1T'], writes=['a1'])
            p.op('vector', lambda e, l=l: e.scalar_tensor_tensor(a2[:], modT[:, 32:40], 1.0, g2T[:, l * 8:(l + 1) * 8], ALU.add, ALU.mult),
                 reads=['modT', 'g2T'], writes=['a2'])
            sh1, gg1, sh2, gg2 = modT[:, 0:8], modT[:, 16:24], modT[:, 24:32], modT[:, 40:48]
            _chk('mod')

            norm_phase(a1, sh1, ['a1', 'modT'], h_out)
            _chk('norm1')
            load_cache(l)
            _chk('cache')
            qkv_group(l, 0)
            _chk('g0')
            qkv_group(l, 1)
            _chk('g1')
            qkv_group(l, 2)
            _chk('g2')
            attn_A(l)
            _chk('aA')
            qkv_group(l, 3)
            _chk('g3')
            qkv_group(l, 5)
            _chk('g5')
            attn_B(l)
            _chk('aB')
            qkv_group(l, 4)
            attn_C(l)
            _chk('aC')

            for oc in range(8):
                Wt, wid = next_block('mrg', l, oc)
                for th in range(2):
                    tsl = slice(th * 512, (th + 1) * 512)
                    for br in range(3):
                        yb_, gb_ = ps[2 * br], ps[2 * br + 1]
                        for k in range(4):
                            kk = br * 4 + k
                            p.op('tensor', lambda e, yb_=yb_, kk=kk, k=k, br=br: e.matmul(
                                yb_[:], Wt[:, kk * 128:(kk + 1) * 128], OT[:, br * 4 + k, tsl], start=(k == 0), stop=(k == 3)),
                                reads=[wid] + id_OT(br * 4 + k, th * 512, (th + 1) * 512), writes=['ps%d' % (2 * br)], sig=(k == 3))
                        for k in range(8):
                            kk = 12 + br * 8 + k
                            p.op('tensor', lambda e, gb_=gb_, kk=kk, k=k: e.matmul(
                                gb_[:], Wt[:, kk * 128:(kk + 1) * 128], hT[:, k, tsl], start=(k == 0), stop=(k == 7)),
                                reads=[wid, 'hT%d.%d' % (k, th)], writes=['ps%d' % (2 * br + 1)], sig=(k == 7))
                        p.op('scalar', lambda e, gb_=gb_: e.activation(sg[:], gb_[:], AF.Sigmoid), reads=['ps%d' % (2 * br + 1)], writes=['sg'])
                        if br == 0:
                            p.op('vector', lambda e, yb_=yb_: e.tensor_tensor(acc[:], yb_[:], sg[:], ALU.mult),
                                 reads=['ps%d' % (2 * br), 'sg'], writes=['ao1'])
                        else:
                            p.op('vector', lambda e, yb_=yb_: e.tensor_tensor(tm2[:], yb_[:], sg[:], ALU.mult),
                                 reads=['ps%d' % (2 * br), 'sg'], writes=['tm2'])
                            if br == 1:
                                p.op('vector', lambda e: e.tensor_tensor(acc[:], acc[:], tm2[:], ALU.add), reads=['ao1', 'tm2'], writes=['ao1'])
                            else:
                                p.op('vector', lambda e, oc=oc, tsl=tsl: e.tensor_tensor(mT[:, oc, tsl], acc[:], tm2[:], ALU.add),
                                     reads=['ao1', 'tm2'], writes=id_mT(oc, th * 512, (th + 1) * 512))
            for og in range(2):
                Wt, wid = next_block('out', l, og)
                for ocl in range(4):
                    oc = og * 4 + ocl
                    for th in range(2):
                        tsl = slice(th * 512, (th + 1) * 512)
                        pb = ps[(ocl * 2 + th) % 4]
                        pid = 'ps%d' % ((ocl * 2 + th) % 4)
                        for k in range(8):
                            p.op('tensor', lambda e, pb=pb, k=k, ocl=ocl, tsl=tsl: e.matmul(
                                pb[:], Wt[:, (ocl * 8 + k) * 128:(ocl * 8 + k + 1) * 128], mT[:, k, tsl], start=(k == 0), stop=(k == 7)),
                                reads=[wid] + id_mT(k, th * 512, (th + 1) * 512), writes=[pid], sig=(k == 7))
                        p.op('vector', lambda e, pb=pb, oc=oc, tsl=tsl: e.scalar_tensor_tensor(
                            xT[:, oc, tsl], pb[:], gg1[:, oc:oc + 1], xT[:, oc, tsl], ALU.mult, ALU.add),
                            reads=[pid, 'modT', 'xT%d.%d' % (oc, th)], writes=['xT%d.%d' % (oc, th)])
            norm_phase(a2, sh2, ['a2', 'modT'], h_out)
            for jb in range(11):
                Wt, wid = next_block('ffi', l, jb)
                for jj in range(2):
                    j = jb * 2 + jj
                    for th in range(2):
                        tsl = slice(th * 512, (th + 1) * 512)
                        n_ = (jj * 2 + th) % 2
                        pa, pbb = ps[2 * n_], ps[2 * n_ + 1]
                        for k in range(8):
                            p.op('tensor', lambda e, pa=pa, k=k, jj=jj, tsl=tsl: e.matmul(
                                pa[:], Wt[:, ((jj * 2) * 8 + k) * 128:((jj * 2) * 8 + k + 1) * 128], hT[:, k, tsl], start=(k == 0), stop=(k == 7)),
                                reads=[wid, 'hT%d.%d' % (k, th)], writes=['ps%d' % (2 * n_)], sig=(k == 7))
                        for k in range(8):
                            p.op('tensor', lambda e, pbb=pbb, k=k, jj=jj, tsl=tsl: e.matmul(
                                pbb[:], Wt[:, ((jj * 2 + 1) * 8 + k) * 128:((jj * 2 + 1) * 8 + k + 1) * 128], hT[:, k, tsl], start=(k == 0), stop=(k == 7)),
                                reads=[wid, 'hT%d.%d' % (k, th)], writes=['ps%d' % (2 * n_ + 1)], sig=(k == 7))
                        p.op('scalar', lambda e, pa=pa: e.activation(sg[:], pa[:], AF.Silu), reads=['ps%d' % (2 * n_)], writes=['sg'])
                        p.op('vector', lambda e, pbb=pbb, j=j, tsl=tsl: e.tensor_tensor(uT[:, j, tsl], pbb[:], sg[:], ALU.mult),
                             reads=['ps%d' % (2 * n_ + 1), 'sg'], writes=id_uT(j, th * 512, (th + 1) * 512))
            for oc in range(8):
                Wt, wid = next_block('ffo', l, oc)
                for th in range(2):
                    tsl = slice(th * 512, (th + 1) * 512)
                    pb = ps[4 + th]
                    pid = 'ps%d' % (4 + th)
                    for j in range(NFF):
                        p.op('tensor', lambda e, pb=pb, j=j, tsl=tsl: e.matmul(
                            pb[:], Wt[:, j * 128:(j + 1) * 128], uT[:, j, tsl], start=(j == 0), stop=(j == NFF - 1)),
                            reads=[wid] + id_uT(j, th * 512, (th + 1) * 512), writes=[pid], sig=(j == NFF - 1))
                    p.op('vector', lambda e, pb=pb, oc=oc, tsl=tsl: e.scalar_tensor_tensor(
                        xT[:, oc, tsl], pb[:], gg2[:, oc:oc + 1], xT[:, oc, tsl], ALU.mult, ALU.add),
                        reads=[pid, 'modT', 'xT%d.%d' % (oc, th)], writes=['xT%d.%d' % (oc, th)])

          except _Stop:
            pass
        norm_phase(gfinT, zero8, ['gfinT', 'zero8'], y_out)
        p.emit()
    return nc


def _feat_major(v):
    return np.ascontiguousarray(v.reshape(8, 128).T)


def make_inputs(inp, depth=DEPTH):
    f32 = np.float32
    Wp = pack_weights(inp, depth)
    rows = np.repeat(np.arange(16), 64).astype(f32)
    cols = np.tile(np.arange(64), 16).astype(f32)
    inv = (10000.0 ** (-np.arange(16, dtype=f32) / 16)).astype(f32)
    ang = np.concatenate([rows[:, None] * inv, cols[:, None] * inv], axis=-1).astype(f32)
    cos_s, sin_s = np.cos(ang).astype(f32), np.sin(ang).astype(f32)
    cos_p, sin_p = np.ones((T, 32), f32), np.zeros((T, 32), f32)
    mb_p = np.full((12, 4), NEGBIG, f32)
    for t in range(8):
        mb_p[4 + t, t // 2] = 0.0
    mb_s = np.zeros((12, 4), f32)
    m01_s = np.ones((8, 4), f32)
    m01_p = np.zeros((8, 4), f32)
    for t in range(8):
        m01_p[t, t // 2] = 1.0
    kl = np.arange(128)[:, None]
    ql = np.arange(128)[None, :]
    cm_s = np.zeros((128, 8, 2, 128), f32)
    cm_p = np.zeros((128, 8, 2, 128), f32)
    for jt in range(8):
        cm_s[:, jt, 0, :] = (kl <= ql)
        cm_s[:, jt, 1, :] = (kl >= ql)
        cm_p[:, jt, 0, :] = 1.0 if (jt % 2 == 1) else 0.0
        cm_p[:, jt, 1, :] = 1.0 if (jt % 2 == 0) else 0.0
    rep = lambda v: np.ascontiguousarray(np.broadcast_to(v.reshape(1, -1), (128, v.size))).astype(f32)
    common = {
        'W': Wp,
        'ident': np.eye(128, dtype=f32),
        'bmodT': np.concatenate([np.ascontiguousarray(inp['b_mod'][l].reshape(48, 128).T) for l in range(depth)], axis=1),
        'g1T': np.concatenate([_feat_major(inp['norm1_g'][l]) for l in range(depth)], axis=1),
        'g2T': np.concatenate([_feat_major(inp['norm2_g'][l]) for l in range(depth)], axis=1),
        'gfinT': _feat_major(inp['final_g']),
        'sublnT': np.ascontiguousarray(inp['a_subln_g'][:depth].T),
        'gqB': rep(inp['b_qnorm_g'][:depth]),
        'gkB': rep(inp['b_knorm_g'][:depth]),
        'lamB': rep(np.concatenate([inp['a_lam_q1'][:depth].ravel(), inp['a_lam_k1'][:depth].ravel(),
                                    inp['a_lam_q2'][:depth].ravel(), inp['a_lam_k2'][:depth].ravel()])),
        'sinkB': rep(inp['c_sink'][:depth]),
    }
    maps = []
    for core in range(8):
        m = dict(common)
        if core < 4:
            xt = inp['x_prompt'][core * 4:(core + 1) * 4].reshape(T, D_MODEL)
            m['cvec'] = _feat_major(inp['c_ctx'])
            m['cKT'] = np.zeros((depth, 768, PAST), f32)
            m['cV'] = np.zeros((depth, PAST, VW), f32)
            m['m01'] = rep(m01_p)
            m['cos_t'], m['sin_t'] = cos_p, sin_p
            m['maskb'] = rep(mb_p)
            m['cmask'] = cm_p.reshape(128, -1)
        else:
            b = core - 4
            xt = inp['x_sample'][b]
            m['cvec'] = _feat_major(inp['c'][b])
            ck = np.concatenate([inp['cache_a_k'][b, :depth].reshape(depth, PAST, 512),
                                 inp['cache_b_k'][b, :depth].reshape(depth, PAST, 128),
                                 inp['cache_c_k'][b, :depth].reshape(depth, PAST, 128)], axis=-1)
            m['cKT'] = np.ascontiguousarray(ck.transpose(0, 2, 1))
            cv = np.ones((depth, PAST, VW), f32)
            cv[:, :, 0:512] = inp['cache_a_v'][b, :depth].reshape(depth, PAST, 512)
            for kv in range(2):
                cv[:, :, 512 + kv * 128:512 + kv * 128 + 64] = inp['cache_b_v'][b, :depth, :, kv, :]
                cv[:, :, 768 + kv * 128:768 + kv * 128 + 64] = inp['cache_c_v'][b, :depth, :, kv, :]
            m['cV'] = cv
            m['m01'] = rep(m01_s)
            m['cos_t'], m['sin_t'] = cos_s, sin_s
            m['maskb'] = rep(mb_s)
            m['cmask'] = cm_s.reshape(128, -1)
        m['xT_in'] = np.ascontiguousarray(xt.T)
        maps.append(m)
    return maps


_NC_CACHE = {}


def run(inp, depth=DEPTH):
    inp = {k: np.asarray(v) for k, v in inp.items()}
    if depth not in _NC_CACHE:
        _NC_CACHE[depth] = build_program(depth)
    nc = _NC_CACHE[depth]
    maps = make_inputs(inp, depth)
    res = run_bass_kernel_spmd(nc, maps, core_ids=list(range(8)))
    r = res.results
    f32 = np.float32
    y_prompt = np.concatenate([r[c]['yT'].T.reshape(4, 256, D_MODEL) for c in range(4)], axis=0).astype(f32)
    y_sample = np.stack([r[4 + b]['yT'].T for b in range(4)], axis=0).astype(f32)

    def gather(name, hd, dd):
        outs = []
        for c in range(4):
            a = r[c][name].reshape(depth, 4, 256, hd, dd).transpose(1, 0, 2, 3, 4)
            outs.append(a)
        return np.ascontiguousarray(np.concatenate(outs, axis=0)).astype(f32)
    return (y_prompt, y_sample,
            gather('nk_a', 4, 128), gather('nv_a', 4, 128),
            gather('nk_b', 2, 64), gather('nv_b', 2, 64),
            gather('nk_c', 2, 64), gather('nv_c', 2, 64))


def kernel(**inputs):
    return run(inputs, DEPTH)
```

```python
import math
import os
import contextlib
import numpy as np
import concourse.bass as bass
import concourse.mybir as mybir
from concourse.bass_utils import run_bass_kernel_spmd

F32 = mybir.dt.float32
BF16 = mybir.dt.bfloat16
ALU = mybir.AluOpType
AF = mybir.ActivationFunctionType
AX = mybir.AxisListType

D_MODEL = 1024
DEPTH = 4
T = 1024
PAST = 512
D_FF = 2816
NFF = 22
EPS = 1e-6
NEGBIG = -30000.0
WSLOT = 4608
NSLOT = 3

O_QA, O_KA, O_VA, O_QB, O_KB, O_VB, O_QC, O_KC, O_VC, O_G = 0, 512, 1024, 1536, 2048, 2176, 2304, 2816, 2944, 3072


def _pair_perm():
    idx = []
    for j in range(4):
        idx += list(range(j * 64, j * 64 + 64)) + list(range((j + 4) * 64, (j + 4) * 64 + 64))
    return np.array(idx)


def qkv_group_cols():
    pp = _pair_perm()
    g = []
    g.append(np.arange(O_QA, O_QA + 512))
    g.append(np.arange(O_KA, O_KA + 512))
    g.append(np.arange(O_VA, O_VA + 512))
    g.append(O_QB + pp)
    g.append(O_QC + pp)
    g.append(np.concatenate([np.arange(O_KB, O_KB + 128), np.arange(O_KC, O_KC + 128),
                             np.arange(O_VB, O_VB + 128), np.arange(O_VC, O_VC + 128)]))
    return g


def weight_plan(depth):
    plan = []
    for l in range(depth):
        for jg in range(12):
            plan.append(('mod', l, jg, 4096))
        for g in (0, 1, 2, 3, 5, 4):
            plan.append(('qkv', l, g, 4096))
        for oc in range(8):
            plan.append(('mrg', l, oc, 4608))
        for og in range(2):
            plan.append(('out', l, og, 4096))
        for jb in range(11):
            plan.append(('ffi', l, jb, 4096))
        for oc in range(8):
            plan.append(('ffo', l, oc, 2816))
    return plan


def _kc(w, cols):
    K = w.shape[0] // 128
    return w[:, cols].reshape(K, 128, len(cols)).transpose(1, 0, 2)


def pack_weights(inp, depth):
    plan = weight_plan(depth)
    tot = sum(b[3] for b in plan)
    W = np.empty((128, tot), np.float32)
    gcols = qkv_group_cols()
    pp = _pair_perm()
    off = 0
    for (kind, l, i, E) in plan:
        if kind == 'mod':
            blk = _kc(inp['w_mod'][l], np.arange(i * 512, i * 512 + 512))
        elif kind == 'qkv':
            blk = _kc(inp['w_in'][l], gcols[i])
        elif kind == 'mrg':
            cs = np.arange(i * 128, i * 128 + 128)
            parts = [_kc(inp['w_br_a'][l], cs),
                     _kc(inp['w_br_b'][l][pp], cs),
                     _kc(inp['w_br_c'][l][pp], cs),
                     _kc(inp['w_in'][l], O_G + cs),
                     _kc(inp['w_in'][l], O_G + 1024 + cs),
                     _kc(inp['w_in'][l], O_G + 2048 + cs)]
            blk = np.concatenate(parts, axis=1)
        elif kind == 'out':
            parts = [_kc(inp['w_out'][l], np.arange((i * 4 + o) * 128, (i * 4 + o) * 128 + 128)) for o in range(4)]
            blk = np.concatenate(parts, axis=1)
        elif kind == 'ffi':
            parts = []
            for jj in range(2):
                j = i * 2 + jj
                parts.append(_kc(inp['w_ffn_in'][l], np.arange(j * 128, j * 128 + 128)))
                parts.append(_kc(inp['w_ffn_in'][l], D_FF + np.arange(j * 128, j * 128 + 128)))
            blk = np.concatenate(parts, axis=1)
        elif kind == 'ffo':
            blk = _kc(inp['w_ffn_out'][l], np.arange(i * 128, i * 128 + 128))
        W[:, off:off + E] = blk.reshape(128, E)
        off += E
    return W


KSTOP = os.environ.get('KSTOP', '')


class _Stop(Exception):
    pass


def _chk(name):
    if KSTOP == name:
        raise _Stop()


class _Rec:
    def __init__(self):
        self.call = None

    def __getattr__(self, name):
        def f(*a, **k):
            self.call = (name, a, k)
            return self
        return f


class Prog:
    ENGS = ['tensor', 'vector', 'scalar', 'gpsimd', 'sync']

    def __init__(self, nc, stack):
        self.nc = nc
        self.stack = stack
        self.q = {e: [] for e in self.ENGS}
        self.epoch = 0
        self.esem = {}
        self.ecnt = {}
        self.dsem = {}
        self.buf = {}
        self.waited = {}
        self.pend = {e: ([], []) for e in self.ENGS}
        self.out_evs = {}

    def _deps(self, eng, reads, writes):
        best = {}

        def add(ev):
            sk, v, prod = ev
            if prod == 'tensor' and eng == 'tensor':
                return
            if best.get(sk, -1) < v:
                best[sk] = v
        for b in reads:
            st = self.buf.get(b)
            if st and st[0] is not None:
                add(st[0])
        for b in writes:
            st = self.buf.get(b)
            if st:
                if st[0] is not None:
                    add(st[0])
                for ev in st[1]:
                    add(ev)
        out = []
        for sk, v in best.items():
            key = (eng, sk)
            if self.waited.get(key, -1) >= v:
                continue
            self.waited[key] = v
            out.append((sk, v))
        return out

    def _commit(self, ev, reads, writes):
        for b in writes:
            self.buf[b] = [ev, []]
        for b in reads:
            st = self.buf.get(b)
            if st is None:
                st = self.buf[b] = [None, []]
            st[1].append(ev)

    def op(self, eng, fn, reads=(), writes=(), sig=True):
        reads = list(reads)
        writes = list(writes)
        waits = self._deps(eng, reads, writes)
        rec = _Rec()
        fn(rec)
        fn = rec.call
        assert fn is not None
        if not sig:
            self.q[eng].append((fn, waits, None))
            self.pend[eng][0].extend(reads)
            self.pend[eng][1].extend(writes)
            return None
        ek = 'E%s@%d' % (eng, self.epoch)
        if ek not in self.esem:
            self.esem[ek] = self.stack.enter_context(self.nc.semaphore("es_%s_%d" % (eng, self.epoch)))
            self.ecnt[ek] = 0
        self.ecnt[ek] += 1
        ev = (ek, self.ecnt[ek], eng)
        self.q[eng].append((fn, waits, ('E', ek)))
        pr, pw = self.pend[eng]
        self._commit(ev, reads + pr, writes + pw)
        self.pend[eng] = ([], [])
        return ev

    def dma(self, eng, out, in_, sem, reads=(), writes=(), is_out=False):
        waits = self._deps(eng, reads, writes)
        sem = '%s@%d' % (sem, self.epoch)
        if sem not in self.dsem:
            self.dsem[sem] = [self.stack.enter_context(self.nc.semaphore("ds_" + sem.replace('@', '_'))), 0]
        s = self.dsem[sem]
        s[1] += 16
        ev = ('D' + sem, s[1], 'dma')
        self.q[eng].append((('dma_start', (), {'out': out, 'in_': in_}), waits, ('D', sem)))
        self._commit(ev, list(reads), list(writes))
        if is_out:
            self.out_evs[sem] = ev
        return ev

    def _semh(self, sk):
        if sk[0] == 'E':
            return self.esem[sk]
        return self.dsem[sk[1:]][0]

    def emit(self):
        nc = self.nc
        self.q['sync'].append((None, [(ev[0], ev[1]) for ev in self.out_evs.values()], None))
        with nc.Block() as block:
            for ename in self.ENGS:
                ops = self.q[ename]

                def body(eng, ops=ops):
                    for (fn, waits, inc) in ops:
                        for (sk, v) in waits:
                            eng.wait_ge(self._semh(sk), v)
                        if fn is None:
                            continue
                        ins = getattr(eng, fn[0])(*fn[1], **fn[2])
                        if inc is None:
                            continue
                        if inc[0] == 'E':
                            ins.then_inc(self.esem[inc[1]], 1)
                        else:
                            ins.then_inc(self.dsem[inc[1]][0], 16)
                getattr(block, ename)(body)


RB = 512
R_QT, R_KT, R_V, R_OT = 0, 4096, 13312, 25600
R_TOT = 37888
VW = 1024


def rid(lo, hi):
    return ['R%d' % i for i in range(lo // RB, (hi - 1) // RB + 1)]


def build_program(depth=DEPTH):
    nc = bass.Bass("TRN2", target_bir_lowering=False)
    plan = weight_plan(depth)
    wtot = sum(b[3] for b in plan)

    def din(name, shape):
        return nc.dram_tensor(name, list(shape), F32, kind="ExternalInput").ap()

    def dout(name, shape):
        return nc.dram_tensor(name, list(shape), F32, kind="ExternalOutput").ap()

    W = din("W", [128, wtot])
    xT_in = din("xT_in", [1024, T])
    cvec_d = din("cvec", [128, 8])
    cKT = din("cKT", [depth, 768, PAST])
    cV = din("cV", [depth, PAST, VW])
    m01_d = din("m01", [128, 32])
    cos_d = din("cos_t", [T, 32])
    sin_d = din("sin_t", [T, 32])
    maskb_d = din("maskb", [128, 48])
    cmask_d = din("cmask", [128, 8 * 2 * 128])
    ident_d = din("ident", [128, 128])
    bmod_d = din("bmodT", [128, depth * 48])
    g1_d = din("g1T", [128, depth * 8])
    g2_d = din("g2T", [128, depth * 8])
    gfin_d = din("gfinT", [128, 8])
    subln_d = din("sublnT", [128, depth])
    gq_d = din("gqB", [128, depth * 64])
    gk_d = din("gkB", [128, depth * 64])
    lam_d = din("lamB", [128, 4 * depth * 64])
    sink_d = din("sinkB", [128, depth * 8])

    yT_out = dout("yT", [1024, T])
    nk_a = dout("nk_a", [depth, T, 512])
    nv_a = dout("nv_a", [depth, T, 512])
    nk_b = dout("nk_b", [depth, T, 128])
    nv_b = dout("nv_b", [depth, T, 128])
    nk_c = dout("nk_c", [depth, T, 128])
    nv_c = dout("nv_c", [depth, T, 128])

    with contextlib.ExitStack() as st:
        def sb(name, shape, dt):
            return st.enter_context(nc.sbuf_tensor(name, list(shape), dt))

        def psum(name, shape, dt):
            return st.enter_context(nc.psum_tensor(name, list(shape), dt))

        xT = sb("xT", [128, 8, T], F32)
        hT = sb("hT", [128, 8, T], BF16)
        R = sb("R", [128, R_TOT], BF16)
        wsl = [sb("wsl%d" % i, [128, WSLOT], BF16) for i in range(NSLOT)]
        cos_s = sb("cos_s", [128, 8, 32], F32)
        sin_s = sb("sin_s", [128, 8, 32], F32)
        maskb = sb("maskb_s", [128, 48], F32)
        cmask = sb("cmask_s", [128, 8, 2, 128], BF16)
        ident = sb("ident_s", [128, 128], BF16)
        ones_f = sb("ones_f", [128, 128], F32)
        ones_b = sb("ones_b", [128, 128], BF16)
        bmodT = sb("bmodT_s", [128, depth * 48], F32)
        g1T = sb("g1T_s", [128, depth * 8], F32)
        g2T = sb("g2T_s", [128, depth * 8], F32)
        gfinT = sb("gfinT_s", [128, 8], F32)
        sublnT = sb("sublnT_s", [128, depth], F32)
        gqB = sb("gqB_s", [128, depth * 64], F32)
        gkB = sb("gkB_s", [128, depth * 64], F32)
        lamS = sb("lamS", [128, 2 * depth], F32)
        neglam = sb("neglam", [128, depth], F32)
        esink = sb("esink", [128, depth * 8], F32)
        cvec = sb("cvec_s", [128, 8], F32)
        scb = sb("scb", [128, 8], BF16)
        modT = sb("modT", [128, 48], F32)
        a1 = sb("a1", [128, 8], F32)
        a2 = sb("a2", [128, 8], F32)
        zero8 = sb("zero8", [128, 8], F32)
        rstd = sb("rstd", [128, 512], F32)
        lnv = sb("lnv", [128, 512], F32)
        sq = [sb("sq%d" % i, [128, 512], F32) for i in range(2)]
        tmpf = [sb("tmpf%d" % i, [128, 512], F32) for i in range(2)]
        rf = [sb("rf%d" % i, [128, 512], F32) for i in range(2)]
        rbt = [sb("rb%d" % i, [128, 512], BF16) for i in range(2)]
        vst = [sb("vst%d" % i, [128, 512], F32) for i in range(2)]
        xn = sb("xn", [128, 512], F32)
        rt = [sb("rt%d" % i, [128, 256], F32) for i in range(4)]
        ss8 = sb("ss8", [128, 16], F32)
        l8 = sb("l8", [128, 16], F32)
        r8 = sb("r8", [128, 16], F32)
        Pt = [sb("Pt%d" % i, [128, 512], BF16) for i in range(4)]
        m01 = sb("m01_s", [128, 32], F32)
        ar = sb("ar", [128, 512], F32)
        ao1 = sb("ao1", [128, 512], F32)
        ao2 = sb("ao2", [128, 512], F32)
        sg = sb("sg", [128, 512], F32)
        tm2 = sb("tm2", [128, 512], F32)

        acc = ao1
        yst = vst
        ps = [psum("ps%d" % i, [128, 512], F32) for i in range(8)]
        psB = ps[7][:].bitcast(BF16)

        p = Prog(nc, st)

        qT = R[:, R_QT:R_QT + 4096].rearrange("p (c t) -> p c t", t=1024)
        kT = R[:, R_KT:R_KT + 9216].rearrange("p (c t) -> p c t", t=1536)
        Vv = R[:, R_V:R_V + 12 * VW].rearrange("p (k f) -> p k f", f=VW)
        OT = R[:, R_OT:R_OT + 12288].rearrange("p (c t) -> p c t", t=1024)
        mT = R[:, 0:8192].rearrange("p (c t) -> p c t", t=1024)
        uT = R[:, 0:22528].rearrange("p (c t) -> p c t", t=1024)

        def id_qT(c, lo=0, hi=1024):
            return rid(R_QT + c * 1024 + lo, R_QT + c * 1024 + hi)

        def id_kT(c, lo, hi):
            return rid(R_KT + c * 1536 + lo, R_KT + c * 1536 + hi)

        def id_V(kt, lo, hi):
            return rid(R_V + kt * VW + lo, R_V + kt * VW + hi)

        def id_OT(c, lo, hi):
            return rid(R_OT + c * 1024 + lo, R_OT + c * 1024 + hi)

        def id_mT(c, lo, hi):
            return rid(c * 1024 + lo, c * 1024 + hi)

        id_uT = id_mT

        for c in range(8):
            p.dma('sync', xT[:, c, :], xT_in[c * 128:(c + 1) * 128, :], 'ldx%d' % c, writes=['xT%d.0' % c, 'xT%d.1' % c])
        small = [(cvec, cvec_d, 'cvec'), (maskb, maskb_d, 'maskb'), (bmodT, bmod_d, 'bmodT'), (g1T, g1_d, 'g1T'),
                 (g2T, g2_d, 'g2T'), (gfinT, gfin_d, 'gfinT'), (sublnT, subln_d, 'sublnT'), (gqB, gq_d, 'gqB'),
                 (gkB, gk_d, 'gkB'), (esink, sink_d, 'esink')]
        for (t_, d_, n_) in small:
            p.dma('sync', t_[:], d_[:], 'lds_' + n_, writes=[n_])
        p.dma('sync', cos_s[:], cos_d.rearrange("(t p) d -> p t d", p=128), 'lds_cos', writes=['cos'])
        p.dma('sync', sin_s[:], sin_d.rearrange("(t p) d -> p t d", p=128), 'lds_sin', writes=['sin'])
        p.dma('gpsimd', ident[:], ident_d[:], 'ldc_i', writes=['ident'])
        p.dma('gpsimd', cmask[:].rearrange("p a b c -> p (a b c)"), cmask_d[:], 'ldc_m', writes=['cmask'])
        p.op('vector', lambda e: e.memset(ones_f[:], 1.0), writes=['ones_f'])
        p.op('vector', lambda e: e.memset(ones_b[:], 1.0), writes=['ones_b'])
        p.op('vector', lambda e: e.memset(zero8[:], 0.0), writes=['zero8'])
        p.op('scalar', lambda e: e.activation(scb[:], cvec[:], AF.Silu), reads=['cvec'], writes=['scb'])
        p.op('scalar', lambda e: e.activation(esink[:], esink[:], AF.Exp), reads=['esink'], writes=['esink'])
        p.op('vector', lambda e: e.memset(esink[0:64, :], 0.0), reads=['esink'], writes=['esink'])
        n64 = depth * 64
        p.dma('sync', tmpf[0][:, 0:2 * n64], lam_d[:, 0:2 * n64], 'lds_lam1', writes=['tmpf0'])
        p.dma('sync', tmpf[1][:, 0:2 * n64], lam_d[:, 2 * n64:4 * n64], 'lds_lam2', writes=['tmpf1'])
        p.dma('sync', m01[:], m01_d[:], 'lds_m01', writes=['m01'])
        p.op('vector', lambda e: e.tensor_tensor(sq[0][:, 0:n64], tmpf[0][:, 0:n64], tmpf[0][:, n64:2 * n64], ALU.mult),
             reads=['tmpf0'], writes=['sq0'])
        p.op('vector', lambda e: e.tensor_tensor(sq[0][:, n64:2 * n64], tmpf[1][:, 0:n64], tmpf[1][:, n64:2 * n64], ALU.mult),
             reads=['tmpf1', 'sq0'], writes=['sq0'])
        p.op('vector', lambda e: e.tensor_reduce(lamS[:], sq[0][:, 0:2 * n64].rearrange("p (a d) -> p a d", d=64), AX.X, ALU.add),
             reads=['sq0'], writes=['lamS'])
        p.op('scalar', lambda e: e.activation(lamS[:], lamS[:], AF.Exp), reads=['lamS'], writes=['lamS'])
        p.op('vector', lambda e: e.tensor_tensor(neglam[:], lamS[:, depth:2 * depth], lamS[:, 0:depth], ALU.subtract),
             reads=['lamS'], writes=['neglam'])
        for l in range(depth):
            li = 0.8 - 0.6 * math.exp(-0.3 * l)
            p.op('vector', lambda e, l=l, li=li: e.tensor_scalar(neglam[:, l:l + 1], neglam[:, l:l + 1], -li, None, ALU.add),
                 reads=['neglam'], writes=['neglam'])

        wstate = {'next': 0, 'off': 0}
        wslot_of = {}

        def issue_loads(upto):
            while wstate['next'] < min(upto, len(plan)):
                i = wstate['next']
                E = plan[i][3]
                s = i % NSLOT
                p.dma('gpsimd', wsl[s][:, 0:E], W[:, wstate['off']:wstate['off'] + E], 'w%d' % s, writes=['W%d' % s])
                wslot_of[i] = s
                wstate['off'] += E
                wstate['next'] += 1

        bidx = {'i': 0}

        def next_block(kind, l, i):
            b = bidx['i']
            assert plan[b][:3] == (kind, l, i), (plan[b], kind, l, i)
            issue_loads(b + NSLOT)
            bidx['i'] += 1
            s = wslot_of[b]
            return wsl[s], 'W%d' % s

        def norm_phase(a_ap, sh_ap, a_ids, out_fn):
            rs = [rstd, lnv]
            accs = [ar, ao2]
            banks = [(ps[6], 'ps6'), (ps[7], 'ps7')]
            for th in range(2):
                tsl = slice(th * 512, (th + 1) * 512)
                acc_ = accs[th]
                for c in range(8):
                    s_ = sq[c % 2]
                    p.op('scalar', lambda e: e.activation(s_[:], xT[:, c, tsl], AF.Square),
                         reads=['xT%d.%d' % (c, th)], writes=[s_.name])
                    if c == 1:
                        p.op('vector', lambda e: e.tensor_tensor(acc_[:], sq[0][:], sq[1][:], ALU.add),
                             reads=['sq0', 'sq1'], writes=[acc_.name])
                    elif c >= 2:
                        p.op('vector', lambda e: e.tensor_tensor(acc_[:], acc_[:], s_[:], ALU.add),
                             reads=[acc_.name, s_.name], writes=[acc_.name])
                bk, bid = banks[th]
                p.op('tensor', lambda e: e.matmul(bk[:], ones_f[:], acc_[:], start=True, stop=True),
                     reads=[acc_.name, 'ones_f'], writes=[bid])
            for th in range(2):
                bk, bid = banks[th]
                r_ = rs[th]
                p.op('scalar', lambda e: e.activation(r_[:], bk[:], AF.Ln, bias=EPS, scale=1.0 / 1024),
                     reads=[bid], writes=[r_.name])
                p.op('scalar', lambda e: e.activation(r_[:], r_[:], AF.Exp, scale=-0.5), reads=[r_.name], writes=[r_.name])
            for th in range(2):
                tsl = slice(th * 512, (th + 1) * 512)
                r_ = rs[th]
                for c in range(8):
                    t_ = tmpf[c % 2]
                    p.op('vector', lambda e: e.tensor_tensor(t_[:], xT[:, c, tsl], r_[:], ALU.mult),
                         reads=['xT%d.%d' % (c, th), r_.name], writes=[t_.name])
                    out_fn(c, th, tsl, t_, a_ap, sh_ap, a_ids)

        def h_out(c, th, tsl, t_, a_ap, sh_ap, a_ids):
            p.op('scalar', lambda e: e.activation(hT[:, c, tsl], t_[:], AF.Identity, bias=sh_ap[:, c:c + 1], scale=a_ap[:, c:c + 1]),
                 reads=[t_.name] + a_ids, writes=['hT%d.%d' % (c, th)])

        def y_out(c, th, tsl, t_, a_ap, sh_ap, a_ids):
            y_ = yst[c % 2]
            p.op('scalar', lambda e: e.activation(y_[:], t_[:], AF.Identity, bias=sh_ap[:, c:c + 1], scale=a_ap[:, c:c + 1]),
                 reads=[t_.name] + a_ids, writes=[y_.name])
            p.dma('sync', yT_out[c * 128:(c + 1) * 128, tsl], y_[:], 'o_' + y_.name, reads=[y_.name], is_out=True)

        rope_n = {'i': 0}

        def rope_block(src, src_ids, U, tt):
            i = rope_n['i'] % 2
            rope_n['i'] += 1
            rf_, rb_ = rf[i], rbt[i]
            n = U * 64
            xs = src.rearrange("p (u two d) -> p u two d", two=2, d=32)
            x1, x2 = xs[:, :, 0, :], xs[:, :, 1, :]
            ds_ = rf_[:, 0:n].rearrange("p (u two d) -> p u two d", two=2, d=32)
            d1, d2 = ds_[:, :, 0, :], ds_[:, :, 1, :]
            cB = cos_s[:, tt, :].unsqueeze(1).broadcast_to([128, U, 32])
            sB = sin_s[:, tt, :].unsqueeze(1).broadcast_to([128, U, 32])
            tv = [rt[k][:, 0:U * 32].rearrange("p (u d) -> p u d", d=32) for k in range(4)]
            p.op('vector', lambda e: e.tensor_tensor(tv[0], x1, cB, ALU.mult), reads=src_ids + ['cos'], writes=['rt0'])
            p.op('vector', lambda e: e.tensor_tensor(tv[1], x2, sB, ALU.mult), reads=src_ids + ['sin'], writes=['rt1'])
            p.op('vector', lambda e: e.tensor_tensor(tv[2], x2, cB, ALU.mult), reads=src_ids + ['cos'], writes=['rt2'])
            p.op('vector', lambda e: e.tensor_tensor(tv[3], x1, sB, ALU.mult), reads=src_ids + ['sin'], writes=['rt3'])
            p.op('vector', lambda e: e.tensor_tensor(d1, tv[0], tv[1], ALU.subtract), reads=['rt0', 'rt1'], writes=[rf_.name + 'a'])
            p.op('vector', lambda e: e.tensor_tensor(d2, tv[2], tv[3], ALU.add), reads=['rt2', 'rt3'], writes=[rf_.name + 'b'])
            p.op('scalar', lambda e: e.activation(rb_[:, 0:n], rf_[:, 0:n], AF.Copy),
                 reads=[rf_.name + 'a', rf_.name + 'b'], writes=[rb_.name])
            return rf_, rb_

        tr_n = {'i': 0}

        def transposes(rb_, nblk, dst_view, dst_ids):
            h = tr_n['i'] % 2
            tr_n['i'] += 1
            pb = psB[:, h * 512:h * 512 + nblk * 128]
            for b in range(nblk):
                p.op('tensor', lambda e, b=b: e.transpose(psB[:, h * 512 + b * 128:h * 512 + (b + 1) * 128],
                                                           rb_[:, b * 128:(b + 1) * 128], ident[:]),
                     reads=[rb_.name, 'ident'], writes=['psB%d' % h, 'ps7'], sig=(b == nblk - 1))
            p.op('vector', lambda e: e.tensor_copy(dst_view, pb.rearrange("p (n t) -> p n t", t=128)),
                 reads=['psB%d' % h, 'ps7'], writes=dst_ids)

        def qknorm(src_ps, src_ids, nh, g_ap, g_id, dst, par):
            n = nh * 64
            tq = (tm2, sq[1])[par]
            o8 = par * 8
            sid, lid, rid_ = 'ss8_%d' % par, 'l8_%d' % par, 'r8_%d' % par
            p.op('scalar', lambda e: e.activation(tq[:, 0:n], src_ps, AF.Square), reads=src_ids, writes=[tq.name])
            p.op('vector', lambda e: e.tensor_reduce(ss8[:, o8:o8 + nh], tq[:, 0:n].rearrange("p (h d) -> p h d", d=64), AX.X, ALU.add),
                 reads=[tq.name], writes=[sid])
            p.op('scalar', lambda e: e.activation(l8[:, o8:o8 + nh], ss8[:, o8:o8 + nh], AF.Ln, bias=EPS, scale=1.0 / 64),
                 reads=[sid], writes=[lid])
            p.op('scalar', lambda e: e.activation(r8[:, o8:o8 + nh], l8[:, o8:o8 + nh], AF.Exp, scale=-0.5), reads=[lid], writes=[rid_])
            for h in range(nh):
                p.op('scalar', lambda e, h=h: e.activation(dst[:, h * 64:(h + 1) * 64], src_ps[:, h * 64:(h + 1) * 64], AF.Identity,
                                                           scale=r8[:, o8 + h:o8 + h + 1]),
                     reads=src_ids + [rid_], writes=[dst.name], sig=(h == nh - 1))
            dv = dst[:, 0:n].rearrange("p (h d) -> p h d", d=64)
            p.op('vector', lambda e: e.tensor_tensor(dv, dv, g_ap.unsqueeze(1).broadcast_to([128, nh, 64]), ALU.mult),
                 reads=[dst.name, g_id], writes=[dst.name])

        SB = [[ps[0], ps[1]], [ps[6], ps[7]]]
        KT_ORDER = [[0, 1, 2, 3, 8, 9, 10, 11, 4, 5, 6, 7], list(range(12))]
        SBN = [['ps0', 'ps1'], ['ps6', 'ps7']]
        PT = [[Pt[0], Pt[1]], [Pt[2], Pt[3]]]

        def attn_pair(kc, qc, qh, kts, v_fn, acc_fn, d_fn, hooks=None):
            hooks = hooks or {}
            n = len(kts)
            qs = slice(qh * 512, (qh + 1) * 512)
            qids = id_qT(qc, qh * 512, (qh + 1) * 512)

            def do_S(i):
                kt = kts[i]
                for u in range(2):
                    rows = slice(u * 64, (u + 1) * 64)
                    p.op('tensor', lambda e: e.matmul(SB[u][i % 2][:], kT[rows, kc, kt * 128:(kt + 1) * 128], qT[rows, qc, qs],
                                                      start=True, stop=True),
                         reads=id_kT(kc, kt * 128, (kt + 1) * 128) + qids, writes=[SBN[u][i % 2]])
            do_S(0)
            if n > 1:
                do_S(1)
            for i in range(n):
                kt = kts[i]
                for u in range(2):
                    P_ = PT[u][i % 2]
                    if kt < 4 or (kt - 4) // 4 != qh:
                        bc = kt * 4 + 2 * qh
                        p.op('scalar', lambda e: e.activation(P_[:], SB[u][i % 2][:], AF.Exp, bias=maskb[:, bc:bc + 1], scale=0.125),
                             reads=[SBN[u][i % 2], 'maskb'], writes=[P_.name])
                    else:
                        p.op('scalar', lambda e: e.activation(P_[:], SB[u][i % 2][:], AF.Exp, scale=0.125),
                             reads=[SBN[u][i % 2]], writes=[P_.name])
                if ('exp', i) in hooks:
                    hooks[('exp', i)](SB[0][i % 2], SBN[0][i % 2])
                if kt >= 4 and (kt - 4) // 4 == qh:
                    c0 = (kt - 4) * 4 + 2 * qh
                    for u in range(2):
                        P_ = PT[u][i % 2]
                        for sq_ in range(2):
                            p.op('vector', lambda e: e.tensor_scalar(P_[:, sq_ * 256:(sq_ + 1) * 256], P_[:, sq_ * 256:(sq_ + 1) * 256],
                                                                     m01[:, c0 + sq_:c0 + sq_ + 1], None, ALU.mult),
                                 reads=[P_.name, 'm01'], writes=[P_.name])
                for u in range(2):
                    if u == 1 and i + 2 < n:
                        do_S(i + 2)
                    P_ = PT[u][i % 2]
                    acc, aid = acc_fn(u)
                    v_ap, v_ids = v_fn(u, kt)
                    dd = d_fn(u) if d_fn is not None else None
                    p.op('tensor', lambda e: e.matmul(acc, v_ap, P_[:], start=(i == 0), stop=(i == n - 1), skip_group_check=True),
                         reads=v_ids + [P_.name], writes=[aid], sig=(dd is None))
                    if dd is not None:
                        p.op('tensor', lambda e: e.matmul(dd[0], ones_b[:], P_[:], start=(i == 0), stop=(i == n - 1), skip_group_check=True),
                             reads=['ones_b', P_.name], writes=[dd[1]], sig=True)
                if ('pv', i) in hooks:
                    hooks[('pv', i)]()

        def load_cache(l):
            p.dma('gpsimd', kT[:, 0:6, 0:PAST], cKT[l].rearrange("(c p) k -> p c k", p=128), 'ldk',
                  writes=sum([id_kT(c, 0, PAST) for c in range(6)], []))
            p.dma('gpsimd', Vv[:, 0:4, :], cV[l].rearrange("(t p) f -> p t f", p=128), 'ldv',
                  writes=rid(R_V, R_V + 4 * VW))
            for kt in range(4, 12):
                ov = Vv[:, kt, 512:1024].rearrange("p (a b d) -> p a b d", a=4, b=2, d=64)[:, :, 1, :]
                p.op('vector', lambda e: e.memset(ov, 1.0), writes=id_V(kt, 512, 1024))

        def attn_A(l):
            li = 0.8 - 0.6 * math.exp(-0.3 * l)
            d1s, d2s, o1c, o2c = tmpf[0], tmpf[1], sq[0], sq[1]
            pend = {}

            def part1(a, qh):
                p.op('scalar', lambda e: e.activation(d1s[:], ps[3][:], AF.Copy), reads=['ps3'], writes=['tmpf0'])
                p.op('scalar', lambda e: e.activation(d2s[:], ps[5][:], AF.Copy), reads=['ps5'], writes=['tmpf1'])
                p.op('vector', lambda e: e.tensor_copy(o1c[:], ps[2][:]), reads=['ps2'], writes=['sq0'])
                p.op('vector', lambda e: e.tensor_copy(o2c[:], ps[4][:]), reads=['ps4'], writes=['sq1'])
                p.op('vector', lambda e: e.reciprocal(ar[:], d1s[:]), reads=['tmpf0'], writes=['ar'])
                p.op('vector', lambda e: e.tensor_tensor(ao1[:], o1c[:], ar[:], ALU.mult), reads=['sq0', 'ar'], writes=['ao1'])
                p.op('vector', lambda e: e.reciprocal(ar[:], d2s[:]), reads=['tmpf1', 'ar'], writes=['ar'])
                p.op('vector', lambda e: e.tensor_tensor(ao2[:], o2c[:], ar[:], ALU.mult), reads=['sq1', 'ar'], writes=['ao2'])
                p.op('vector', lambda e: e.scalar_tensor_tensor(ao1[:], ao2[:], neglam[:, l:l + 1], ao1[:], ALU.mult, ALU.add),
                     reads=['ao1', 'ao2', 'neglam'], writes=['ao1'])

            def part2a(bank, bid):
                p.op('scalar', lambda e: e.activation(sg[:], ao1[:], AF.Square), reads=['ao1'], writes=['sg'])
                p.op('tensor', lambda e: e.matmul(bank[:], ones_f[:], sg[:], start=True, stop=True),
                     reads=['ones_f', 'sg'], writes=[bid])
                p.op('scalar', lambda e: e.activation(lnv[:], bank[:], AF.Ln, bias=EPS, scale=1.0 / 128),
                     reads=[bid], writes=['lnv'])
                p.op('scalar', lambda e: e.activation(rstd[:], lnv[:], AF.Exp, scale=-0.5), reads=['lnv'], writes=['rstd'])

            def part2b(a, qh):
                qs = slice(qh * 512, (qh + 1) * 512)
                p.op('vector', lambda e: e.tensor_tensor(ao2[:], ao1[:], rstd[:], ALU.mult), reads=['ao1', 'rstd'], writes=['ao2'])
                p.op('vector', lambda e: e.tensor_scalar(OT[:, a, qs], ao2[:], sublnT[:, l:l + 1], 1.0 - li, ALU.mult, ALU.mult),
                     reads=['ao2', 'sublnT'], writes=id_OT(a, qh * 512, (qh + 1) * 512))

            prev = None
            for a in range(4):
                for qh in range(2):
                    hooks = {}
                    if prev is not None:
                        hooks[('exp', 5)] = part2a
                        hooks[('pv', 6)] = (lambda pa=prev: part2b(*pa))
                    attn_pair(a, a, qh, KT_ORDER[qh],
                              lambda u, kt: (Vv[:, kt, a * 128:(a + 1) * 128], id_V(kt, a * 128, (a + 1) * 128)),
                              lambda u: (ps[2 + 2 * u][:], 'ps%d' % (2 + 2 * u)),
                              lambda u: (ps[3 + 2 * u][:], 'ps%d' % (3 + 2 * u)), hooks)
                    part1(a, qh)
                    prev = (a, qh)
            part2a(ps[1], 'ps1')
            part2b(*prev)

        def attn_B(l):
            n = 0
            prev = None

            def post(j, qh, ab):
                qs = slice(qh * 512, (qh + 1) * 512)
                for u in range(2):
                    rows = slice(u * 64, (u + 1) * 64)
                    p.op('vector', lambda e: e.reciprocal(ar[0:64, :], ps[ab + u][64:128, :]), reads=['ps%d' % (ab + u)], writes=['ar'])
                    p.op('vector', lambda e: e.tensor_tensor(OT[rows, 4 + j, qs], ps[ab + u][0:64, :], ar[0:64, :], ALU.mult),
                         reads=['ps%d' % (ab + u), 'ar'], writes=id_OT(4 + j, qh * 512, (qh + 1) * 512))
            for j in range(4):
                for qh in range(2):
                    ab = 2 + 2 * (n % 2)
                    n += 1
                    hooks = {}
                    if prev is not None:
                        hooks[('pv', 1)] = (lambda pa=prev: post(*pa))
                    attn_pair(4, j, qh, KT_ORDER[qh],
                              lambda u, kt: (Vv[:, kt, 512 + u * 128:512 + (u + 1) * 128], id_V(kt, 512 + u * 128, 512 + (u + 1) * 128)),
                              lambda u: (ps[ab + u][:], 'ps%d' % (ab + u)), None, hooks)
                    prev = (j, qh, ab)
            post(*prev)

        def attn_C(l):
            cbuf = {(0, 0): tmpf[0], (0, 1): tmpf[1], (1, 0): sq[0], (1, 1): sq[1]}

            def evac(j):
                for qh in range(2):
                    for u in range(2):
                        h = j + 4 * u
                        bk = 2 + 2 * qh + u
                        c_ = cbuf[(qh, u)]
                        col = l * 8 + h
                        if u == 0:
                            p.op('scalar', lambda e: e.activation(c_[:], ps[bk][:], AF.Identity, bias=esink[:, col:col + 1], scale=1.0),
                                 reads=['ps%d' % bk, 'esink'], writes=[c_.name])
                        else:
                            p.op('vector', lambda e: e.tensor_scalar(c_[:], ps[bk][:], esink[:, col:col + 1], None, ALU.add),
                                 reads=['ps%d' % bk, 'esink'], writes=[c_.name])

            def post(j):
                for qh in range(2):
                    qs = slice(qh * 512, (qh + 1) * 512)
                    for u in range(2):
                        rows = slice(u * 64, (u + 1) * 64)
                        c_ = cbuf[(qh, u)]
                        p.op('vector', lambda e: e.reciprocal(ar[0:64, :], c_[64:128, :]), reads=[c_.name], writes=['ar'])
                        p.op('vector', lambda e: e.tensor_tensor(OT[rows, 8 + j, qs], c_[0:64, :], ar[0:64, :], ALU.mult),
                             reads=[c_.name, 'ar'], writes=id_OT(8 + j, qh * 512, (qh + 1) * 512))

            prev = None
            for j in range(4):
                v_fn = lambda u, kt: (Vv[:, kt, 768 + u * 128:768 + (u + 1) * 128], id_V(kt, 768 + u * 128, 768 + (u + 1) * 128))
                for qh in range(2):
                    hooks = {}
                    if qh == 0 and prev is not None:
                        hooks[('pv', 0)] = (lambda pj=prev: post(pj))
                    attn_pair(5, j, qh, list(range(4)), v_fn,
                              lambda u: (ps[2 + 2 * qh + u][:], 'ps%d' % (2 + 2 * qh + u)), None, hooks)

                def c_S(jt):
                    qlo, qhi = max(0, jt - 1) * 128, min(8, jt + 2) * 128
                    N = qhi - qlo
                    for u in range(2):
                        rows = slice(u * 64, (u + 1) * 64)
                        p.op('tensor', lambda e: e.matmul(SB[u][jt % 2][:, 0:N], kT[rows, 5, PAST + jt * 128:PAST + (jt + 1) * 128],
                                                          qT[rows, j, qlo:qhi], start=True, stop=True),
                             reads=id_kT(5, PAST + jt * 128, PAST + (jt + 1) * 128) + id_qT(j, qlo, qhi), writes=[SBN[u][jt % 2]])
                    return (qlo, qhi, N)
                cinfo = {0: c_S(0), 1: c_S(1)}
                for jt in range(8):
                    qlo, qhi, N = cinfo[jt]
                    for u in range(2):
                        P_ = PT[u][jt % 2]
                        p.op('scalar', lambda e: e.activation(P_[:, 0:N], SB[u][jt % 2][:, 0:N], AF.Exp, scale=0.125),
                             reads=[SBN[u][jt % 2]], writes=[P_.name])
                    if jt + 2 < 8:
                        cinfo[jt + 2] = c_S(jt + 2)
                    for u in range(2):
                        P_ = PT[u][jt % 2]
                        if jt >= 1:
                            p.op('vector', lambda e: e.tensor_tensor(P_[:, 0:128], P_[:, 0:128], cmask[:, jt, 0, :], ALU.mult),
                                 reads=[P_.name, 'cmask'], writes=[P_.name])
                        if jt <= 6:
                            p.op('vector', lambda e: e.tensor_tensor(P_[:, N - 128:N], P_[:, N - 128:N], cmask[:, jt, 1, :], ALU.mult),
                                 reads=[P_.name, 'cmask'], writes=[P_.name])
                    for u in range(2):
                        P_ = PT[u][jt % 2]
                        for qh in range(2):
                            lo, hi = max(qlo, qh * 512), min(qhi, (qh + 1) * 512)
                            if lo >= hi:
                                continue
                            osl = slice(lo - qh * 512, hi - qh * 512)
                            psl = slice(lo - qlo, hi - qlo)
                            bk = 2 + 2 * qh + u
                            p.op('tensor', lambda e: e.matmul(ps[bk][:, osl], Vv[:, 4 + jt, 768 + u * 128:768 + (u + 1) * 128], P_[:, psl],
                                                              start=False, stop=False, skip_group_check=True),
                                 reads=id_V(4 + jt, 768 + u * 128, 768 + (u + 1) * 128) + [P_.name], writes=['ps%d' % bk], sig=True)
                evac(j)
                prev = j
            post(prev)

        def qkv_group(l, g):
            Wt, wid = next_block('qkv', l, g)
            pend = []
            for tt in range(8):
                b = 2 + (tt % 4)
                pb = ps[b]
                pid = 'ps%d' % b
                for k in range(8):
                    p.op('tensor', lambda e, k=k, tt=tt, pb=pb: e.matmul(pb[:], hT[:, k, tt * 128:(tt + 1) * 128], Wt[:, k * 512:(k + 1) * 512],
                                                                         start=(k == 0), stop=(k == 7)),
                         reads=['hT%d.%d' % (k, tt // 4), wid], writes=[pid], sig=(k == 7))
                def post(tt=tt, pb=pb, pid=pid):
                    tsl = slice(tt * 128, (tt + 1) * 128)
                    if g == 0:
                        rf_, rb_ = rope_block(pb[:], [pid], 8, tt)
                        transposes(rb_, 4, qT[:, 0:4, tsl], sum([id_qT(c, tt * 128, (tt + 1) * 128) for c in range(4)], []))
                    elif g == 1:
                        rf_, rb_ = rope_block(pb[:], [pid], 8, tt)
                        p.dma('sync', nk_a[l, tsl, :], rf_[:], 'o_' + rf_.name, reads=[rf_.name + 'a', rf_.name + 'b'], is_out=True)
                        transposes(rb_, 4, kT[:, 0:4, PAST + tt * 128:PAST + (tt + 1) * 128],
                                   sum([id_kT(c, PAST + tt * 128, PAST + (tt + 1) * 128) for c in range(4)], []))
                    elif g == 2:
                        v_ = vst[tt % 2]
                        KV = os.environ.get('KVAR', '')
                        if 'a' not in KV:
                            p.op('scalar', lambda e, v_=v_, pb=pb: e.activation(v_[:], pb[:], AF.Copy), reads=[pid], writes=[v_.name])
                        if 'b' not in KV:
                            p.dma('sync', nv_a[l, tsl, :], v_[:], 'o_' + v_.name, reads=[v_.name], is_out=True)
                        if 'c' not in KV:
                            p.op('vector', lambda e, v_=v_, tt=tt: e.tensor_copy(Vv[:, 4 + tt, 0:512], v_[:]), reads=[v_.name], writes=id_V(4 + tt, 0, 512))
                    elif g == 3:
                        xq = (xn, sq[0])[tt % 2]
                        qknorm(pb[:], [pid], 8, gqB[:, l * 64:(l + 1) * 64], 'gqB', xq, tt % 2)
                        rf_, rb_ = rope_block(xq[:], [xq.name], 8, tt)
                        transposes(rb_, 4, qT[:, 0:4, tsl], sum([id_qT(c, tt * 128, (tt + 1) * 128) for c in range(4)], []))
                    elif g == 4:
                        rf_, rb_ = rope_block(pb[:], [pid], 8, tt)
                        transposes(rb_, 4, qT[:, 0:4, tsl], sum([id_qT(c, tt * 128, (tt + 1) * 128) for c in range(4)], []))
                    else:
                        v_ = vst[tt % 2]
                        xq = (xn, sq[0])[tt % 2]
                        p.op('scalar', lambda e, pb=pb: e.activation(xq[:, 128:256], pb[:, 128:256], AF.Copy), reads=[pid], writes=[xq.name])
                        p.op('scalar', lambda e, v_=v_, pb=pb: e.activation(v_[:, 0:256], pb[:, 256:512], AF.Copy), reads=[pid], writes=[v_.name])
                        qknorm(pb[:, 0:128], [pid], 2, gkB[:, l * 64:(l + 1) * 64], 'gkB', xq, tt % 2)
                        rf_, rb_ = rope_block(xq[:, 0:256], [xq.name], 4, tt)
                        p.dma('sync', nk_b[l, tsl, :], rf_[:, 0:128], 'o_' + rf_.name, reads=[rf_.name + 'a', rf_.name + 'b'], is_out=True)
                        p.dma('sync', nk_c[l, tsl, :], rf_[:, 128:256], 'o_' + rf_.name, reads=[rf_.name + 'a', rf_.name + 'b'], is_out=True)
                        transposes(rb_, 2, kT[:, 4:6, PAST + tt * 128:PAST + (tt + 1) * 128],
                                   sum([id_kT(c, PAST + tt * 128, PAST + (tt + 1) * 128) for c in (4, 5)], []))
                        p.dma('sync', nv_b[l, tsl, :], v_[:, 0:128], 'o_' + v_.name, reads=[v_.name], is_out=True)
                        p.dma('sync', nv_c[l, tsl, :], v_[:, 128:256], 'o_' + v_.name, reads=[v_.name], is_out=True)
                        p.op('vector', lambda e, v_=v_, tt=tt: e.tensor_copy(
                            Vv[:, 4 + tt, 512:1024].rearrange("p (a b d) -> p a b d", a=4, b=2, d=64)[:, :, 0, :],
                            v_[:, 0:256].rearrange("p (a d) -> p a d", d=64)), reads=[v_.name],
                             writes=id_V(4 + tt, 512, 1024))
                pend.append(post)
                if len(pend) > 2:
                    pend.pop(0)()
            while pend:
                pend.pop(0)()

        for l in range(depth if not KSTOP else 1):
          try:
            p.epoch = l + 1
            for jg in range(12):
                Wt, wid = next_block('mod', l, jg)
                for j in range(4):
                    col = jg * 4 + j
                    for k in range(8):
                        p.op('tensor', lambda e, Wt=Wt, j=j, k=k, col=col: e.matmul(
                            ps[6][:, col:col + 1], Wt[:, k * 512 + j * 128:k * 512 + (j + 1) * 128], scb[:, k:k + 1],
                            start=(k == 0), stop=(k == 7)),
                            reads=[wid, 'scb'], writes=['ps6'], sig=(k == 7))
            p.op('vector', lambda e, l=l: e.tensor_tensor(modT[:], ps[6][:, 0:48], bmodT[:, l * 48:(l + 1) * 48], ALU.add),
                 reads=['ps6', 'bmodT'], writes=['modT'])
            p.op('vector', lambda e, l=l: e.scalar_tensor_tensor(a1[:], modT[:, 8:16], 1.0, g1T[:, l * 8:(l + 1) * 8], ALU.add, ALU.mult),
                 reads=['modT', 'g1T'], writes=['a1'])
            p.op('vector', lambda e, l=l: e.scalar_tensor_tensor(a2[:], modT[:, 32:40], 1.0, g2T[:, l * 8:(l + 1) * 8], ALU.add, ALU.mult),
                 reads=['modT', 'g2T'], writes=['a2'])
            sh1, gg1, sh2, gg2 = modT[:, 0:8], modT[:, 16:24], modT[:, 24:32], modT[:, 40:48]
            _chk('mod')

            norm_phase(a1, sh1, ['a1', 'modT'], h_out)
            _chk('norm1')
            load_cache(l)
            _chk('cache')
            qkv_group(l, 0)
            _chk('g0')
            qkv_group(l, 1)
            _chk('g1')
            qkv_group(l, 2)
            _chk('g2')
            attn_A(l)
            _chk('aA')
            qkv_group(l, 3)
            _chk('g3')
            qkv_group(l, 5)
            _chk('g5')
            attn_B(l)
            _chk('aB')
            qkv_group(l, 4)
            attn_C(l)
            _chk('aC')

            for oc in range(8):
                Wt, wid = next_block('mrg', l, oc)
                for th in range(2):
                    tsl = slice(th * 512, (th + 1) * 512)
                    for br in range(3):
                        yb_, gb_ = ps[2 * br], ps[2 * br + 1]
                        for k in range(4):
                            kk = br * 4 + k
                            p.op('tensor', lambda e, yb_=yb_, kk=kk, k=k, br=br: e.matmul(
                                yb_[:], Wt[:, kk * 128:(kk + 1) * 128], OT[:, br * 4 + k, tsl], start=(k == 0), stop=(k == 3)),
                                reads=[wid] + id_OT(br * 4 + k, th * 512, (th + 1) * 512), writes=['ps%d' % (2 * br)], sig=(k == 3))
                        for k in range(8):
                            kk = 12 + br * 8 + k
                            p.op('tensor', lambda e, gb_=gb_, kk=kk, k=k: e.matmul(
                                gb_[:], Wt[:, kk * 128:(kk + 1) * 128], hT[:, k, tsl], start=(k == 0), stop=(k == 7)),
                                reads=[wid, 'hT%d.%d' % (k, th)], writes=['ps%d' % (2 * br + 1)], sig=(k == 7))
                        p.op('scalar', lambda e, gb_=gb_: e.activation(sg[:], gb_[:], AF.Sigmoid), reads=['ps%d' % (2 * br + 1)], writes=['sg'])
                        if br == 0:
                            p.op('vector', lambda e, yb_=yb_: e.tensor_tensor(acc[:], yb_[:], sg[:], ALU.mult),
                                 reads=['ps%d' % (2 * br), 'sg'], writes=['ao1'])
                        else:
                            p.op('vector', lambda e, yb_=yb_: e.tensor_tensor(tm2[:], yb_[:], sg[:], ALU.mult),
                                 reads=['ps%d' % (2 * br), 'sg'], writes=['tm2'])
                            if br == 1:
                                p.op('vector', lambda e: e.tensor_tensor(acc[:], acc[:], tm2[:], ALU.add), reads=['ao1', 'tm2'], writes=['ao1'])
                            else:
                                p.op('vector', lambda e, oc=oc, tsl=tsl: e.tensor_tensor(mT[:, oc, tsl], acc[:], tm2[:], ALU.add),
                                     reads=['ao1', 'tm2'], writes=id_mT(oc, th * 512, (th + 1) * 512))
            for og in range(2):
                Wt, wid = next_block('out', l, og)
                for ocl in range(4):
                    oc = og * 4 + ocl
                    for th in range(2):
                        tsl = slice(th * 512, (th + 1) * 512)
                        pb = ps[(ocl * 2 + th) % 4]
                        pid = 'ps%d' % ((ocl * 2 + th) % 4)
                        for k in range(8):
                            p.op('tensor', lambda e, pb=pb, k=k, ocl=ocl, tsl=tsl: e.matmul(
                                pb[:], Wt[:, (ocl * 8 + k) * 128:(ocl * 8 + k + 1) * 128], mT[:, k, tsl], start=(k == 0), stop=(k == 7)),
                                reads=[wid] + id_mT(k, th * 512, (th + 1) * 512), writes=[pid], sig=(k == 7))
                        p.op('vector', lambda e, pb=pb, oc=oc, tsl=tsl: e.scalar_tensor_tensor(
                            xT[:, oc, tsl], pb[:], gg1[:, oc:oc + 1], xT[:, oc, tsl], ALU.mult, ALU.add),
                            reads=[pid, 'modT', 'xT%d.%d' % (oc, th)], writes=['xT%d.%d' % (oc, th)])
            norm_phase(a2, sh2, ['a2', 'modT'], h_out)
            for jb in range(11):
                Wt, wid = next_block('ffi', l, jb)
                for jj in range(2):
                    j = jb * 2 + jj
                    for th in range(2):
                        tsl = slice(th * 512, (th + 1) * 512)
                        n_ = (jj * 2 + th) % 2
                        pa, pbb = ps[2 * n_], ps[2 * n_ + 1]
                        for k in range(8):
                            p.op('tensor', lambda e, pa=pa, k=k, jj=jj, tsl=tsl: e.matmul(
                                pa[:], Wt[:, ((jj * 2) * 8 + k) * 128:((jj * 2) * 8 + k + 1) * 128], hT[:, k, tsl], start=(k == 0), stop=(k == 7)),
                                reads=[wid, 'hT%d.%d' % (k, th)], writes=['ps%d' % (2 * n_)], sig=(k == 7))
                        for k in range(8):
                            p.op('tensor', lambda e, pbb=pbb, k=k, jj=jj, tsl=tsl: e.matmul(
                                pbb[:], Wt[:, ((jj * 2 + 1) * 8 + k) * 128:((jj * 2 + 1) * 8 + k + 1) * 128], hT[:, k, tsl], start=(k == 0), stop=(k == 7)),
                                reads=[wid, 'hT%d.%d' % (k, th)], writes=['ps%d' % (2 * n_ + 1)], sig=(k == 7))
                        p.op('scalar', lambda e, pa=pa: e.activation(sg[:], pa[:], AF.Silu), reads=['ps%d' % (2 * n_)], writes=['sg'])
                        p.op('vector', lambda e, pbb=pbb, j=j, tsl=tsl: e.tensor_tensor(uT[:, j, tsl], pbb[:], sg[:], ALU.mult),
                             reads=['ps%d' % (2 * n_ + 1), 'sg'], writes=id_uT(j, th * 512, (th + 1) * 512))
            for oc in range(8):
                Wt, wid = next_block('ffo', l, oc)
                for th in range(2):
                    tsl = slice(th * 512, (th + 1) * 512)
                    pb = ps[4 + th]
                    pid = 'ps%d' % (4 + th)
                    for j in range(NFF):
                        p.op('tensor', lambda e, pb=pb, j=j, tsl=tsl: e.matmul(
                            pb[:], Wt[:, j * 128:(j + 1) * 128], uT[:, j, tsl], start=(j == 0), stop=(j == NFF - 1)),
                            reads=[wid] + id_uT(j, th * 512, (th + 1) * 512), writes=[pid], sig=(j == NFF - 1))
                    p.op('vector', lambda e, pb=pb, oc=oc, tsl=tsl: e.scalar_tensor_tensor(
                        xT[:, oc, tsl], pb[:], gg2[:, oc:oc + 1], xT[:, oc, tsl], ALU.mult, ALU.add),
                        reads=[pid, 'modT', 'xT%d.%d' % (oc, th)], writes=['xT%d.%d' % (oc, th)])

          except _Stop:
            pass
        norm_phase(gfinT, zero8, ['gfinT', 'zero8'], y_out)
        p.emit()
    return nc


def _feat_major(v):
    return np.ascontiguousarray(v.reshape(8, 128).T)


def make_inputs(inp, depth=DEPTH):
    f32 = np.float32
    Wp = pack_weights(inp, depth)
    rows = np.repeat(np.arange(16), 64).astype(f32)
    cols = np.tile(np.arange(64), 16).astype(f32)
    inv = (10000.0 ** (-np.arange(16, dtype=f32) / 16)).astype(f32)
    ang = np.concatenate([rows[:, None] * inv, cols[:, None] * inv], axis=-1).astype(f32)
    cos_s, sin_s = np.cos(ang).astype(f32), np.sin(ang).astype(f32)
    cos_p, sin_p = np.ones((T, 32), f32), np.zeros((T, 32), f32)
    mb_p = np.full((12, 4), NEGBIG, f32)
    for t in range(8):
        mb_p[4 + t, t // 2] = 0.0
    mb_s = np.zeros((12, 4), f32)
    m01_s = np.ones((8, 4), f32)
    m01_p = np.zeros((8, 4), f32)
    for t in range(8):
        m01_p[t, t // 2] = 1.0
    kl = np.arange(128)[:, None]
    ql = np.arange(128)[None, :]
    cm_s = np.zeros((128, 8, 2, 128), f32)
    cm_p = np.zeros((128, 8, 2, 128), f32)
    for jt in range(8):
        cm_s[:, jt, 0, :] = (kl <= ql)
        cm_s[:, jt, 1, :] = (kl >= ql)
        cm_p[:, jt, 0, :] = 1.0 if (jt % 2 == 1) else 0.0
        cm_p[:, jt, 1, :] = 1.0 if (jt % 2 == 0) else 0.0
    rep = lambda v: np.ascontiguousarray(np.broadcast_to(v.reshape(1, -1), (128, v.size))).astype(f32)
    common = {
        'W': Wp,
        'ident': np.eye(128, dtype=f32),
        'bmodT': np.concatenate([np.ascontiguousarray(inp['b_mod'][l].reshape(48, 128).T) for l in range(depth)], axis=1),
        'g1T': np.concatenate([_feat_major(inp['norm1_g'][l]) for l in range(depth)], axis=1),
        'g2T': np.concatenate([_feat_major(inp['norm2_g'][l]) for l in range(depth)], axis=1),
        'gfinT': _feat_major(inp['final_g']),
        'sublnT': np.ascontiguousarray(inp['a_subln_g'][:depth].T),
        'gqB': rep(inp['b_qnorm_g'][:depth]),
        'gkB': rep(inp['b_knorm_g'][:depth]),
        'lamB': rep(np.concatenate([inp['a_lam_q1'][:depth].ravel(), inp['a_lam_k1'][:depth].ravel(),
                                    inp['a_lam_q2'][:depth].ravel(), inp['a_lam_k2'][:depth].ravel()])),
        'sinkB': rep(inp['c_sink'][:depth]),
    }
    maps = []
    for core in range(8):
        m = dict(common)
        if core < 4:
            xt = inp['x_prompt'][core * 4:(core + 1) * 4].reshape(T, D_MODEL)
            m['cvec'] = _feat_major(inp['c_ctx'])
            m['cKT'] = np.zeros((depth, 768, PAST), f32)
            m['cV'] = np.zeros((depth, PAST, VW), f32)
            m['m01'] = rep(m01_p)
            m['cos_t'], m['sin_t'] = cos_p, sin_p
            m['maskb'] = rep(mb_p)
            m['cmask'] = cm_p.reshape(128, -1)
        else:
            b = core - 4
            xt = inp['x_sample'][b]
            m['cvec'] = _feat_major(inp['c'][b])
            ck = np.concatenate([inp['cache_a_k'][b, :depth].reshape(depth, PAST, 512),
                                 inp['cache_b_k'][b, :depth].reshape(depth, PAST, 128),
                                 inp['cache_c_k'][b, :depth].reshape(depth, PAST, 128)], axis=-1)
            m['cKT'] = np.ascontiguousarray(ck.transpose(0, 2, 1))
            cv = np.ones((depth, PAST, VW), f32)
            cv[:, :, 0:512] = inp['cache_a_v'][b, :depth].reshape(depth, PAST, 512)
            for kv in range(2):
                cv[:, :, 512 + kv * 128:512 + kv * 128 + 64] = inp['cache_b_v'][b, :depth, :, kv, :]
                cv[:, :, 768 + kv * 128:768 + kv * 128 + 64] = inp['cache_c_v'][b, :depth, :, kv, :]
            m['cV'] = cv
            m['m01'] = rep(m01_s)
            m['cos_t'], m['sin_t'] = cos_s, sin_s
            m['maskb'] = rep(mb_s)
            m['cmask'] = cm_s.reshape(128, -1)
        m['xT_in'] = np.ascontiguousarray(xt.T)
        maps.append(m)
    return maps


_NC_CACHE = {}


def run(inp, depth=DEPTH):
    inp = {k: np.asarray(v) for k, v in inp.items()}
    if depth not in _NC_CACHE:
        _NC_CACHE[depth] = build_program(depth)
    nc = _NC_CACHE[depth]
    maps = make_inputs(inp, depth)
    res = run_bass_kernel_spmd(nc, maps, core_ids=list(range(8)))
    r = res.results
    f32 = np.float32
    y_prompt = np.concatenate([r[c]['yT'].T.reshape(4, 256, D_MODEL) for c in range(4)], axis=0).astype(f32)
    y_sample = np.stack([r[4 + b]['yT'].T for b in range(4)], axis=0).astype(f32)

    def gather(name, hd, dd):
        outs = []
        for c in range(4):
            a = r[c][name].reshape(depth, 4, 256, hd, dd).transpose(1, 0, 2, 3, 4)
            outs.append(a)
        return np.ascontiguousarray(np.concatenate(outs, axis=0)).astype(f32)
    return (y_prompt, y_sample,
            gather('nk_a', 4, 128), gather('nv_a', 4, 128),
            gather('nk_b', 2, 64), gather('nv_b', 2, 64),
            gather('nk_c', 2, 64), gather('nv_c', 2, 64))


def kernel(**inputs):
    return run(inputs, DEPTH)
```

```python
import math
import os
import contextlib
import numpy as np
import concourse.bass as bass
import concourse.mybir as mybir
from concourse.bass_utils import run_bass_kernel_spmd

F32 = mybir.dt.float32
BF16 = mybir.dt.bfloat16
ALU = mybir.AluOpType
AF = mybir.ActivationFunctionType
AX = mybir.AxisListType

D_MODEL = 1024
DEPTH = 4
T = 1024
PAST = 512
D_FF = 2816
NFF = 22
EPS = 1e-6
NEGBIG = -30000.0
WSLOT = 4608
NSLOT = 3

O_QA, O_KA, O_VA, O_QB, O_KB, O_VB, O_QC, O_KC, O_VC, O_G = 0, 512, 1024, 1536, 2048, 2176, 2304, 2816, 2944, 3072


def _pair_perm():
    idx = []
    for j in range(4):
        idx += list(range(j * 64, j * 64 + 64)) + list(range((j + 4) * 64, (j + 4) * 64 + 64))
    return np.array(idx)


def qkv_group_cols():
    pp = _pair_perm()
    g = []
    g.append(np.arange(O_QA, O_QA + 512))
    g.append(np.arange(O_KA, O_KA + 512))
    g.append(np.arange(O_VA, O_VA + 512))
    g.append(O_QB + pp)
    g.append(O_QC + pp)
    g.append(np.concatenate([np.arange(O_KB, O_KB + 128), np.arange(O_KC, O_KC + 128),
                             np.arange(O_VB, O_VB + 128), np.arange(O_VC, O_VC + 128)]))
    return g


def weight_plan(depth):
    plan = []
    for l in range(depth):
        for jg in range(12):
            plan.append(('mod', l, jg, 4096))
        for g in (0, 1, 2, 3, 5, 4):
            plan.append(('qkv', l, g, 4096))
        for oc in range(8):
            plan.append(('mrg', l, oc, 4608))
        for og in range(2):
            plan.append(('out', l, og, 4096))
        for jb in range(11):
            plan.append(('ffi', l, jb, 4096))
        for oc in range(8):
            plan.append(('ffo', l, oc, 2816))
    return plan


def _kc(w, cols):
    K = w.shape[0] // 128
    return w[:, cols].reshape(K, 128, len(cols)).transpose(1, 0, 2)


def pack_weights(inp, depth):
    plan = weight_plan(depth)
    tot = sum(b[3] for b in plan)
    W = np.empty((128, tot), np.float32)
    gcols = qkv_group_cols()
    pp = _pair_perm()
    off = 0
    for (kind, l, i, E) in plan:
        if kind == 'mod':
            blk = _kc(inp['w_mod'][l], np.arange(i * 512, i * 512 + 512))
        elif kind == 'qkv':
            blk = _kc(inp['w_in'][l], gcols[i])
        elif kind == 'mrg':
            cs = np.arange(i * 128, i * 128 + 128)
            parts = [_kc(inp['w_br_a'][l], cs),
                     _kc(inp['w_br_b'][l][pp], cs),
                     _kc(inp['w_br_c'][l][pp], cs),
                     _kc(inp['w_in'][l], O_G + cs),
                     _kc(inp['w_in'][l], O_G + 1024 + cs),
                     _kc(inp['w_in'][l], O_G + 2048 + cs)]
            blk = np.concatenate(parts, axis=1)
        elif kind == 'out':
            parts = [_kc(inp['w_out'][l], np.arange((i * 4 + o) * 128, (i * 4 + o) * 128 + 128)) for o in range(4)]
            blk = np.concatenate(parts, axis=1)
        elif kind == 'ffi':
            parts = []
            for jj in range(2):
                j = i * 2 + jj
                parts.append(_kc(inp['w_ffn_in'][l], np.arange(j * 128, j * 128 + 128)))
                parts.append(_kc(inp['w_ffn_in'][l], D_FF + np.arange(j * 128, j * 128 + 128)))
            blk = np.concatenate(parts, axis=1)
        elif kind == 'ffo':
            blk = _kc(inp['w_ffn_out'][l], np.arange(i * 128, i * 128 + 128))
        W[:, off:off + E] = blk.reshape(128, E)
        off += E
    return W


KSTOP = os.environ.get('KSTOP', '')


class _Stop(Exception):
    pass


def _chk(name):
    if KSTOP == name:
        raise _Stop()


class _Rec:
    def __init__(self):
        self.call = None

    def __getattr__(self, name):
        def f(*a, **k):
            self.call = (name, a, k)
            return self
        return f


class Prog:
    ENGS = ['tensor', 'vector', 'scalar', 'gpsimd', 'sync']

    def __init__(self, nc, stack):
        self.nc = nc
        self.stack = stack
        self.q = {e: [] for e in self.ENGS}
        self.epoch = 0
        self.esem = {}
        self.ecnt = {}
        self.dsem = {}
        self.buf = {}
        self.waited = {}
        self.pend = {e: ([], []) for e in self.ENGS}
        self.out_evs = {}

    def _deps(self, eng, reads, writes):
        best = {}

        def add(ev):
            sk, v, prod = ev
            if prod == 'tensor' and eng == 'tensor':
                return
            if best.get(sk, -1) < v:
                best[sk] = v
        for b in reads:
            st = self.buf.get(b)
            if st and st[0] is not None:
                add(st[0])
        for b in writes:
            st = self.buf.get(b)
            if st:
                if st[0] is not None:
                    add(st[0])
                for ev in st[1]:
                    add(ev)
        out = []
        for sk, v in best.items():
            key = (eng, sk)
            if self.waited.get(key, -1) >= v:
                continue
            self.waited[key] = v
            out.append((sk, v))
        return out

    def _commit(self, ev, reads, writes):
        for b in writes:
            self.buf[b] = [ev, []]
        for b in reads:
            st = self.buf.get(b)
            if st is None:
                st = self.buf[b] = [None, []]
            st[1].append(ev)

    def op(self, eng, fn, reads=(), writes=(), sig=True):
        reads = list(reads)
        writes = list(writes)
        waits = self._deps(eng, reads, writes)
        rec = _Rec()
        fn(rec)
        fn = rec.call
        assert fn is not None
        if not sig:
            self.q[eng].append((fn, waits, None))
            self.pend[eng][0].extend(reads)
            self.pend[eng][1].extend(writes)
            return None
        ek = 'E%s@%d' % (eng, self.epoch)
        if ek not in self.esem:
            self.esem[ek] = self.stack.enter_context(self.nc.semaphore("es_%s_%d" % (eng, self.epoch)))
            self.ecnt[ek] = 0
        self.ecnt[ek] += 1
        ev = (ek, self.ecnt[ek], eng)
        self.q[eng].append((fn, waits, ('E', ek)))
        pr, pw = self.pend[eng]
        self._commit(ev, reads + pr, writes + pw)
        self.pend[eng] = ([], [])
        return ev

    def dma(self, eng, out, in_, sem, reads=(), writes=(), is_out=False):
        waits = self._deps(eng, reads, writes)
        sem = '%s@%d' % (sem, self.epoch)
        if sem not in self.dsem:
            self.dsem[sem] = [self.stack.enter_context(self.nc.semaphore("ds_" + sem.replace('@', '_'))), 0]
        s = self.dsem[sem]
        s[1] += 16
        ev = ('D' + sem, s[1], 'dma')
        self.q[eng].append((('dma_start', (), {'out': out, 'in_': in_}), waits, ('D', sem)))
        self._commit(ev, list(reads), list(writes))
        if is_out:
            self.out_evs[sem] = ev
        return ev

    def _semh(self, sk):
        if sk[0] == 'E':
            return self.esem[sk]
        return self.dsem[sk[1:]][0]

    def emit(self):
        nc = self.nc
        self.q['sync'].append((None, [(ev[0], ev[1]) for ev in self.out_evs.values()], None))
        with nc.Block() as block:
            for ename in self.ENGS:
                ops = self.q[ename]

                def body(eng, ops=ops):
                    for (fn, waits, inc) in ops:
                        for (sk, v) in waits:
                            eng.wait_ge(self._semh(sk), v)
                        if fn is None:
                            continue
                        ins = getattr(eng, fn[0])(*fn[1], **fn[2])
                        if inc is None:
                            continue
                        if inc[0] == 'E':
                            ins.then_inc(self.esem[inc[1]], 1)
                        else:
                            ins.then_inc(self.dsem[inc[1]][0], 16)
                getattr(block, ename)(body)


RB = 512
R_QT, R_KT, R_V, R_OT = 0, 4096, 13312, 25600
R_TOT = 37888
VW = 1024


def rid(lo, hi):
    return ['R%d' % i for i in range(lo // RB, (hi - 1) // RB + 1)]


def build_program(depth=DEPTH):
    nc = bass.Bass("TRN2", target_bir_lowering=False)
    plan = weight_plan(depth)
    wtot = sum(b[3] for b in plan)

    def din(name, shape):
        return nc.dram_tensor(name, list(shape), F32, kind="ExternalInput").ap()

    def dout(name, shape):
        return nc.dram_tensor(name, list(shape), F32, kind="ExternalOutput").ap()

    W = din("W", [128, wtot])
    xT_in = din("xT_in", [1024, T])
    cvec_d = din("cvec", [128, 8])
    cKT = din("cKT", [depth, 768, PAST])
    cV = din("cV", [depth, PAST, VW])
    m01_d = din("m01", [128, 32])
    cos_d = din("cos_t", [T, 32])
    sin_d = din("sin_t", [T, 32])
    maskb_d = din("maskb", [128, 48])
    cmask_d = din("cmask", [128, 8 * 2 * 128])
    ident_d = din("ident", [128, 128])
    bmod_d = din("bmodT", [128, depth * 48])
    g1_d = din("g1T", [128, depth * 8])
    g2_d = din("g2T", [128, depth * 8])
    gfin_d = din("gfinT", [128, 8])
    subln_d = din("sublnT", [128, depth])
    gq_d = din("gqB", [128, depth * 64])
    gk_d = din("gkB", [128, depth * 64])
    lam_d = din("lamB", [128, 4 * depth * 64])
    sink_d = din("sinkB", [128, depth * 8])

    yT_out = dout("yT", [1024, T])
    nk_a = dout("nk_a", [depth, T, 512])
    nv_a = dout("nv_a", [depth, T, 512])
    nk_b = dout("nk_b", [depth, T, 128])
    nv_b = dout("nv_b", [depth, T, 128])
    nk_c = dout("nk_c", [depth, T, 128])
    nv_c = dout("nv_c", [depth, T, 128])

    with contextlib.ExitStack() as st:
        def sb(name, shape, dt):
            return st.enter_context(nc.sbuf_tensor(name, list(shape), dt))

        def psum(name, shape, dt):
            return st.enter_context(nc.psum_tensor(name, list(shape), dt))

        xT = sb("xT", [128, 8, T], F32)
        hT = sb("hT", [128, 8, T], BF16)
        R = sb("R", [128, R_TOT], BF16)
        wsl = [sb("wsl%d" % i, [128, WSLOT], BF16) for i in range(NSLOT)]
        cos_s = sb("cos_s", [128, 8, 32], F32)
        sin_s = sb("sin_s", [128, 8, 32], F32)
        maskb = sb("maskb_s", [128, 48], F32)
        cmask = sb("cmask_s", [128, 8, 2, 128], BF16)
        ident = sb("ident_s", [128, 128], BF16)
        ones_f = sb("ones_f", [128, 128], F32)
        ones_b = sb("ones_b", [128, 128], BF16)
        bmodT = sb("bmodT_s", [128, depth * 48], F32)
        g1T = sb("g1T_s", [128, depth * 8], F32)
        g2T = sb("g2T_s", [128, depth * 8], F32)
        gfinT = sb("gfinT_s", [128, 8], F32)
        sublnT = sb("sublnT_s", [128, depth], F32)
        gqB = sb("gqB_s", [128, depth * 64], F32)
        gkB = sb("gkB_s", [128, depth * 64], F32)
        lamS = sb("lamS", [128, 2 * depth], F32)
        neglam = sb("neglam", [128, depth], F32)
        esink = sb("esink", [128, depth * 8], F32)
        cvec = sb("cvec_s", [128, 8], F32)
        scb = sb("scb", [128, 8], BF16)
        modT = sb("modT", [128, 48], F32)
        a1 = sb("a1", [128, 8], F32)
        a2 = sb("a2", [128, 8], F32)
        zero8 = sb("zero8", [128, 8], F32)
        rstd = sb("rstd", [128, 512], F32)
        lnv = sb("lnv", [128, 512], F32)
        sq = [sb("sq%d" % i, [128, 512], F32) for i in range(2)]
        tmpf = [sb("tmpf%d" % i, [128, 512], F32) for i in range(2)]
        rf = [sb("rf%d" % i, [128, 512], F32) for i in range(2)]
        rbt = [sb("rb%d" % i, [128, 512], BF16) for i in range(2)]
        vst = [sb("vst%d" % i, [128, 512], F32) for i in range(2)]
        xn = sb("xn", [128, 512], F32)
        rt = [sb("rt%d" % i, [128, 256], F32) for i in range(4)]
        ss8 = sb("ss8", [128, 16], F32)
        l8 = sb("l8", [128, 16], F32)
        r8 = sb("r8", [128, 16], F32)
        Pt = [sb("Pt%d" % i, [128, 512], BF16) for i in range(4)]
        m01 = sb("m01_s", [128, 32], F32)
        ar = sb("ar", [128, 512], F32)
        ao1 = sb("ao1", [128, 512], F32)
        ao2 = sb("ao2", [128, 512], F32)
        sg = sb("sg", [128, 512], F32)
        tm2 = sb("tm2", [128, 512], F32)

        acc = ao1
        yst = vst
        ps = [psum("ps%d" % i, [128, 512], F32) for i in range(8)]
        psB = ps[7][:].bitcast(BF16)

        p = Prog(nc, st)

        qT = R[:, R_QT:R_QT + 4096].rearrange("p (c t) -> p c t", t=1024)
        kT = R[:, R_KT:R_KT + 9216].rearrange("p (c t) -> p c t", t=1536)
        Vv = R[:, R_V:R_V + 12 * VW].rearrange("p (k f) -> p k f", f=VW)
        OT = R[:, R_OT:R_OT + 12288].rearrange("p (c t) -> p c t", t=1024)
        mT = R[:, 0:8192].rearrange("p (c t) -> p c t", t=1024)
        uT = R[:, 0:22528].rearrange("p (c t) -> p c t", t=1024)

        def id_qT(c, lo=0, hi=1024):
            return rid(R_QT + c * 1024 + lo, R_QT + c * 1024 + hi)

        def id_kT(c, lo, hi):
            return rid(R_KT + c * 1536 + lo, R_KT + c * 1536 + hi)

        def id_V(kt, lo, hi):
            return rid(R_V + kt * VW + lo, R_V + kt * VW + hi)

        def id_OT(c, lo, hi):
            return rid(R_OT + c * 1024 + lo, R_OT + c * 1024 + hi)

        def id_mT(c, lo, hi):
            return rid(c * 1024 + lo, c * 1024 + hi)

        id_uT = id_mT

        for c in range(8):
            p.dma('sync', xT[:, c, :], xT_in[c * 128:(c + 1) * 128, :], 'ldx%d' % c, writes=['xT%d.0' % c, 'xT%d.1' % c])
        small = [(cvec, cvec_d, 'cvec'), (maskb, maskb_d, 'maskb'), (bmodT, bmod_d, 'bmodT'), (g1T, g1_d, 'g1T'),
                 (g2T, g2_d, 'g2T'), (gfinT, gfin_d, 'gfinT'), (sublnT, subln_d, 'sublnT'), (gqB, gq_d, 'gqB'),
                 (gkB, gk_d, 'gkB'), (esink, sink_d, 'esink')]
        for (t_, d_, n_) in small:
            p.dma('sync', t_[:], d_[:], 'lds_' + n_, writes=[n_])
        p.dma('sync', cos_s[:], cos_d.rearrange("(t p) d -> p t d", p=128), 'lds_cos', writes=['cos'])
        p.dma('sync', sin_s[:], sin_d.rearrange("(t p) d -> p t d", p=128), 'lds_sin', writes=['sin'])
        p.dma('gpsimd', ident[:], ident_d[:], 'ldc_i', writes=['ident'])
        p.dma('gpsimd', cmask[:].rearrange("p a b c -> p (a b c)"), cmask_d[:], 'ldc_m', writes=['cmask'])
        p.op('vector', lambda e: e.memset(ones_f[:], 1.0), writes=['ones_f'])
        p.op('vector', lambda e: e.memset(ones_b[:], 1.0), writes=['ones_b'])
        p.op('vector', lambda e: e.memset(zero8[:], 0.0), writes=['zero8'])
        p.op('scalar', lambda e: e.activation(scb[:], cvec[:], AF.Silu), reads=['cvec'], writes=['scb'])
        p.op('scalar', lambda e: e.activation(esink[:], esink[:], AF.Exp), reads=['esink'], writes=['esink'])
        p.op('vector', lambda e: e.memset(esink[0:64, :], 0.0), reads=['esink'], writes=['esink'])
        n64 = depth * 64
        p.dma('sync', tmpf[0][:, 0:2 * n64], lam_d[:, 0:2 * n64], 'lds_lam1', writes=['tmpf0'])
        p.dma('sync', tmpf[1][:, 0:2 * n64], lam_d[:, 2 * n64:4 * n64], 'lds_lam2', writes=['tmpf1'])
        p.dma('sync', m01[:], m01_d[:], 'lds_m01', writes=['m01'])
        p.op('vector', lambda e: e.tensor_tensor(sq[0][:, 0:n64], tmpf[0][:, 0:n64], tmpf[0][:, n64:2 * n64], ALU.mult),
             reads=['tmpf0'], writes=['sq0'])
        p.op('vector', lambda e: e.tensor_tensor(sq[0][:, n64:2 * n64], tmpf[1][:, 0:n64], tmpf[1][:, n64:2 * n64], ALU.mult),
             reads=['tmpf1', 'sq0'], writes=['sq0'])
        p.op('vector', lambda e: e.tensor_reduce(lamS[:], sq[0][:, 0:2 * n64].rearrange("p (a d) -> p a d", d=64), AX.X, ALU.add),
             reads=['sq0'], writes=['lamS'])
        p.op('scalar', lambda e: e.activation(lamS[:], lamS[:], AF.Exp), reads=['lamS'], writes=['lamS'])
        p.op('vector', lambda e: e.tensor_tensor(neglam[:], lamS[:, depth:2 * depth], lamS[:, 0:depth], ALU.subtract),
             reads=['lamS'], writes=['neglam'])
        for l in range(depth):
            li = 0.8 - 0.6 * math.exp(-0.3 * l)
            p.op('vector', lambda e, l=l, li=li: e.tensor_scalar(neglam[:, l:l + 1], neglam[:, l:l + 1], -li, None, ALU.add),
                 reads=['neglam'], writes=['neglam'])

        wstate = {'next': 0, 'off': 0}
        wslot_of = {}

        def issue_loads(upto):
            while wstate['next'] < min(upto, len(plan)):
                i = wstate['next']
                E = plan[i][3]
                s = i % NSLOT
                p.dma('gpsimd', wsl[s][:, 0:E], W[:, wstate['off']:wstate['off'] + E], 'w%d' % s, writes=['W%d' % s])
                wslot_of[i] = s
                wstate['off'] += E
                wstate['next'] += 1

        bidx = {'i': 0}

        def next_block(kind, l, i):
            b = bidx['i']
            assert plan[b][:3] == (kind, l, i), (plan[b], kind, l, i)
            issue_loads(b + NSLOT)
            bidx['i'] += 1
            s = wslot_of[b]
            return wsl[s], 'W%d' % s

        rs = [rstd, lnv]
        accs = [ar, ao2]
        banks = [(ps[0], 'ps0'), (ps[1], 'ps1')]

        def norm_phase(a_ap, sh_ap, a_ids, out_fn):
            norm_front()
            norm_pe()
            norm_apply(a_ap, sh_ap, a_ids, out_fn)

        def norm_front():
            for th in range(2):
                tsl = slice(th * 512, (th + 1) * 512)
                acc_ = accs[th]
                for c in range(8):
                    s_ = sq[c % 2]
                    p.op('scalar', lambda e: e.activation(s_[:], xT[:, c, tsl], AF.Square),
                         reads=['xT%d.%d' % (c, th)], writes=[s_.name])
                    if c == 1:
                        p.op('vector', lambda e: e.tensor_tensor(acc_[:], sq[0][:], sq[1][:], ALU.add),
                             reads=['sq0', 'sq1'], writes=[acc_.name])
                    elif c >= 2:
                        p.op('vector', lambda e: e.tensor_tensor(acc_[:], acc_[:], s_[:], ALU.add),
                             reads=[acc_.name, s_.name], writes=[acc_.name])

        def norm_pe():
            for th in range(2):
                bk, bid = banks[th]
                acc_ = accs[th]
                p.op('tensor', lambda e: e.matmul(bk[:], ones_f[:], acc_[:], start=True, stop=True),
                     reads=[acc_.name, 'ones_f'], writes=[bid])
            for th in range(2):
                bk, bid = banks[th]
                r_ = rs[th]
                p.op('scalar', lambda e: e.activation(r_[:], bk[:], AF.Ln, bias=EPS, scale=1.0 / 1024),
                     reads=[bid], writes=[r_.name])
                p.op('scalar', lambda e: e.activation(r_[:], r_[:], AF.Exp, scale=-0.5), reads=[r_.name], writes=[r_.name])

        def norm_apply(a_ap, sh_ap, a_ids, out_fn):
            for th in range(2):
                tsl = slice(th * 512, (th + 1) * 512)
                r_ = rs[th]
                for c in range(8):
                    t_ = tmpf[c % 2]
                    p.op('vector', lambda e: e.tensor_tensor(t_[:], xT[:, c, tsl], r_[:], ALU.mult),
                         reads=['xT%d.%d' % (c, th), r_.name], writes=[t_.name])
                    out_fn(c, th, tsl, t_, a_ap, sh_ap, a_ids)

        def h_out(c, th, tsl, t_, a_ap, sh_ap, a_ids):
            p.op('scalar', lambda e: e.activation(hT[:, c, tsl], t_[:], AF.Identity, bias=sh_ap[:, c:c + 1], scale=a_ap[:, c:c + 1]),
                 reads=[t_.name] + a_ids, writes=['hT%d.%d' % (c, th)])

        def y_out(c, th, tsl, t_, a_ap, sh_ap, a_ids):
            y_ = yst[c % 2]
            p.op('scalar', lambda e: e.activation(y_[:], t_[:], AF.Identity, bias=sh_ap[:, c:c + 1], scale=a_ap[:, c:c + 1]),
                 reads=[t_.name] + a_ids, writes=[y_.name])
            p.dma('sync', yT_out[c * 128:(c + 1) * 128, tsl], y_[:], 'o_' + y_.name, reads=[y_.name], is_out=True)

        rope_n = {'i': 0}

        def rope_block(src, src_ids, U, tt):
            i = rope_n['i'] % 2
            rope_n['i'] += 1
            rf_, rb_ = rf[i], rbt[i]
            n = U * 64
            xs = src.rearrange("p (u two d) -> p u two d", two=2, d=32)
            x1, x2 = xs[:, :, 0, :], xs[:, :, 1, :]
            ds_ = rf_[:, 0:n].rearrange("p (u two d) -> p u two d", two=2, d=32)
            d1, d2 = ds_[:, :, 0, :], ds_[:, :, 1, :]
            cB = cos_s[:, tt, :].unsqueeze(1).broadcast_to([128, U, 32])
            sB = sin_s[:, tt, :].unsqueeze(1).broadcast_to([128, U, 32])
            tv = [rt[k][:, 0:U * 32].rearrange("p (u d) -> p u d", d=32) for k in range(4)]
            p.op('vector', lambda e: e.tensor_tensor(tv[0], x1, cB, ALU.mult), reads=src_ids + ['cos'], writes=['rt0'])
            p.op('vector', lambda e: e.tensor_tensor(tv[1], x2, sB, ALU.mult), reads=src_ids + ['sin'], writes=['rt1'])
            p.op('vector', lambda e: e.tensor_tensor(tv[2], x2, cB, ALU.mult), reads=src_ids + ['cos'], writes=['rt2'])
            p.op('vector', lambda e: e.tensor_tensor(tv[3], x1, sB, ALU.mult), reads=src_ids + ['sin'], writes=['rt3'])
            p.op('vector', lambda e: e.tensor_tensor(d1, tv[0], tv[1], ALU.subtract), reads=['rt0', 'rt1'], writes=[rf_.name + 'a'])
            p.op('vector', lambda e: e.tensor_tensor(d2, tv[2], tv[3], ALU.add), reads=['rt2', 'rt3'], writes=[rf_.name + 'b'])
            p.op('scalar', lambda e: e.activation(rb_[:, 0:n], rf_[:, 0:n], AF.Copy),
                 reads=[rf_.name + 'a', rf_.name + 'b'], writes=[rb_.name])
            return rf_, rb_

        tr_n = {'i': 0}

        def transposes(rb_, nblk, dst_view, dst_ids):
            h = tr_n['i'] % 2
            tr_n['i'] += 1
            pb = psB[:, h * 512:h * 512 + nblk * 128]
            for b in range(nblk):
                p.op('tensor', lambda e, b=b: e.transpose(psB[:, h * 512 + b * 128:h * 512 + (b + 1) * 128],
                                                           rb_[:, b * 128:(b + 1) * 128], ident[:]),
                     reads=[rb_.name, 'ident'], writes=['psB%d' % h, 'ps7'], sig=(b == nblk - 1))
            p.op('vector', lambda e: e.tensor_copy(dst_view, pb.rearrange("p (n t) -> p n t", t=128)),
                 reads=['psB%d' % h, 'ps7'], writes=dst_ids)

        def qknorm(src_ps, src_ids, nh, g_ap, g_id, dst, par):
            n = nh * 64
            tq = (tm2, sq[1])[par]
            o8 = par * 8
            sid, lid, rid_ = 'ss8_%d' % par, 'l8_%d' % par, 'r8_%d' % par
            p.op('scalar', lambda e: e.activation(tq[:, 0:n], src_ps, AF.Square), reads=src_ids, writes=[tq.name])
            p.op('vector', lambda e: e.tensor_reduce(ss8[:, o8:o8 + nh], tq[:, 0:n].rearrange("p (h d) -> p h d", d=64), AX.X, ALU.add),
                 reads=[tq.name], writes=[sid])
            p.op('scalar', lambda e: e.activation(l8[:, o8:o8 + nh], ss8[:, o8:o8 + nh], AF.Ln, bias=EPS, scale=1.0 / 64),
                 reads=[sid], writes=[lid])
            p.op('scalar', lambda e: e.activation(r8[:, o8:o8 + nh], l8[:, o8:o8 + nh], AF.Exp, scale=-0.5), reads=[lid], writes=[rid_])
            for h in range(nh):
                p.op('scalar', lambda e, h=h: e.activation(dst[:, h * 64:(h + 1) * 64], src_ps[:, h * 64:(h + 1) * 64], AF.Identity,
                                                           scale=r8[:, o8 + h:o8 + h + 1]),
                     reads=src_ids + [rid_], writes=[dst.name], sig=(h == nh - 1))
            dv = dst[:, 0:n].rearrange("p (h d) -> p h d", d=64)
            p.op('vector', lambda e: e.tensor_tensor(dv, dv, g_ap.unsqueeze(1).broadcast_to([128, nh, 64]), ALU.mult),
                 reads=[dst.name, g_id], writes=[dst.name])

        SB = [[ps[0], ps[1]], [ps[6], ps[7]]]
        KT_ORDER = [[0, 1, 2, 3, 8, 9, 10, 11, 4, 5, 6, 7], list(range(12))]
        SBN = [['ps0', 'ps1'], ['ps6', 'ps7']]
        PT = [[Pt[0], Pt[1]], [Pt[2], Pt[3]]]

        def attn_pair(kc, qc, qh, kts, v_fn, acc_fn, d_fn, hooks=None):
            hooks = hooks or {}
            n = len(kts)
            qs = slice(qh * 512, (qh + 1) * 512)
            qids = id_qT(qc, qh * 512, (qh + 1) * 512)

            def do_S(i):
                kt = kts[i]
                for u in range(2):
                    rows = slice(u * 64, (u + 1) * 64)
                    p.op('tensor', lambda e: e.matmul(SB[u][i % 2][:], kT[rows, kc, kt * 128:(kt + 1) * 128], qT[rows, qc, qs],
                                                      start=True, stop=True),
                         reads=id_kT(kc, kt * 128, (kt + 1) * 128) + qids, writes=[SBN[u][i % 2]])
            do_S(0)
            if n > 1:
                do_S(1)
            for i in range(n):
                kt = kts[i]
                for u in range(2):
                    P_ = PT[u][i % 2]
                    if kt < 4 or (kt - 4) // 4 != qh:
                        bc = kt * 4 + 2 * qh
                        p.op('scalar', lambda e: e.activation(P_[:], SB[u][i % 2][:], AF.Exp, bias=maskb[:, bc:bc + 1], scale=0.125),
                             reads=[SBN[u][i % 2], 'maskb'], writes=[P_.name])
                    else:
                        p.op('scalar', lambda e: e.activation(P_[:], SB[u][i % 2][:], AF.Exp, scale=0.125),
                             reads=[SBN[u][i % 2]], writes=[P_.name])
                if ('exp', i) in hooks:
                    hooks[('exp', i)](SB[0][i % 2], SBN[0][i % 2])
                if kt >= 4 and (kt - 4) // 4 == qh:
                    c0 = (kt - 4) * 4 + 2 * qh
                    for u in range(2):
                        P_ = PT[u][i % 2]
                        for sq_ in range(2):
                            p.op('vector', lambda e: e.tensor_scalar(P_[:, sq_ * 256:(sq_ + 1) * 256], P_[:, sq_ * 256:(sq_ + 1) * 256],
                                                                     m01[:, c0 + sq_:c0 + sq_ + 1], None, ALU.mult),
                                 reads=[P_.name, 'm01'], writes=[P_.name])
                for u in range(2):
                    if u == 1 and i + 2 < n:
                        do_S(i + 2)
                    P_ = PT[u][i % 2]
                    acc, aid = acc_fn(u)
                    v_ap, v_ids = v_fn(u, kt)
                    dd = d_fn(u) if d_fn is not None else None
                    p.op('tensor', lambda e: e.matmul(acc, v_ap, P_[:], start=(i == 0), stop=(i == n - 1), skip_group_check=True),
                         reads=v_ids + [P_.name], writes=[aid], sig=(dd is None))
                    if dd is not None:
                        p.op('tensor', lambda e: e.matmul(dd[0], ones_b[:], P_[:], start=(i == 0), stop=(i == n - 1), skip_group_check=True),
                             reads=['ones_b', P_.name], writes=[dd[1]], sig=True)
                if ('pv', i) in hooks:
                    hooks[('pv', i)]()

        def load_cache(l):
            p.dma('gpsimd', kT[:, 0:6, 0:PAST], cKT[l].rearrange("(c p) k -> p c k", p=128), 'ldk',
                  writes=sum([id_kT(c, 0, PAST) for c in range(6)], []))
            p.dma('gpsimd', Vv[:, 0:4, :], cV[l].rearrange("(t p) f -> p t f", p=128), 'ldv',
                  writes=rid(R_V, R_V + 4 * VW))
            for kt in range(4, 12):
                ov = Vv[:, kt, 512:1024].rearrange("p (a b d) -> p a b d", a=4, b=2, d=64)[:, :, 1, :]
                p.op('vector', lambda e: e.memset(ov, 1.0), writes=id_V(kt, 512, 1024))

        def attn_A(l):
            li = 0.8 - 0.6 * math.exp(-0.3 * l)
            d1s, d2s, o1c, o2c = tmpf[0], tmpf[1], sq[0], sq[1]
            pend = {}

            def part1(a, qh):
                p.op('scalar', lambda e: e.activation(d1s[:], ps[3][:], AF.Copy), reads=['ps3'], writes=['tmpf0'])
                p.op('scalar', lambda e: e.activation(d2s[:], ps[5][:], AF.Copy), reads=['ps5'], writes=['tmpf1'])
                p.op('vector', lambda e: e.tensor_copy(o1c[:], ps[2][:]), reads=['ps2'], writes=['sq0'])
                p.op('vector', lambda e: e.tensor_copy(o2c[:], ps[4][:]), reads=['ps4'], writes=['sq1'])
                p.op('vector', lambda e: e.reciprocal(ar[:], d1s[:]), reads=['tmpf0'], writes=['ar'])
                p.op('vector', lambda e: e.tensor_tensor(ao1[:], o1c[:], ar[:], ALU.mult), reads=['sq0', 'ar'], writes=['ao1'])
                p.op('vector', lambda e: e.reciprocal(ar[:], d2s[:]), reads=['tmpf1', 'ar'], writes=['ar'])
                p.op('vector', lambda e: e.tensor_tensor(ao2[:], o2c[:], ar[:], ALU.mult), reads=['sq1', 'ar'], writes=['ao2'])
                p.op('vector', lambda e: e.scalar_tensor_tensor(ao1[:], ao2[:], neglam[:, l:l + 1], ao1[:], ALU.mult, ALU.add),
                     reads=['ao1', 'ao2', 'neglam'], writes=['ao1'])

            def part2a(bank, bid):
                p.op('scalar', lambda e: e.activation(sg[:], ao1[:], AF.Square), reads=['ao1'], writes=['sg'])
                p.op('tensor', lambda e: e.matmul(bank[:], ones_f[:], sg[:], start=True, stop=True),
                     reads=['ones_f', 'sg'], writes=[bid])
                p.op('scalar', lambda e: e.activation(lnv[:], bank[:], AF.Ln, bias=EPS, scale=1.0 / 128),
                     reads=[bid], writes=['lnv'])
                p.op('scalar', lambda e: e.activation(rstd[:], lnv[:], AF.Exp, scale=-0.5), reads=['lnv'], writes=['rstd'])

            def part2b(a, qh):
                qs = slice(qh * 512, (qh + 1) * 512)
                p.op('vector', lambda e: e.tensor_tensor(ao2[:], ao1[:], rstd[:], ALU.mult), reads=['ao1', 'rstd'], writes=['ao2'])
                p.op('vector', lambda e: e.tensor_scalar(OT[:, a, qs], ao2[:], sublnT[:, l:l + 1], 1.0 - li, ALU.mult, ALU.mult),
                     reads=['ao2', 'sublnT'], writes=id_OT(a, qh * 512, (qh + 1) * 512))

            prev = None
            for a in range(4):
                for qh in range(2):
                    hooks = {}
                    if prev is not None:
                        hooks[('exp', 5)] = part2a
                        hooks[('pv', 6)] = (lambda pa=prev: part2b(*pa))
                    attn_pair(a, a, qh, KT_ORDER[qh],
                              lambda u, kt: (Vv[:, kt, a * 128:(a + 1) * 128], id_V(kt, a * 128, (a + 1) * 128)),
                              lambda u: (ps[2 + 2 * u][:], 'ps%d' % (2 + 2 * u)),
                              lambda u: (ps[3 + 2 * u][:], 'ps%d' % (3 + 2 * u)), hooks)
                    part1(a, qh)
                    prev = (a, qh)
            part2a(ps[1], 'ps1')
            part2b(*prev)

        def attn_B(l):
            n = 0
            prev = None

            def post(j, qh, ab):
                qs = slice(qh * 512, (qh + 1) * 512)
                for u in range(2):
                    rows = slice(u * 64, (u + 1) * 64)
                    p.op('vector', lambda e: e.reciprocal(ar[0:64, :], ps[ab + u][64:128, :]), reads=['ps%d' % (ab + u)], writes=['ar'])
                    p.op('vector', lambda e: e.tensor_tensor(OT[rows, 4 + j, qs], ps[ab + u][0:64, :], ar[0:64, :], ALU.mult),
                         reads=['ps%d' % (ab + u), 'ar'], writes=id_OT(4 + j, qh * 512, (qh + 1) * 512))
            for j in range(4):
                for qh in range(2):
                    ab = 2 + 2 * (n % 2)
                    n += 1
                    hooks = {}
                    if prev is not None:
                        hooks[('pv', 1)] = (lambda pa=prev: post(*pa))
                    attn_pair(4, j, qh, KT_ORDER[qh],
                              lambda u, kt: (Vv[:, kt, 512 + u * 128:512 + (u + 1) * 128], id_V(kt, 512 + u * 128, 512 + (u + 1) * 128)),
                              lambda u: (ps[ab + u][:], 'ps%d' % (ab + u)), None, hooks)
                    prev = (j, qh, ab)
            post(*prev)

        def attn_C(l):
            cbuf = {(0, 0): tmpf[0], (0, 1): tmpf[1], (1, 0): sq[0], (1, 1): sq[1]}

            def evac(j):
                for qh in range(2):
                    for u in range(2):
                        h = j + 4 * u
                        bk = 2 + 2 * qh + u
                        c_ = cbuf[(qh, u)]
                        col = l * 8 + h
                        if u == 0:
                            p.op('scalar', lambda e: e.activation(c_[:], ps[bk][:], AF.Identity, bias=esink[:, col:col + 1], scale=1.0),
                                 reads=['ps%d' % bk, 'esink'], writes=[c_.name])
                        else:
                            p.op('vector', lambda e: e.tensor_scalar(c_[:], ps[bk][:], esink[:, col:col + 1], None, ALU.add),
                                 reads=['ps%d' % bk, 'esink'], writes=[c_.name])

            def post(j):
                for qh in range(2):
                    qs = slice(qh * 512, (qh + 1) * 512)
                    for u in range(2):
                        rows = slice(u * 64, (u + 1) * 64)
                        c_ = cbuf[(qh, u)]
                        p.op('vector', lambda e: e.reciprocal(ar[0:64, :], c_[64:128, :]), reads=[c_.name], writes=['ar'])
                        p.op('vector', lambda e: e.tensor_tensor(OT[rows, 8 + j, qs], c_[0:64, :], ar[0:64, :], ALU.mult),
                             reads=[c_.name, 'ar'], writes=id_OT(8 + j, qh * 512, (qh + 1) * 512))

            prev = None
            for j in range(4):
                v_fn = lambda u, kt: (Vv[:, kt, 768 + u * 128:768 + (u + 1) * 128], id_V(kt, 768 + u * 128, 768 + (u + 1) * 128))
                for qh in range(2):
                    hooks = {}
                    if qh == 0 and prev is not None:
                        hooks[('pv', 0)] = (lambda pj=prev: post(pj))
                    attn_pair(5, j, qh, list(range(4)), v_fn,
                              lambda u: (ps[2 + 2 * qh + u][:], 'ps%d' % (2 + 2 * qh + u)), None, hooks)

                def c_S(jt):
                    qlo, qhi = max(0, jt - 1) * 128, min(8, jt + 2) * 128
                    N = qhi - qlo
                    for u in range(2):
                        rows = slice(u * 64, (u + 1) * 64)
                        p.op('tensor', lambda e: e.matmul(SB[u][jt % 2][:, 0:N], kT[rows, 5, PAST + jt * 128:PAST + (jt + 1) * 128],
                                                          qT[rows, j, qlo:qhi], start=True, stop=True),
                             reads=id_kT(5, PAST + jt * 128, PAST + (jt + 1) * 128) + id_qT(j, qlo, qhi), writes=[SBN[u][jt % 2]])
                    return (qlo, qhi, N)
                cinfo = {0: c_S(0), 1: c_S(1)}
                for jt in range(8):
                    qlo, qhi, N = cinfo[jt]
                    for u in range(2):
                        P_ = PT[u][jt % 2]
                        p.op('scalar', lambda e: e.activation(P_[:, 0:N], SB[u][jt % 2][:, 0:N], AF.Exp, scale=0.125),
                             reads=[SBN[u][jt % 2]], writes=[P_.name])
                    if jt + 2 < 8:
                        cinfo[jt + 2] = c_S(jt + 2)
                    for u in range(2):
                        P_ = PT[u][jt % 2]
                        if jt >= 1:
                            p.op('vector', lambda e: e.tensor_tensor(P_[:, 0:128], P_[:, 0:128], cmask[:, jt, 0, :], ALU.mult),
                                 reads=[P_.name, 'cmask'], writes=[P_.name])
                        if jt <= 6:
                            p.op('vector', lambda e: e.tensor_tensor(P_[:, N - 128:N], P_[:, N - 128:N], cmask[:, jt, 1, :], ALU.mult),
                                 reads=[P_.name, 'cmask'], writes=[P_.name])
                    for u in range(2):
                        P_ = PT[u][jt % 2]
                        for qh in range(2):
                            lo, hi = max(qlo, qh * 512), min(qhi, (qh + 1) * 512)
                            if lo >= hi:
                                continue
                            osl = slice(lo - qh * 512, hi - qh * 512)
                            psl = slice(lo - qlo, hi - qlo)
                            bk = 2 + 2 * qh + u
                            p.op('tensor', lambda e: e.matmul(ps[bk][:, osl], Vv[:, 4 + jt, 768 + u * 128:768 + (u + 1) * 128], P_[:, psl],
                                                              start=False, stop=False, skip_group_check=True),
                                 reads=id_V(4 + jt, 768 + u * 128, 768 + (u + 1) * 128) + [P_.name], writes=['ps%d' % bk], sig=True)
                evac(j)
                prev = j
            post(prev)

        def qkv_group(l, g):
            Wt, wid = next_block('qkv', l, g)
            pend = []
            for tt in range(8):
                b = 2 + (tt % 4)
                pb = ps[b]
                pid = 'ps%d' % b
                for k in range(8):
                    p.op('tensor', lambda e, k=k, tt=tt, pb=pb: e.matmul(pb[:], hT[:, k, tt * 128:(tt + 1) * 128], Wt[:, k * 512:(k + 1) * 512],
                                                                         start=(k == 0), stop=(k == 7)),
                         reads=['hT%d.%d' % (k, tt // 4), wid], writes=[pid], sig=(k == 7))
                def post(tt=tt, pb=pb, pid=pid):
                    tsl = slice(tt * 128, (tt + 1) * 128)
                    if g == 0:
                        rf_, rb_ = rope_block(pb[:], [pid], 8, tt)
                        transposes(rb_, 4, qT[:, 0:4, tsl], sum([id_qT(c, tt * 128, (tt + 1) * 128) for c in range(4)], []))
                    elif g == 1:
                        rf_, rb_ = rope_block(pb[:], [pid], 8, tt)
                        p.dma('sync', nk_a[l, tsl, :], rf_[:], 'o_' + rf_.name, reads=[rf_.name + 'a', rf_.name + 'b'], is_out=True)
                        transposes(rb_, 4, kT[:, 0:4, PAST + tt * 128:PAST + (tt + 1) * 128],
                                   sum([id_kT(c, PAST + tt * 128, PAST + (tt + 1) * 128) for c in range(4)], []))
                    elif g == 2:
                        v_ = vst[tt % 2]
                        KV = os.environ.get('KVAR', '')
                        if 'a' not in KV:
                            p.op('scalar', lambda e, v_=v_, pb=pb: e.activation(v_[:], pb[:], AF.Copy), reads=[pid], writes=[v_.name])
                        if 'b' not in KV:
                            p.dma('sync', nv_a[l, tsl, :], v_[:], 'o_' + v_.name, reads=[v_.name], is_out=True)
                        if 'c' not in KV:
                            p.op('vector', lambda e, v_=v_, tt=tt: e.tensor_copy(Vv[:, 4 + tt, 0:512], v_[:]), reads=[v_.name], writes=id_V(4 + tt, 0, 512))
                    elif g == 3:
                        xq = (xn, sq[0])[tt % 2]
                        qknorm(pb[:], [pid], 8, gqB[:, l * 64:(l + 1) * 64], 'gqB', xq, tt % 2)
                        rf_, rb_ = rope_block(xq[:], [xq.name], 8, tt)
                        transposes(rb_, 4, qT[:, 0:4, tsl], sum([id_qT(c, tt * 128, (tt + 1) * 128) for c in range(4)], []))
                    elif g == 4:
                        rf_, rb_ = rope_block(pb[:], [pid], 8, tt)
                        transposes(rb_, 4, qT[:, 0:4, tsl], sum([id_qT(c, tt * 128, (tt + 1) * 128) for c in range(4)], []))
                    else:
                        v_ = vst[tt % 2]
                        xq = (xn, sq[0])[tt % 2]
                        p.op('scalar', lambda e, pb=pb: e.activation(xq[:, 128:256], pb[:, 128:256], AF.Copy), reads=[pid], writes=[xq.name])
                        p.op('scalar', lambda e, v_=v_, pb=pb: e.activation(v_[:, 0:256], pb[:, 256:512], AF.Copy), reads=[pid], writes=[v_.name])
                        qknorm(pb[:, 0:128], [pid], 2, gkB[:, l * 64:(l + 1) * 64], 'gkB', xq, tt % 2)
                        rf_, rb_ = rope_block(xq[:, 0:256], [xq.name], 4, tt)
                        p.dma('sync', nk_b[l, tsl, :], rf_[:, 0:128], 'o_' + rf_.name, reads=[rf_.name + 'a', rf_.name + 'b'], is_out=True)
                        p.dma('sync', nk_c[l, tsl, :], rf_[:, 128:256], 'o_' + rf_.name, reads=[rf_.name + 'a', rf_.name + 'b'], is_out=True)
                        transposes(rb_, 2, kT[:, 4:6, PAST + tt * 128:PAST + (tt + 1) * 128],
                                   sum([id_kT(c, PAST + tt * 128, PAST + (tt + 1) * 128) for c in (4, 5)], []))
                        p.dma('sync', nv_b[l, tsl, :], v_[:, 0:128], 'o_' + v_.name, reads=[v_.name], is_out=True)
                        p.dma('sync', nv_c[l, tsl, :], v_[:, 128:256], 'o_' + v_.name, reads=[v_.name], is_out=True)
                        p.op('vector', lambda e, v_=v_, tt=tt: e.tensor_copy(
                            Vv[:, 4 + tt, 512:1024].rearrange("p (a b d) -> p a b d", a=4, b=2, d=64)[:, :, 0, :],
                            v_[:, 0:256].rearrange("p (a d) -> p a d", d=64)), reads=[v_.name],
                             writes=id_V(4 + tt, 512, 1024))
                pend.append(post)
                if len(pend) > 2:
                    pend.pop(0)()
            while pend:
                pend.pop(0)()

        for l in range(depth if not KSTOP else 1):
          try:
            p.epoch = l + 1
            norm_front()
            for jg in range(12):
                if jg == 6:
                    norm_pe()
                Wt, wid = next_block('mod', l, jg)
                for j in range(4):
                    col = jg * 4 + j
                    for k in range(8):
                        p.op('tensor', lambda e, Wt=Wt, j=j, k=k, col=col: e.matmul(
                            ps[6][:, col:col + 1], Wt[:, k * 512 + j * 128:k * 512 + (j + 1) * 128], scb[:, k:k + 1],
                            start=(k == 0), stop=(k == 7)),
                            reads=[wid, 'scb'], writes=['ps6'], sig=(k == 7))
            p.op('vector', lambda e, l=l: e.tensor_tensor(modT[:], ps[6][:, 0:48], bmodT[:, l * 48:(l + 1) * 48], ALU.add),
                 reads=['ps6', 'bmodT'], writes=['modT'])
            p.op('vector', lambda e, l=l: e.scalar_tensor_tensor(a1[:], modT[:, 8:16], 1.0, g1T[:, l * 8:(l + 1) * 8], ALU.add, ALU.mult),
                 reads=['modT', 'g1T'], writes=['a1'])
            p.op('vector', lambda e, l=l: e.scalar_tensor_tensor(a2[:], modT[:, 32:40], 1.0, g2T[:, l * 8:(l + 1) * 8], ALU.add, ALU.mult),
                 reads=['modT', 'g2T'], writes=['a2'])
            sh1, gg1, sh2, gg2 = modT[:, 0:8], modT[:, 16:24], modT[:, 24:32], modT[:, 40:48]
            _chk('mod')

            norm_apply(a1, sh1, ['a1', 'modT'], h_out)
            _chk('norm1')
            load_cache(l)
            _chk('cache')
            qkv_group(l, 0)
            _chk('g0')
            qkv_group(l, 1)
            _chk('g1')
            qkv_group(l, 2)
            _chk('g2')
            attn_A(l)
            _chk('aA')
            qkv_group(l, 3)
            _chk('g3')
            qkv_group(l, 5)
            _chk('g5')
            attn_B(l)
            _chk('aB')
            qkv_group(l, 4)
            attn_C(l)
            _chk('aC')

            for oc in range(8):
                Wt, wid = next_block('mrg', l, oc)
                for th in range(2):
                    tsl = slice(th * 512, (th + 1) * 512)
                    for br in range(3):
                        yb_, gb_ = ps[2 * br], ps[2 * br + 1]
                        for k in range(4):
                            kk = br * 4 + k
                            p.op('tensor', lambda e, yb_=yb_, kk=kk, k=k, br=br: e.matmul(
                                yb_[:], Wt[:, kk * 128:(kk + 1) * 128], OT[:, br * 4 + k, tsl], start=(k == 0), stop=(k == 3)),
                                reads=[wid] + id_OT(br * 4 + k, th * 512, (th + 1) * 512), writes=['ps%d' % (2 * br)], sig=(k == 3))
                        for k in range(8):
                            kk = 12 + br * 8 + k
                            p.op('tensor', lambda e, gb_=gb_, kk=kk, k=k: e.matmul(
                                gb_[:], Wt[:, kk * 128:(kk + 1) * 128], hT[:, k, tsl], start=(k == 0), stop=(k == 7)),
                                reads=[wid, 'hT%d.%d' % (k, th)], writes=['ps%d' % (2 * br + 1)], sig=(k == 7))
                        p.op('scalar', lambda e, gb_=gb_: e.activation(sg[:], gb_[:], AF.Sigmoid), reads=['ps%d' % (2 * br + 1)], writes=['sg'])
                        if br == 0:
                            p.op('vector', lambda e, yb_=yb_: e.tensor_tensor(acc[:], yb_[:], sg[:], ALU.mult),
                                 reads=['ps%d' % (2 * br), 'sg'], writes=['ao1'])
                        else:
                            p.op('vector', lambda e, yb_=yb_: e.tensor_tensor(tm2[:], yb_[:], sg[:], ALU.mult),
                                 reads=['ps%d' % (2 * br), 'sg'], writes=['tm2'])
                            if br == 1:
                                p.op('vector', lambda e: e.tensor_tensor(acc[:], acc[:], tm2[:], ALU.add), reads=['ao1', 'tm2'], writes=['ao1'])
                            else:
                                p.op('vector', lambda e, oc=oc, tsl=tsl: e.tensor_tensor(mT[:, oc, tsl], acc[:], tm2[:], ALU.add),
                                     reads=['ao1', 'tm2'], writes=id_mT(oc, th * 512, (th + 1) * 512))
            for og in range(2):
                Wt, wid = next_block('out', l, og)
                for ocl in range(4):
                    oc = og * 4 + ocl
                    for th in range(2):
                        tsl = slice(th * 512, (th + 1) * 512)
                        pb = ps[(ocl * 2 + th) % 4]
                        pid = 'ps%d' % ((ocl * 2 + th) % 4)
                        for k in range(8):
                            p.op('tensor', lambda e, pb=pb, k=k, ocl=ocl, tsl=tsl: e.matmul(
                                pb[:], Wt[:, (ocl * 8 + k) * 128:(ocl * 8 + k + 1) * 128], mT[:, k, tsl], start=(k == 0), stop=(k == 7)),
                                reads=[wid] + id_mT(k, th * 512, (th + 1) * 512), writes=[pid], sig=(k == 7))
                        p.op('vector', lambda e, pb=pb, oc=oc, tsl=tsl: e.scalar_tensor_tensor(
                            xT[:, oc, tsl], pb[:], gg1[:, oc:oc + 1], xT[:, oc, tsl], ALU.mult, ALU.add),
                            reads=[pid, 'modT', 'xT%d.%d' % (oc, th)], writes=['xT%d.%d' % (oc, th)])
            norm_phase(a2, sh2, ['a2', 'modT'], h_out)
            for jb in range(11):
                Wt, wid = next_block('ffi', l, jb)
                for jj in range(2):
                    j = jb * 2 + jj
                    for th in range(2):
                        tsl = slice(th * 512, (th + 1) * 512)
                        n_ = (jj * 2 + th) % 2
                        pa, pbb = ps[2 * n_], ps[2 * n_ + 1]
                        for k in range(8):
                            p.op('tensor', lambda e, pa=pa, k=k, jj=jj, tsl=tsl: e.matmul(
                                pa[:], Wt[:, ((jj * 2) * 8 + k) * 128:((jj * 2) * 8 + k + 1) * 128], hT[:, k, tsl], start=(k == 0), stop=(k == 7)),
                                reads=[wid, 'hT%d.%d' % (k, th)], writes=['ps%d' % (2 * n_)], sig=(k == 7))
                        for k in range(8):
                            p.op('tensor', lambda e, pbb=pbb, k=k, jj=jj, tsl=tsl: e.matmul(
                                pbb[:], Wt[:, ((jj * 2 + 1) * 8 + k) * 128:((jj * 2 + 1) * 8 + k + 1) * 128], hT[:, k, tsl], start=(k == 0), stop=(k == 7)),
                                reads=[wid, 'hT%d.%d' % (k, th)], writes=['ps%d' % (2 * n_ + 1)], sig=(k == 7))
                        p.op('scalar', lambda e, pa=pa: e.activation(sg[:], pa[:], AF.Silu), reads=['ps%d' % (2 * n_)], writes=['sg'])
                        p.op('vector', lambda e, pbb=pbb, j=j, tsl=tsl: e.tensor_tensor(uT[:, j, tsl], pbb[:], sg[:], ALU.mult),
                             reads=['ps%d' % (2 * n_ + 1), 'sg'], writes=id_uT(j, th * 512, (th + 1) * 512))
            for oc in range(8):
                Wt, wid = next_block('ffo', l, oc)
                for th in range(2):
                    tsl = slice(th * 512, (th + 1) * 512)
                    pb = ps[4 + th]
                    pid = 'ps%d' % (4 + th)
                    for j in range(NFF):
                        p.op('tensor', lambda e, pb=pb, j=j, tsl=tsl: e.matmul(
                            pb[:], Wt[:, j * 128:(j + 1) * 128], uT[:, j, tsl], start=(j == 0), stop=(j == NFF - 1)),
                            reads=[wid] + id_uT(j, th * 512, (th + 1) * 512), writes=[pid], sig=(j == NFF - 1))
                    p.op('vector', lambda e, pb=pb, oc=oc, tsl=tsl: e.scalar_tensor_tensor(
                        xT[:, oc, tsl], pb[:], gg2[:, oc:oc + 1], xT[:, oc, tsl], ALU.mult, ALU.add),
                        reads=[pid, 'modT', 'xT%d.%d' % (oc, th)], writes=['xT%d.%d' % (oc, th)])

          except _Stop:
            pass
        norm_phase(gfinT, zero8, ['gfinT', 'zero8'], y_out)
        p.emit()
    return nc


def _feat_major(v):
    return np.ascontiguousarray(v.reshape(8, 128).T)


def make_inputs(inp, depth=DEPTH):
    f32 = np.float32
    Wp = pack_weights(inp, depth)
    rows = np.repeat(np.arange(16), 64).astype(f32)
    cols = np.tile(np.arange(64), 16).astype(f32)
    inv = (10000.0 ** (-np.arange(16, dtype=f32) / 16)).astype(f32)
    ang = np.concatenate([rows[:, None] * inv, cols[:, None] * inv], axis=-1).astype(f32)
    cos_s, sin_s = np.cos(ang).astype(f32), np.sin(ang).astype(f32)
    cos_p, sin_p = np.ones((T, 32), f32), np.zeros((T, 32), f32)
    mb_p = np.full((12, 4), NEGBIG, f32)
    for t in range(8):
        mb_p[4 + t, t // 2] = 0.0
    mb_s = np.zeros((12, 4), f32)
    m01_s = np.ones((8, 4), f32)
    m01_p = np.zeros((8, 4), f32)
    for t in range(8):
        m01_p[t, t // 2] = 1.0
    kl = np.arange(128)[:, None]
    ql = np.arange(128)[None, :]
    cm_s = np.zeros((128, 8, 2, 128), f32)
    cm_p = np.zeros((128, 8, 2, 128), f32)
    for jt in range(8):
        cm_s[:, jt, 0, :] = (kl <= ql)
        cm_s[:, jt, 1, :] = (kl >= ql)
        cm_p[:, jt, 0, :] = 1.0 if (jt % 2 == 1) else 0.0
        cm_p[:, jt, 1, :] = 1.0 if (jt % 2 == 0) else 0.0
    rep = lambda v: np.ascontiguousarray(np.broadcast_to(v.reshape(1, -1), (128, v.size))).astype(f32)
    common = {
        'W': Wp,
        'ident': np.eye(128, dtype=f32),
        'bmodT': np.concatenate([np.ascontiguousarray(inp['b_mod'][l].reshape(48, 128).T) for l in range(depth)], axis=1),
        'g1T': np.concatenate([_feat_major(inp['norm1_g'][l]) for l in range(depth)], axis=1),
        'g2T': np.concatenate([_feat_major(inp['norm2_g'][l]) for l in range(depth)], axis=1),
        'gfinT': _feat_major(inp['final_g']),
        'sublnT': np.ascontiguousarray(inp['a_subln_g'][:depth].T),
        'gqB': rep(inp['b_qnorm_g'][:depth]),
        'gkB': rep(inp['b_knorm_g'][:depth]),
        'lamB': rep(np.concatenate([inp['a_lam_q1'][:depth].ravel(), inp['a_lam_k1'][:depth].ravel(),
                                    inp['a_lam_q2'][:depth].ravel(), inp['a_lam_k2'][:depth].ravel()])),
        'sinkB': rep(inp['c_sink'][:depth]),
    }
    maps = []
    for core in range(8):
        m = dict(common)
        if core < 4:
            xt = inp['x_prompt'][core * 4:(core + 1) * 4].reshape(T, D_MODEL)
            m['cvec'] = _feat_major(inp['c_ctx'])
            m['cKT'] = np.zeros((depth, 768, PAST), f32)
            m['cV'] = np.zeros((depth, PAST, VW), f32)
            m['m01'] = rep(m01_p)
            m['cos_t'], m['sin_t'] = cos_p, sin_p
            m['maskb'] = rep(mb_p)
            m['cmask'] = cm_p.reshape(128, -1)
        else:
            b = core - 4
            xt = inp['x_sample'][b]
            m['cvec'] = _feat_major(inp['c'][b])
            ck = np.concatenate([inp['cache_a_k'][b, :depth].reshape(depth, PAST, 512),
                                 inp['cache_b_k'][b, :depth].reshape(depth, PAST, 128),
                                 inp['cache_c_k'][b, :depth].reshape(depth, PAST, 128)], axis=-1)
            m['cKT'] = np.ascontiguousarray(ck.transpose(0, 2, 1))
            cv = np.ones((depth, PAST, VW), f32)
            cv[:, :, 0:512] = inp['cache_a_v'][b, :depth].reshape(depth, PAST, 512)
            for kv in range(2):
                cv[:, :, 512 + kv * 128:512 + kv * 128 + 64] = inp['cache_b_v'][b, :depth, :, kv, :]
                cv[:, :, 768 + kv * 128:768 + kv * 128 + 64] = inp['cache_c_v'][b, :depth, :, kv, :]
            m['cV'] = cv
            m['m01'] = rep(m01_s)
            m['cos_t'], m['sin_t'] = cos_s, sin_s
            m['maskb'] = rep(mb_s)
            m['cmask'] = cm_s.reshape(128, -1)
        m['xT_in'] = np.ascontiguousarray(xt.T)
        maps.append(m)
    return maps


_NC_CACHE = {}


def run(inp, depth=DEPTH):
    inp = {k: np.asarray(v) for k, v in inp.items()}
    if depth not in _NC_CACHE:
        _NC_CACHE[depth] = build_program(depth)
    nc = _NC_CACHE[depth]
    maps = make_inputs(inp, depth)
    res = run_bass_kernel_spmd(nc, maps, core_ids=list(range(8)))
    r = res.results
    f32 = np.float32
    y_prompt = np.concatenate([r[c]['yT'].T.reshape(4, 256, D_MODEL) for c in range(4)], axis=0).astype(f32)
    y_sample = np.stack([r[4 + b]['yT'].T for b in range(4)], axis=0).astype(f32)

    def gather(name, hd, dd):
        outs = []
        for c in range(4):
            a = r[c][name].reshape(depth, 4, 256, hd, dd).transpose(1, 0, 2, 3, 4)
            outs.append(a)
        return np.ascontiguousarray(np.concatenate(outs, axis=0)).astype(f32)
    return (y_prompt, y_sample,
            gather('nk_a', 4, 128), gather('nv_a', 4, 128),
            gather('nk_b', 2, 64), gather('nv_b', 2, 64),
            gather('nk_c', 2, 64), gather('nv_c', 2, 64))


def kernel(**inputs):
    return run(inputs, DEPTH)
```

```python
import math
import os
import contextlib
import numpy as np
import concourse.bass as bass
import concourse.mybir as mybir
from concourse.bass_utils import run_bass_kernel_spmd

F32 = mybir.dt.float32
BF16 = mybir.dt.bfloat16
ALU = mybir.AluOpType
AF = mybir.ActivationFunctionType
AX = mybir.AxisListType

D_MODEL = 1024
DEPTH = 4
T = 1024
PAST = 512
D_FF = 2816
NFF = 22
EPS = 1e-6
NEGBIG = -30000.0
WSLOT = 4608
NSLOT = 3

O_QA, O_KA, O_VA, O_QB, O_KB, O_VB, O_QC, O_KC, O_VC, O_G = 0, 512, 1024, 1536, 2048, 2176, 2304, 2816, 2944, 3072


def _pair_perm():
    idx = []
    for j in range(4):
        idx += list(range(j * 64, j * 64 + 64)) + list(range((j + 4) * 64, (j + 4) * 64 + 64))
    return np.array(idx)


def qkv_group_cols():
    pp = _pair_perm()
    g = []
    g.append(np.arange(O_QA, O_QA + 512))
    g.append(np.arange(O_KA, O_KA + 512))
    g.append(np.arange(O_VA, O_VA + 512))
    g.append(O_QB + pp)
    g.append(O_QC + pp)
    g.append(np.concatenate([np.arange(O_KB, O_KB + 128), np.arange(O_KC, O_KC + 128),
                             np.arange(O_VB, O_VB + 128), np.arange(O_VC, O_VC + 128)]))
    return g


def weight_plan(depth):
    plan = []
    for l in range(depth):
        for jg in range(12):
            plan.append(('mod', l, jg, 4096))
        for g in (0, 1, 2, 3, 5, 4):
            plan.append(('qkv', l, g, 4096))
        for oc in range(8):
            plan.append(('mrg', l, oc, 4608))
        for og in range(2):
            plan.append(('out', l, og, 4096))
        for jb in range(11):
            plan.append(('ffi', l, jb, 4096))
        for oc in range(8):
            plan.append(('ffo', l, oc, 2816))
    return plan


def _kc(w, cols):
    K = w.shape[0] // 128
    return w[:, cols].reshape(K, 128, len(cols)).transpose(1, 0, 2)


def pack_weights(inp, depth):
    plan = weight_plan(depth)
    tot = sum(b[3] for b in plan)
    W = np.empty((128, tot), np.float32)
    gcols = qkv_group_cols()
    pp = _pair_perm()
    off = 0
    for (kind, l, i, E) in plan:
        if kind == 'mod':
            blk = _kc(inp['w_mod'][l], np.arange(i * 512, i * 512 + 512))
        elif kind == 'qkv':
            blk = _kc(inp['w_in'][l], gcols[i])
        elif kind == 'mrg':
            cs = np.arange(i * 128, i * 128 + 128)
            parts = [_kc(inp['w_br_a'][l], cs),
                     _kc(inp['w_br_b'][l][pp], cs),
                     _kc(inp['w_br_c'][l][pp], cs),
                     _kc(inp['w_in'][l], O_G + cs),
                     _kc(inp['w_in'][l], O_G + 1024 + cs),
                     _kc(inp['w_in'][l], O_G + 2048 + cs)]
            blk = np.concatenate(parts, axis=1)
        elif kind == 'out':
            parts = [_kc(inp['w_out'][l], np.arange((i * 4 + o) * 128, (i * 4 + o) * 128 + 128)) for o in range(4)]
            blk = np.concatenate(parts, axis=1)
        elif kind == 'ffi':
            parts = []
            for jj in range(2):
                j = i * 2 + jj
                parts.append(_kc(inp['w_ffn_in'][l], np.arange(j * 128, j * 128 + 128)))
                parts.append(_kc(inp['w_ffn_in'][l], D_FF + np.arange(j * 128, j * 128 + 128)))
            blk = np.concatenate(parts, axis=1)
        elif kind == 'ffo':
            blk = _kc(inp['w_ffn_out'][l], np.arange(i * 128, i * 128 + 128))
        W[:, off:off + E] = blk.reshape(128, E)
        off += E
    return W


KSTOP = os.environ.get('KSTOP', '')


class _Stop(Exception):
    pass


def _chk(name):
    if KSTOP == name:
        raise _Stop()


class _Rec:
    def __init__(self):
        self.call = None

    def __getattr__(self, name):
        def f(*a, **k):
            self.call = (name, a, k)
            return self
        return f


class Prog:
    ENGS = ['tensor', 'vector', 'scalar', 'gpsimd', 'sync']

    def __init__(self, nc, stack):
        self.nc = nc
        self.stack = stack
        self.q = {e: [] for e in self.ENGS}
        self.epoch = 0
        self.esem = {}
        self.ecnt = {}
        self.dsem = {}
        self.buf = {}
        self.waited = {}
        self.pend = {e: ([], []) for e in self.ENGS}
        self.out_evs = {}

    def _deps(self, eng, reads, writes):
        best = {}

        def add(ev):
            sk, v, prod = ev
            if prod == 'tensor' and eng == 'tensor':
                return
            if best.get(sk, -1) < v:
                best[sk] = v
        for b in reads:
            st = self.buf.get(b)
            if st and st[0] is not None:
                add(st[0])
        for b in writes:
            st = self.buf.get(b)
            if st:
                if st[0] is not None:
                    add(st[0])
                for ev in st[1]:
                    add(ev)
        out = []
        for sk, v in best.items():
            key = (eng, sk)
            if self.waited.get(key, -1) >= v:
                continue
            self.waited[key] = v
            out.append((sk, v))
        return out

    def _commit(self, ev, reads, writes):
        for b in writes:
            self.buf[b] = [ev, []]
        for b in reads:
            st = self.buf.get(b)
            if st is None:
                st = self.buf[b] = [None, []]
            st[1].append(ev)

    def op(self, eng, fn, reads=(), writes=(), sig=True):
        reads = list(reads)
        writes = list(writes)
        waits = self._deps(eng, reads, writes)
        rec = _Rec()
        fn(rec)
        fn = rec.call
        assert fn is not None
        if not sig:
            self.q[eng].append((fn, waits, None))
            self.pend[eng][0].extend(reads)
            self.pend[eng][1].extend(writes)
            return None
        ek = 'E%s@%d' % (eng, self.epoch)
        if ek not in self.esem:
            self.esem[ek] = self.stack.enter_context(self.nc.semaphore("es_%s_%d" % (eng, self.epoch)))
            self.ecnt[ek] = 0
        self.ecnt[ek] += 1
        ev = (ek, self.ecnt[ek], eng)
        self.q[eng].append((fn, waits, ('E', ek)))
        pr, pw = self.pend[eng]
        self._commit(ev, reads + pr, writes + pw)
        self.pend[eng] = ([], [])
        return ev

    def dma(self, eng, out, in_, sem, reads=(), writes=(), is_out=False):
        waits = self._deps(eng, reads, writes)
        sem = '%s@%d' % (sem, self.epoch)
        if sem not in self.dsem:
            self.dsem[sem] = [self.stack.enter_context(self.nc.semaphore("ds_" + sem.replace('@', '_'))), 0]
        s = self.dsem[sem]
        s[1] += 16
        ev = ('D' + sem, s[1], 'dma')
        self.q[eng].append((('dma_start', (), {'out': out, 'in_': in_}), waits, ('D', sem)))
        self._commit(ev, list(reads), list(writes))
        if is_out:
            self.out_evs[sem] = ev
        return ev

    def _semh(self, sk):
        if sk[0] == 'E':
            return self.esem[sk]
        return self.dsem[sk[1:]][0]

    def emit(self):
        nc = self.nc
        self.q['sync'].append((None, [(ev[0], ev[1]) for ev in self.out_evs.values()], None))
        with nc.Block() as block:
            for ename in self.ENGS:
                ops = self.q[ename]

                def body(eng, ops=ops):
                    for (fn, waits, inc) in ops:
                        for (sk, v) in waits:
                            eng.wait_ge(self._semh(sk), v)
                        if fn is None:
                            continue
                        ins = getattr(eng, fn[0])(*fn[1], **fn[2])
                        if inc is None:
                            continue
                        if inc[0] == 'E':
                            ins.then_inc(self.esem[inc[1]], 1)
                        else:
                            ins.then_inc(self.dsem[inc[1]][0], 16)
                getattr(block, ename)(body)


RB = 512
R_QT, R_KT, R_V, R_OT = 0, 4096, 13312, 25600
R_TOT = 37888
VW = 1024


def rid(lo, hi):
    return ['R%d' % i for i in range(lo // RB, (hi - 1) // RB + 1)]


def build_program(depth=DEPTH):
    nc = bass.Bass("TRN2", target_bir_lowering=False)
    plan = weight_plan(depth)
    wtot = sum(b[3] for b in plan)

    def din(name, shape):
        return nc.dram_tensor(name, list(shape), F32, kind="ExternalInput").ap()

    def dout(name, shape):
        return nc.dram_tensor(name, list(shape), F32, kind="ExternalOutput").ap()

    W = din("W", [128, wtot])
    xT_in = din("xT_in", [1024, T])
    cvec_d = din("cvec", [128, 8])
    cKT = din("cKT", [depth, 768, PAST])
    cV = din("cV", [depth, PAST, VW])
    m01_d = din("m01", [128, 32])
    cos_d = din("cos_t", [T, 32])
    sin_d = din("sin_t", [T, 32])
    maskb_d = din("maskb", [128, 48])
    cmask_d = din("cmask", [128, 8 * 2 * 128])
    ident_d = din("ident", [128, 128])
    bmod_d = din("bmodT", [128, depth * 48])
    g1_d = din("g1T", [128, depth * 8])
    g2_d = din("g2T", [128, depth * 8])
    gfin_d = din("gfinT", [128, 8])
    subln_d = din("sublnT", [128, depth])
    gq_d = din("gqB", [128, depth * 64])
    gk_d = din("gkB", [128, depth * 64])
    lam_d = din("lamB", [128, 4 * depth * 64])
    sink_d = din("sinkB", [128, depth * 8])

    yT_out = dout("yT", [1024, T])
    nk_a = dout("nk_a", [depth, T, 512])
    nv_a = dout("nv_a", [depth, T, 512])
    nk_b = dout("nk_b", [depth, T, 128])
    nv_b = dout("nv_b", [depth, T, 128])
    nk_c = dout("nk_c", [depth, T, 128])
    nv_c = dout("nv_c", [depth, T, 128])

    with contextlib.ExitStack() as st:
        def sb(name, shape, dt):
            return st.enter_context(nc.sbuf_tensor(name, list(shape), dt))

        def psum(name, shape, dt):
            return st.enter_context(nc.psum_tensor(name, list(shape), dt))

        xT = sb("xT", [128, 8, T], F32)
        hT = sb("hT", [128, 8, T], BF16)
        R = sb("R", [128, R_TOT], BF16)
        wsl = [sb("wsl%d" % i, [128, WSLOT], BF16) for i in range(NSLOT)]
        cos_s = sb("cos_s", [128, 8, 32], F32)
        sin_s = sb("sin_s", [128, 8, 32], F32)
        maskb = sb("maskb_s", [128, 48], F32)
        cmask = sb("cmask_s", [128, 8, 2, 128], BF16)
        ident = sb("ident_s", [128, 128], BF16)
        ones_f = sb("ones_f", [128, 128], F32)
        ones_b = sb("ones_b", [128, 128], BF16)
        bmodT = sb("bmodT_s", [128, depth * 48], F32)
        g1T = sb("g1T_s", [128, depth * 8], F32)
        g2T = sb("g2T_s", [128, depth * 8], F32)
        gfinT = sb("gfinT_s", [128, 8], F32)
        sublnT = sb("sublnT_s", [128, depth], F32)
        gqB = sb("gqB_s", [128, depth * 64], F32)
        gkB = sb("gkB_s", [128, depth * 64], F32)
        lamS = sb("lamS", [128, 2 * depth], F32)
        neglam = sb("neglam", [128, depth], F32)
        esink = sb("esink", [128, depth * 8], F32)
        cvec = sb("cvec_s", [128, 8], F32)
        scb = sb("scb", [128, 8], BF16)
        modT = sb("modT", [128, 48], F32)
        a1 = sb("a1", [128, 8], F32)
        a2 = sb("a2", [128, 8], F32)
        zero8 = sb("zero8", [128, 8], F32)
        rstd = sb("rstd", [128, 512], F32)
        lnv = sb("lnv", [128, 512], F32)
        sq = [sb("sq%d" % i, [128, 512], F32) for i in range(2)]
        tmpf = [sb("tmpf%d" % i, [128, 512], F32) for i in range(2)]
        rf = [sb("rf%d" % i, [128, 512], F32) for i in range(2)]
        rbt = [sb("rb%d" % i, [128, 512], BF16) for i in range(2)]
        vst = [sb("vst%d" % i, [128, 512], F32) for i in range(2)]
        xn = sb("xn", [128, 512], F32)
        rt = [sb("rt%d" % i, [128, 256], F32) for i in range(4)]
        ss8 = sb("ss8", [128, 16], F32)
        l8 = sb("l8", [128, 16], F32)
        r8 = sb("r8", [128, 16], F32)
        Pt = [sb("Pt%d" % i, [128, 512], BF16) for i in range(4)]
        m01 = sb("m01_s", [128, 32], F32)
        ar = sb("ar", [128, 512], F32)
        ao1 = sb("ao1", [128, 512], F32)
        ao2 = sb("ao2", [128, 512], F32)
        sg = sb("sg", [128, 512], F32)
        tm2 = sb("tm2", [128, 512], F32)

        acc = ao1
        yst = vst
        ps = [psum("ps%d" % i, [128, 512], F32) for i in range(8)]
        psB = ps[7][:].bitcast(BF16)

        p = Prog(nc, st)

        qT = R[:, R_QT:R_QT + 4096].rearrange("p (c t) -> p c t", t=1024)
        kT = R[:, R_KT:R_KT + 9216].rearrange("p (c t) -> p c t", t=1536)
        Vv = R[:, R_V:R_V + 12 * VW].rearrange("p (k f) -> p k f", f=VW)
        OT = R[:, R_OT:R_OT + 12288].rearrange("p (c t) -> p c t", t=1024)
        mT = R[:, 0:8192].rearrange("p (c t) -> p c t", t=1024)
        uT = R[:, 0:22528].rearrange("p (c t) -> p c t", t=1024)

        def id_qT(c, lo=0, hi=1024):
            return rid(R_QT + c * 1024 + lo, R_QT + c * 1024 + hi)

        def id_kT(c, lo, hi):
            return rid(R_KT + c * 1536 + lo, R_KT + c * 1536 + hi)

        def id_V(kt, lo, hi):
            return rid(R_V + kt * VW + lo, R_V + kt * VW + hi)

        def id_OT(c, lo, hi):
            return rid(R_OT + c * 1024 + lo, R_OT + c * 1024 + hi)

        def id_mT(c, lo, hi):
            return rid(c * 1024 + lo, c * 1024 + hi)

        id_uT = id_mT

        for c in range(8):
            p.dma('sync', xT[:, c, :], xT_in[c * 128:(c + 1) * 128, :], 'ldx%d' % c, writes=['xT%d.0' % c, 'xT%d.1' % c])
        small = [(cvec, cvec_d, 'cvec'), (maskb, maskb_d, 'maskb'), (bmodT, bmod_d, 'bmodT'), (g1T, g1_d, 'g1T'),
                 (g2T, g2_d, 'g2T'), (gfinT, gfin_d, 'gfinT'), (sublnT, subln_d, 'sublnT'), (gqB, gq_d, 'gqB'),
                 (gkB, gk_d, 'gkB'), (esink, sink_d, 'esink')]
        for (t_, d_, n_) in small:
            p.dma('sync', t_[:], d_[:], 'lds_' + n_, writes=[n_])
        p.dma('sync', cos_s[:], cos_d.rearrange("(t p) d -> p t d", p=128), 'lds_cos', writes=['cos'])
        p.dma('sync', sin_s[:], sin_d.rearrange("(t p) d -> p t d", p=128), 'lds_sin', writes=['sin'])
        p.dma('gpsimd', ident[:], ident_d[:], 'ldc_i', writes=['ident'])
        p.dma('gpsimd', cmask[:].rearrange("p a b c -> p (a b c)"), cmask_d[:], 'ldc_m', writes=['cmask'])
        p.op('vector', lambda e: e.memset(ones_f[:], 1.0), writes=['ones_f'])
        p.op('vector', lambda e: e.memset(ones_b[:], 1.0), writes=['ones_b'])
        p.op('vector', lambda e: e.memset(zero8[:], 0.0), writes=['zero8'])
        p.op('scalar', lambda e: e.activation(scb[:], cvec[:], AF.Silu), reads=['cvec'], writes=['scb'])
        p.op('scalar', lambda e: e.activation(esink[:], esink[:], AF.Exp), reads=['esink'], writes=['esink'])
        p.op('vector', lambda e: e.memset(esink[0:64, :], 0.0), reads=['esink'], writes=['esink'])
        n64 = depth * 64
        p.dma('sync', tmpf[0][:, 0:2 * n64], lam_d[:, 0:2 * n64], 'lds_lam1', writes=['tmpf0'])
        p.dma('sync', tmpf[1][:, 0:2 * n64], lam_d[:, 2 * n64:4 * n64], 'lds_lam2', writes=['tmpf1'])
        p.dma('sync', m01[:], m01_d[:], 'lds_m01', writes=['m01'])
        p.op('vector', lambda e: e.tensor_tensor(sq[0][:, 0:n64], tmpf[0][:, 0:n64], tmpf[0][:, n64:2 * n64], ALU.mult),
             reads=['tmpf0'], writes=['sq0'])
        p.op('vector', lambda e: e.tensor_tensor(sq[0][:, n64:2 * n64], tmpf[1][:, 0:n64], tmpf[1][:, n64:2 * n64], ALU.mult),
             reads=['tmpf1', 'sq0'], writes=['sq0'])
        p.op('vector', lambda e: e.tensor_reduce(lamS[:], sq[0][:, 0:2 * n64].rearrange("p (a d) -> p a d", d=64), AX.X, ALU.add),
             reads=['sq0'], writes=['lamS'])
        p.op('scalar', lambda e: e.activation(lamS[:], lamS[:], AF.Exp), reads=['lamS'], writes=['lamS'])
        p.op('vector', lambda e: e.tensor_tensor(neglam[:], lamS[:, depth:2 * depth], lamS[:, 0:depth], ALU.subtract),
             reads=['lamS'], writes=['neglam'])
        for l in range(depth):
            li = 0.8 - 0.6 * math.exp(-0.3 * l)
            p.op('vector', lambda e, l=l, li=li: e.tensor_scalar(neglam[:, l:l + 1], neglam[:, l:l + 1], -li, None, ALU.add),
                 reads=['neglam'], writes=['neglam'])

        wstate = {'next': 0, 'off': 0}
        wslot_of = {}

        def issue_loads(upto):
            while wstate['next'] < min(upto, len(plan)):
                i = wstate['next']
                E = plan[i][3]
                s = i % NSLOT
                p.dma('gpsimd', wsl[s][:, 0:E], W[:, wstate['off']:wstate['off'] + E], 'w%d' % s, writes=['W%d' % s])
                wslot_of[i] = s
                wstate['off'] += E
                wstate['next'] += 1

        bidx = {'i': 0}

        def next_block(kind, l, i):
            b = bidx['i']
            assert plan[b][:3] == (kind, l, i), (plan[b], kind, l, i)
            issue_loads(b + NSLOT)
            bidx['i'] += 1
            s = wslot_of[b]
            return wsl[s], 'W%d' % s

        rs = [rstd, lnv]
        accs = [ar, ao2]
        banks = [(ps[0], 'ps0'), (ps[1], 'ps1')]

        def norm_phase(a_ap, sh_ap, a_ids, out_fn):
            norm_front()
            norm_pe()
            norm_apply(a_ap, sh_ap, a_ids, out_fn)

        def norm_front():
            for th in range(2):
                tsl = slice(th * 512, (th + 1) * 512)
                acc_ = accs[th]
                for c in range(8):
                    s_ = sq[c % 2]
                    p.op('scalar', lambda e: e.activation(s_[:], xT[:, c, tsl], AF.Square),
                         reads=['xT%d.%d' % (c, th)], writes=[s_.name])
                    if c == 1:
                        p.op('vector', lambda e: e.tensor_tensor(acc_[:], sq[0][:], sq[1][:], ALU.add),
                             reads=['sq0', 'sq1'], writes=[acc_.name])
                    elif c >= 2:
                        p.op('vector', lambda e: e.tensor_tensor(acc_[:], acc_[:], s_[:], ALU.add),
                             reads=[acc_.name, s_.name], writes=[acc_.name])

        def norm_pe():
            for th in range(2):
                bk, bid = banks[th]
                acc_ = accs[th]
                p.op('tensor', lambda e: e.matmul(bk[:], ones_f[:], acc_[:], start=True, stop=True),
                     reads=[acc_.name, 'ones_f'], writes=[bid])
            for th in range(2):
                bk, bid = banks[th]
                r_ = rs[th]
                p.op('scalar', lambda e: e.activation(r_[:], bk[:], AF.Ln, bias=EPS, scale=1.0 / 1024),
                     reads=[bid], writes=[r_.name])
                p.op('scalar', lambda e: e.activation(r_[:], r_[:], AF.Exp, scale=-0.5), reads=[r_.name], writes=[r_.name])

        def norm_apply(a_ap, sh_ap, a_ids, out_fn):
            for th in range(2):
                tsl = slice(th * 512, (th + 1) * 512)
                r_ = rs[th]
                for c in range(8):
                    t_ = tmpf[c % 2]
                    p.op('vector', lambda e: e.tensor_tensor(t_[:], xT[:, c, tsl], r_[:], ALU.mult),
                         reads=['xT%d.%d' % (c, th), r_.name], writes=[t_.name])
                    out_fn(c, th, tsl, t_, a_ap, sh_ap, a_ids)

        def h_out(c, th, tsl, t_, a_ap, sh_ap, a_ids):
            p.op('scalar', lambda e: e.activation(hT[:, c, tsl], t_[:], AF.Identity, bias=sh_ap[:, c:c + 1], scale=a_ap[:, c:c + 1]),
                 reads=[t_.name] + a_ids, writes=['hT%d.%d' % (c, th)])

        def y_out(c, th, tsl, t_, a_ap, sh_ap, a_ids):
            y_ = yst[c % 2]
            p.op('scalar', lambda e: e.activation(y_[:], t_[:], AF.Identity, bias=sh_ap[:, c:c + 1], scale=a_ap[:, c:c + 1]),
                 reads=[t_.name] + a_ids, writes=[y_.name])
            p.dma('sync', yT_out[c * 128:(c + 1) * 128, tsl], y_[:], 'o_' + y_.name, reads=[y_.name], is_out=True)

        rope_n = {'i': 0}

        def rope_block(src, src_ids, U, tt, f32=True):
            i = rope_n['i'] % 2
            rope_n['i'] += 1
            rf_, rb_ = rf[i], rbt[i]
            n = U * 64
            xs = src.rearrange("p (u two d) -> p u two d", two=2, d=32)
            x1, x2 = xs[:, :, 0, :], xs[:, :, 1, :]
            ds_ = rf_[:, 0:n].rearrange("p (u two d) -> p u two d", two=2, d=32)
            d1, d2 = ds_[:, :, 0, :], ds_[:, :, 1, :]
            cB = cos_s[:, tt, :].unsqueeze(1).broadcast_to([128, U, 32])
            sB = sin_s[:, tt, :].unsqueeze(1).broadcast_to([128, U, 32])
            tv = [rt[k][:, 0:U * 32].rearrange("p (u d) -> p u d", d=32) for k in range(4)]
            p.op('vector', lambda e: e.tensor_tensor(tv[0], x1, cB, ALU.mult), reads=src_ids + ['cos'], writes=['rt0'])
            p.op('vector', lambda e: e.tensor_tensor(tv[1], x2, sB, ALU.mult), reads=src_ids + ['sin'], writes=['rt1'])
            p.op('vector', lambda e: e.tensor_tensor(tv[2], x2, cB, ALU.mult), reads=src_ids + ['cos'], writes=['rt2'])
            p.op('vector', lambda e: e.tensor_tensor(tv[3], x1, sB, ALU.mult), reads=src_ids + ['sin'], writes=['rt3'])
            if not f32:
                db_ = rb_[:, 0:n].rearrange("p (u two d) -> p u two d", two=2, d=32)
                p.op('vector', lambda e: e.tensor_tensor(db_[:, :, 0, :], tv[0], tv[1], ALU.subtract), reads=['rt0', 'rt1'], writes=[rb_.name])
                p.op('vector', lambda e: e.tensor_tensor(db_[:, :, 1, :], tv[2], tv[3], ALU.add), reads=['rt2', 'rt3'], writes=[rb_.name])
                return rf_, rb_
            p.op('vector', lambda e: e.tensor_tensor(d1, tv[0], tv[1], ALU.subtract), reads=['rt0', 'rt1'], writes=[rf_.name + 'a'])
            p.op('vector', lambda e: e.tensor_tensor(d2, tv[2], tv[3], ALU.add), reads=['rt2', 'rt3'], writes=[rf_.name + 'b'])
            p.op('scalar', lambda e: e.activation(rb_[:, 0:n], rf_[:, 0:n], AF.Copy),
                 reads=[rf_.name + 'a', rf_.name + 'b'], writes=[rb_.name])
            return rf_, rb_

        tr_n = {'i': 0}

        def transposes(rb_, nblk, dst_view, dst_ids):
            h = tr_n['i'] % 2
            tr_n['i'] += 1
            pb = psB[:, h * 512:h * 512 + nblk * 128]
            for b in range(nblk):
                p.op('tensor', lambda e, b=b: e.transpose(psB[:, h * 512 + b * 128:h * 512 + (b + 1) * 128],
                                                           rb_[:, b * 128:(b + 1) * 128], ident[:]),
                     reads=[rb_.name, 'ident'], writes=['psB%d' % h, 'ps7'], sig=(b == nblk - 1))
            p.op('vector', lambda e: e.tensor_copy(dst_view, pb.rearrange("p (n t) -> p n t", t=128)),
                 reads=['psB%d' % h, 'ps7'], writes=dst_ids)

        def qknorm(src_ps, src_ids, nh, g_ap, g_id, dst, par):
            n = nh * 64
            tq = (tm2, sq[1])[par]
            o8 = par * 8
            sid, lid, rid_ = 'ss8_%d' % par, 'l8_%d' % par, 'r8_%d' % par
            for h in range(nh):
                p.op('scalar', lambda e, h=h: e.activation(tq[:, h * 64:(h + 1) * 64], src_ps[:, h * 64:(h + 1) * 64], AF.Square,
                                                           accum_out=ss8[:, o8 + h:o8 + h + 1]),
                     reads=src_ids, writes=[tq.name, sid], sig=(h == nh - 1))
            p.op('scalar', lambda e: e.activation(l8[:, o8:o8 + nh], ss8[:, o8:o8 + nh], AF.Ln, bias=EPS, scale=1.0 / 64),
                 reads=[sid], writes=[lid])
            p.op('scalar', lambda e: e.activation(r8[:, o8:o8 + nh], l8[:, o8:o8 + nh], AF.Exp, scale=-0.5), reads=[lid], writes=[rid_])
            for h in range(nh):
                p.op('scalar', lambda e, h=h: e.activation(dst[:, h * 64:(h + 1) * 64], src_ps[:, h * 64:(h + 1) * 64], AF.Identity,
                                                           scale=r8[:, o8 + h:o8 + h + 1]),
                     reads=src_ids + [rid_], writes=[dst.name], sig=(h == nh - 1))
            dv = dst[:, 0:n].rearrange("p (h d) -> p h d", d=64)
            p.op('vector', lambda e: e.tensor_tensor(dv, dv, g_ap.unsqueeze(1).broadcast_to([128, nh, 64]), ALU.mult),
                 reads=[dst.name, g_id], writes=[dst.name])

        SB = [[ps[0], ps[1]], [ps[6], ps[7]]]
        KT_ORDER = [[0, 1, 2, 3, 8, 9, 10, 11, 4, 5, 6, 7], list(range(12))]
        SBN = [['ps0', 'ps1'], ['ps6', 'ps7']]
        PT = [[Pt[0], Pt[1]], [Pt[2], Pt[3]]]

        def attn_pair(kc, qc, qh, kts, v_fn, acc_fn, d_fn, hooks=None):
            hooks = hooks or {}
            n = len(kts)
            qs = slice(qh * 512, (qh + 1) * 512)
            qids = id_qT(qc, qh * 512, (qh + 1) * 512)

            def do_S(i):
                kt = kts[i]
                for u in range(2):
                    rows = slice(u * 64, (u + 1) * 64)
                    p.op('tensor', lambda e: e.matmul(SB[u][i % 2][:], kT[rows, kc, kt * 128:(kt + 1) * 128], qT[rows, qc, qs],
                                                      start=True, stop=True),
                         reads=id_kT(kc, kt * 128, (kt + 1) * 128) + qids, writes=[SBN[u][i % 2]])
            do_S(0)
            if n > 1:
                do_S(1)
            for i in range(n):
                kt = kts[i]
                for u in range(2):
                    P_ = PT[u][i % 2]
                    if kt < 4 or (kt - 4) // 4 != qh:
                        bc = kt * 4 + 2 * qh
                        p.op('scalar', lambda e: e.activation(P_[:], SB[u][i % 2][:], AF.Exp, bias=maskb[:, bc:bc + 1], scale=0.125),
                             reads=[SBN[u][i % 2], 'maskb'], writes=[P_.name])
                    else:
                        p.op('scalar', lambda e: e.activation(P_[:], SB[u][i % 2][:], AF.Exp, scale=0.125),
                             reads=[SBN[u][i % 2]], writes=[P_.name])
                if ('exp', i) in hooks:
                    hooks[('exp', i)](SB[0][i % 2], SBN[0][i % 2])
                if kt >= 4 and (kt - 4) // 4 == qh:
                    c0 = (kt - 4) * 4 + 2 * qh
                    for u in range(2):
                        P_ = PT[u][i % 2]
                        for sq_ in range(2):
                            p.op('vector', lambda e: e.tensor_scalar(P_[:, sq_ * 256:(sq_ + 1) * 256], P_[:, sq_ * 256:(sq_ + 1) * 256],
                                                                     m01[:, c0 + sq_:c0 + sq_ + 1], None, ALU.mult),
                                 reads=[P_.name, 'm01'], writes=[P_.name])
                for u in range(2):
                    if u == 1 and i + 2 < n:
                        do_S(i + 2)
                    P_ = PT[u][i % 2]
                    acc, aid = acc_fn(u)
                    v_ap, v_ids = v_fn(u, kt)
                    dd = d_fn(u) if d_fn is not None else None
                    p.op('tensor', lambda e: e.matmul(acc, v_ap, P_[:], start=(i == 0), stop=(i == n - 1), skip_group_check=True),
                         reads=v_ids + [P_.name], writes=[aid], sig=(dd is None))
                    if dd is not None:
                        p.op('tensor', lambda e: e.matmul(dd[0], ones_b[:], P_[:], start=(i == 0), stop=(i == n - 1), skip_group_check=True),
                             reads=['ones_b', P_.name], writes=[dd[1]], sig=True)
                if ('pv', i) in hooks:
                    hooks[('pv', i)]()

        def load_cache(l):
            p.dma('gpsimd', kT[:, 0:6, 0:PAST], cKT[l].rearrange("(c p) k -> p c k", p=128), 'ldk',
                  writes=sum([id_kT(c, 0, PAST) for c in range(6)], []))
            p.dma('gpsimd', Vv[:, 0:4, :], cV[l].rearrange("(t p) f -> p t f", p=128), 'ldv',
                  writes=rid(R_V, R_V + 4 * VW))
            for kt in range(4, 12):
                ov = Vv[:, kt, 512:1024].rearrange("p (a b d) -> p a b d", a=4, b=2, d=64)[:, :, 1, :]
                p.op('vector', lambda e: e.memset(ov, 1.0), writes=id_V(kt, 512, 1024))

        def attn_A(l):
            li = 0.8 - 0.6 * math.exp(-0.3 * l)
            d1s, d2s, o1c, o2c = tmpf[0], tmpf[1], sq[0], sq[1]
            pend = {}

            def part1(a, qh):
                p.op('scalar', lambda e: e.activation(d1s[:], ps[3][:], AF.Copy), reads=['ps3'], writes=['tmpf0'])
                p.op('scalar', lambda e: e.activation(d2s[:], ps[5][:], AF.Copy), reads=['ps5'], writes=['tmpf1'])
                p.op('vector', lambda e: e.tensor_copy(o1c[:], ps[2][:]), reads=['ps2'], writes=['sq0'])
                p.op('vector', lambda e: e.tensor_copy(o2c[:], ps[4][:]), reads=['ps4'], writes=['sq1'])
                p.op('vector', lambda e: e.reciprocal(ar[:], d1s[:]), reads=['tmpf0'], writes=['ar'])
                p.op('vector', lambda e: e.tensor_tensor(ao1[:], o1c[:], ar[:], ALU.mult), reads=['sq0', 'ar'], writes=['ao1'])
                p.op('vector', lambda e: e.reciprocal(ar[:], d2s[:]), reads=['tmpf1', 'ar'], writes=['ar'])
                p.op('vector', lambda e: e.tensor_tensor(ao2[:], o2c[:], ar[:], ALU.mult), reads=['sq1', 'ar'], writes=['ao2'])
                p.op('vector', lambda e: e.scalar_tensor_tensor(ao1[:], ao2[:], neglam[:, l:l + 1], ao1[:], ALU.mult, ALU.add),
                     reads=['ao1', 'ao2', 'neglam'], writes=['ao1'])

            def part2a(bank, bid):
                p.op('scalar', lambda e: e.activation(sg[:], ao1[:], AF.Square), reads=['ao1'], writes=['sg'])
                p.op('tensor', lambda e: e.matmul(bank[:], ones_f[:], sg[:], start=True, stop=True),
                     reads=['ones_f', 'sg'], writes=[bid])
                p.op('scalar', lambda e: e.activation(lnv[:], bank[:], AF.Ln, bias=EPS, scale=1.0 / 128),
                     reads=[bid], writes=['lnv'])
                p.op('scalar', lambda e: e.activation(rstd[:], lnv[:], AF.Exp, scale=-0.5), reads=['lnv'], writes=['rstd'])

            def part2b(a, qh):
                qs = slice(qh * 512, (qh + 1) * 512)
                p.op('vector', lambda e: e.tensor_tensor(ao2[:], ao1[:], rstd[:], ALU.mult), reads=['ao1', 'rstd'], writes=['ao2'])
                p.op('vector', lambda e: e.tensor_scalar(OT[:, a, qs], ao2[:], sublnT[:, l:l + 1], 1.0 - li, ALU.mult, ALU.mult),
                     reads=['ao2', 'sublnT'], writes=id_OT(a, qh * 512, (qh + 1) * 512))

            prev = None
            for a in range(4):
                for qh in range(2):
                    hooks = {}
                    if prev is not None:
                        hooks[('exp', 5)] = part2a
                        hooks[('pv', 6)] = (lambda pa=prev: part2b(*pa))
                    attn_pair(a, a, qh, KT_ORDER[qh],
                              lambda u, kt: (Vv[:, kt, a * 128:(a + 1) * 128], id_V(kt, a * 128, (a + 1) * 128)),
                              lambda u: (ps[2 + 2 * u][:], 'ps%d' % (2 + 2 * u)),
                              lambda u: (ps[3 + 2 * u][:], 'ps%d' % (3 + 2 * u)), hooks)
                    part1(a, qh)
                    prev = (a, qh)
            part2a(ps[1], 'ps1')
            part2b(*prev)

        def attn_B(l):
            n = 0
            prev = None

            def post(j, qh, ab):
                qs = slice(qh * 512, (qh + 1) * 512)
                for u in range(2):
                    rows = slice(u * 64, (u + 1) * 64)
                    p.op('vector', lambda e: e.reciprocal(ar[0:64, :], ps[ab + u][64:128, :]), reads=['ps%d' % (ab + u)], writes=['ar'])
                    p.op('vector', lambda e: e.tensor_tensor(OT[rows, 4 + j, qs], ps[ab + u][0:64, :], ar[0:64, :], ALU.mult),
                         reads=['ps%d' % (ab + u), 'ar'], writes=id_OT(4 + j, qh * 512, (qh + 1) * 512))
            for j in range(4):
                for qh in range(2):
                    ab = 2 + 2 * (n % 2)
                    n += 1
                    hooks = {}
                    if prev is not None:
                        hooks[('pv', 1)] = (lambda pa=prev: post(*pa))
                    attn_pair(4, j, qh, KT_ORDER[qh],
                              lambda u, kt: (Vv[:, kt, 512 + u * 128:512 + (u + 1) * 128], id_V(kt, 512 + u * 128, 512 + (u + 1) * 128)),
                              lambda u: (ps[ab + u][:], 'ps%d' % (ab + u)), None, hooks)
                    prev = (j, qh, ab)
            post(*prev)

        def attn_C(l):
            cbuf = {(0, 0): tmpf[0], (0, 1): tmpf[1], (1, 0): sq[0], (1, 1): sq[1]}

            def evac(j):
                for qh in range(2):
                    for u in range(2):
                        h = j + 4 * u
                        bk = 2 + 2 * qh + u
                        c_ = cbuf[(qh, u)]
                        col = l * 8 + h
                        if u == 0:
                            p.op('scalar', lambda e: e.activation(c_[:], ps[bk][:], AF.Identity, bias=esink[:, col:col + 1], scale=1.0),
                                 reads=['ps%d' % bk, 'esink'], writes=[c_.name])
                        else:
                            p.op('vector', lambda e: e.tensor_scalar(c_[:], ps[bk][:], esink[:, col:col + 1], None, ALU.add),
                                 reads=['ps%d' % bk, 'esink'], writes=[c_.name])

            def post(j):
                for qh in range(2):
                    qs = slice(qh * 512, (qh + 1) * 512)
                    for u in range(2):
                        rows = slice(u * 64, (u + 1) * 64)
                        c_ = cbuf[(qh, u)]
                        p.op('vector', lambda e: e.reciprocal(ar[0:64, :], c_[64:128, :]), reads=[c_.name], writes=['ar'])
                        p.op('vector', lambda e: e.tensor_tensor(OT[rows, 8 + j, qs], c_[0:64, :], ar[0:64, :], ALU.mult),
                             reads=[c_.name, 'ar'], writes=id_OT(8 + j, qh * 512, (qh + 1) * 512))

            prev = None
            for j in range(4):
                v_fn = lambda u, kt: (Vv[:, kt, 768 + u * 128:768 + (u + 1) * 128], id_V(kt, 768 + u * 128, 768 + (u + 1) * 128))
                for qh in range(2):
                    hooks = {}
                    if qh == 0 and prev is not None:
                        hooks[('pv', 0)] = (lambda pj=prev: post(pj))
                    attn_pair(5, j, qh, list(range(4)), v_fn,
                              lambda u: (ps[2 + 2 * qh + u][:], 'ps%d' % (2 + 2 * qh + u)), None, hooks)

                def c_S(jt):
                    qlo, qhi = max(0, jt - 1) * 128, min(8, jt + 2) * 128
                    N = qhi - qlo
                    for u in range(2):
                        rows = slice(u * 64, (u + 1) * 64)
                        p.op('tensor', lambda e: e.matmul(SB[u][jt % 2][:, 0:N], kT[rows, 5, PAST + jt * 128:PAST + (jt + 1) * 128],
                                                          qT[rows, j, qlo:qhi], start=True, stop=True),
                             reads=id_kT(5, PAST + jt * 128, PAST + (jt + 1) * 128) + id_qT(j, qlo, qhi), writes=[SBN[u][jt % 2]])
                    return (qlo, qhi, N)
                cinfo = {0: c_S(0), 1: c_S(1)}
                for jt in range(8):
                    qlo, qhi, N = cinfo[jt]
                    for u in range(2):
                        P_ = PT[u][jt % 2]
                        p.op('scalar', lambda e: e.activation(P_[:, 0:N], SB[u][jt % 2][:, 0:N], AF.Exp, scale=0.125),
                             reads=[SBN[u][jt % 2]], writes=[P_.name])
                    if jt + 2 < 8:
                        cinfo[jt + 2] = c_S(jt + 2)
                    for u in range(2):
                        P_ = PT[u][jt % 2]
                        if jt >= 1:
                            p.op('vector', lambda e: e.tensor_tensor(P_[:, 0:128], P_[:, 0:128], cmask[:, jt, 0, :], ALU.mult),
                                 reads=[P_.name, 'cmask'], writes=[P_.name])
                        if jt <= 6:
                            p.op('vector', lambda e: e.tensor_tensor(P_[:, N - 128:N], P_[:, N - 128:N], cmask[:, jt, 1, :], ALU.mult),
                                 reads=[P_.name, 'cmask'], writes=[P_.name])
                    for u in range(2):
                        P_ = PT[u][jt % 2]
                        for qh in range(2):
                            lo, hi = max(qlo, qh * 512), min(qhi, (qh + 1) * 512)
                            if lo >= hi:
                                continue
                            osl = slice(lo - qh * 512, hi - qh * 512)
                            psl = slice(lo - qlo, hi - qlo)
                            bk = 2 + 2 * qh + u
                            p.op('tensor', lambda e: e.matmul(ps[bk][:, osl], Vv[:, 4 + jt, 768 + u * 128:768 + (u + 1) * 128], P_[:, psl],
                                                              start=False, stop=False, skip_group_check=True),
                                 reads=id_V(4 + jt, 768 + u * 128, 768 + (u + 1) * 128) + [P_.name], writes=['ps%d' % bk], sig=True)
                evac(j)
                prev = j
            post(prev)

        def qkv_group(l, g):
            Wt, wid = next_block('qkv', l, g)
            pend = []
            for tt in range(8):
                b = 2 + (tt % 4)
                pb = ps[b]
                pid = 'ps%d' % b
                for k in range(8):
                    p.op('tensor', lambda e, k=k, tt=tt, pb=pb: e.matmul(pb[:], hT[:, k, tt * 128:(tt + 1) * 128], Wt[:, k * 512:(k + 1) * 512],
                                                                         start=(k == 0), stop=(k == 7)),
                         reads=['hT%d.%d' % (k, tt // 4), wid], writes=[pid], sig=(k == 7))
                def post(tt=tt, pb=pb, pid=pid):
                    tsl = slice(tt * 128, (tt + 1) * 128)
                    if g == 0:
                        rf_, rb_ = rope_block(pb[:], [pid], 8, tt, f32=False)
                        transposes(rb_, 4, qT[:, 0:4, tsl], sum([id_qT(c, tt * 128, (tt + 1) * 128) for c in range(4)], []))
                    elif g == 1:
                        rf_, rb_ = rope_block(pb[:], [pid], 8, tt)
                        p.dma('sync', nk_a[l, tsl, :], rf_[:], 'o_' + rf_.name, reads=[rf_.name + 'a', rf_.name + 'b'], is_out=True)
                        transposes(rb_, 4, kT[:, 0:4, PAST + tt * 128:PAST + (tt + 1) * 128],
                                   sum([id_kT(c, PAST + tt * 128, PAST + (tt + 1) * 128) for c in range(4)], []))
                    elif g == 2:
                        v_ = vst[tt % 2]
                        KV = os.environ.get('KVAR', '')
                        if 'a' not in KV:
                            p.op('scalar', lambda e, v_=v_, pb=pb: e.activation(v_[:], pb[:], AF.Copy), reads=[pid], writes=[v_.name])
                        if 'b' not in KV:
                            p.dma('sync', nv_a[l, tsl, :], v_[:], 'o_' + v_.name, reads=[v_.name], is_out=True)
                        if 'c' not in KV:
                            p.op('vector', lambda e, v_=v_, tt=tt: e.tensor_copy(Vv[:, 4 + tt, 0:512], v_[:]), reads=[v_.name], writes=id_V(4 + tt, 0, 512))
                    elif g == 3:
                        xq = (xn, sq[0])[tt % 2]
                        qknorm(pb[:], [pid], 8, gqB[:, l * 64:(l + 1) * 64], 'gqB', xq, tt % 2)
                        rf_, rb_ = rope_block(xq[:], [xq.name], 8, tt, f32=False)
                        transposes(rb_, 4, qT[:, 0:4, tsl], sum([id_qT(c, tt * 128, (tt + 1) * 128) for c in range(4)], []))
                    elif g == 4:
                        rf_, rb_ = rope_block(pb[:], [pid], 8, tt, f32=False)
                        transposes(rb_, 4, qT[:, 0:4, tsl], sum([id_qT(c, tt * 128, (tt + 1) * 128) for c in range(4)], []))
                    else:
                        v_ = vst[tt % 2]
                        xq = (xn, sq[0])[tt % 2]
                        p.op('scalar', lambda e, pb=pb: e.activation(xq[:, 128:256], pb[:, 128:256], AF.Copy), reads=[pid], writes=[xq.name])
                        p.op('scalar', lambda e, v_=v_, pb=pb: e.activation(v_[:, 0:256], pb[:, 256:512], AF.Copy), reads=[pid], writes=[v_.name])
                        qknorm(pb[:, 0:128], [pid], 2, gkB[:, l * 64:(l + 1) * 64], 'gkB', xq, tt % 2)
                        rf_, rb_ = rope_block(xq[:, 0:256], [xq.name], 4, tt)
                        p.dma('sync', nk_b[l, tsl, :], rf_[:, 0:128], 'o_' + rf_.name, reads=[rf_.name + 'a', rf_.name + 'b'], is_out=True)
                        p.dma('sync', nk_c[l, tsl, :], rf_[:, 128:256], 'o_' + rf_.name, reads=[rf_.name + 'a', rf_.name + 'b'], is_out=True)
                        transposes(rb_, 2, kT[:, 4:6, PAST + tt * 128:PAST + (tt + 1) * 128],
                                   sum([id_kT(c, PAST + tt * 128, PAST + (tt + 1) * 128) for c in (4, 5)], []))
                        p.dma('sync', nv_b[l, tsl, :], v_[:, 0:128], 'o_' + v_.name, reads=[v_.name], is_out=True)
                        p.dma('sync', nv_c[l, tsl, :], v_[:, 128:256], 'o_' + v_.name, reads=[v_.name], is_out=True)
                        p.op('vector', lambda e, v_=v_, tt=tt: e.tensor_copy(
                            Vv[:, 4 + tt, 512:1024].rearrange("p (a b d) -> p a b d", a=4, b=2, d=64)[:, :, 0, :],
                            v_[:, 0:256].rearrange("p (a d) -> p a d", d=64)), reads=[v_.name],
                             writes=id_V(4 + tt, 512, 1024))
                pend.append(post)
                if len(pend) > 2:
                    pend.pop(0)()
            while pend:
                pend.pop(0)()

        for l in range(depth if not KSTOP else 1):
          try:
            p.epoch = l + 1
            norm_front()
            for jg in range(12):
                if jg == 6:
                    norm_pe()
                Wt, wid = next_block('mod', l, jg)
                for j in range(4):
                    col = jg * 4 + j
                    for k in range(8):
                        p.op('tensor', lambda e, Wt=Wt, j=j, k=k, col=col: e.matmul(
                            ps[6][:, col:col + 1], Wt[:, k * 512 + j * 128:k * 512 + (j + 1) * 128], scb[:, k:k + 1],
                            start=(k == 0), stop=(k == 7)),
                            reads=[wid, 'scb'], writes=['ps6'], sig=(k == 7))
            p.op('vector', lambda e, l=l: e.tensor_tensor(modT[:], ps[6][:, 0:48], bmodT[:, l * 48:(l + 1) * 48], ALU.add),
                 reads=['ps6', 'bmodT'], writes=['modT'])
            p.op('vector', lambda e, l=l: e.scalar_tensor_tensor(a1[:], modT[:, 8:16], 1.0, g1T[:, l * 8:(l + 1) * 8], ALU.add, ALU.mult),
                 reads=['modT', 'g1T'], writes=['a1'])
            p.op('vector', lambda e, l=l: e.scalar_tensor_tensor(a2[:], modT[:, 32:40], 1.0, g2T[:, l * 8:(l + 1) * 8], ALU.add, ALU.mult),
                 reads=['modT', 'g2T'], writes=['a2'])
            sh1, gg1, sh2, gg2 = modT[:, 0:8], modT[:, 16:24], modT[:, 24:32], modT[:, 40:48]
            _chk('mod')

            norm_apply(a1, sh1, ['a1', 'modT'], h_out)
            _chk('norm1')
            load_cache(l)
            _chk('cache')
            qkv_group(l, 0)
            _chk('g0')
            qkv_group(l, 1)
            _chk('g1')
            qkv_group(l, 2)
            _chk('g2')
            attn_A(l)
            _chk('aA')
            qkv_group(l, 3)
            _chk('g3')
            qkv_group(l, 5)
            _chk('g5')
            attn_B(l)
            _chk('aB')
            qkv_group(l, 4)
            attn_C(l)
            _chk('aC')

            for oc in range(8):
                Wt, wid = next_block('mrg', l, oc)
                for th in range(2):
                    tsl = slice(th * 512, (th + 1) * 512)
                    for br in range(3):
                        yb_, gb_ = ps[2 * br], ps[2 * br + 1]
                        for k in range(4):
                            kk = br * 4 + k
                            p.op('tensor', lambda e, yb_=yb_, kk=kk, k=k, br=br: e.matmul(
                                yb_[:], Wt[:, kk * 128:(kk + 1) * 128], OT[:, br * 4 + k, tsl], start=(k == 0), stop=(k == 3)),
                                reads=[wid] + id_OT(br * 4 + k, th * 512, (th + 1) * 512), writes=['ps%d' % (2 * br)], sig=(k == 3))
                        for k in range(8):
                            kk = 12 + br * 8 + k
                            p.op('tensor', lambda e, gb_=gb_, kk=kk, k=k: e.matmul(
                                gb_[:], Wt[:, kk * 128:(kk + 1) * 128], hT[:, k, tsl], start=(k == 0), stop=(k == 7)),
                                reads=[wid, 'hT%d.%d' % (k, th)], writes=['ps%d' % (2 * br + 1)], sig=(k == 7))
                        p.op('scalar', lambda e, gb_=gb_: e.activation(sg[:], gb_[:], AF.Sigmoid), reads=['ps%d' % (2 * br + 1)], writes=['sg'])
                        if br == 0:
                            p.op('vector', lambda e, yb_=yb_: e.tensor_tensor(acc[:], yb_[:], sg[:], ALU.mult),
                                 reads=['ps%d' % (2 * br), 'sg'], writes=['ao1'])
                        else:
                            p.op('vector', lambda e, yb_=yb_: e.tensor_tensor(tm2[:], yb_[:], sg[:], ALU.mult),
                                 reads=['ps%d' % (2 * br), 'sg'], writes=['tm2'])
                            if br == 1:
                                p.op('vector', lambda e: e.tensor_tensor(acc[:], acc[:], tm2[:], ALU.add), reads=['ao1', 'tm2'], writes=['ao1'])
                            else:
                                p.op('vector', lambda e, oc=oc, tsl=tsl: e.tensor_tensor(mT[:, oc, tsl], acc[:], tm2[:], ALU.add),
                                     reads=['ao1', 'tm2'], writes=id_mT(oc, th * 512, (th + 1) * 512))
            for og in range(2):
                Wt, wid = next_block('out', l, og)
                for ocl in range(4):
                    oc = og * 4 + ocl
                    for th in range(2):
                        tsl = slice(th * 512, (th + 1) * 512)
                        pb = ps[(ocl * 2 + th) % 4]
                        pid = 'ps%d' % ((ocl * 2 + th) % 4)
                        for k in range(8):
                            p.op('tensor', lambda e, pb=pb, k=k, ocl=ocl, tsl=tsl: e.matmul(
                                pb[:], Wt[:, (ocl * 8 + k) * 128:(ocl * 8 + k + 1) * 128], mT[:, k, tsl], start=(k == 0), stop=(k == 7)),
                                reads=[wid] + id_mT(k, th * 512, (th + 1) * 512), writes=[pid], sig=(k == 7))
                        p.op('vector', lambda e, pb=pb, oc=oc, tsl=tsl: e.scalar_tensor_tensor(
                            xT[:, oc, tsl], pb[:], gg1[:, oc:oc + 1], xT[:, oc, tsl], ALU.mult, ALU.add),
                            reads=[pid, 'modT', 'xT%d.%d' % (oc, th)], writes=['xT%d.%d' % (oc, th)])
            norm_phase(a2, sh2, ['a2', 'modT'], h_out)
            for jb in range(11):
                Wt, wid = next_block('ffi', l, jb)
                for jj in range(2):
                    j = jb * 2 + jj
                    for th in range(2):
                        tsl = slice(th * 512, (th + 1) * 512)
                        n_ = (jj * 2 + th) % 2
                        pa, pbb = ps[2 * n_], ps[2 * n_ + 1]
                        for k in range(8):
                            p.op('tensor', lambda e, pa=pa, k=k, jj=jj, tsl=tsl: e.matmul(
                                pa[:], Wt[:, ((jj * 2) * 8 + k) * 128:((jj * 2) * 8 + k + 1) * 128], hT[:, k, tsl], start=(k == 0), stop=(k == 7)),
                                reads=[wid, 'hT%d.%d' % (k, th)], writes=['ps%d' % (2 * n_)], sig=(k == 7))
                        for k in range(8):
                            p.op('tensor', lambda e, pbb=pbb, k=k, jj=jj, tsl=tsl: e.matmul(
                                pbb[:], Wt[:, ((jj * 2 + 1) * 8 + k) * 128:((jj * 2 + 1) * 8 + k + 1) * 128], hT[:, k, tsl], start=(k == 0), stop=(k == 7)),
                                reads=[wid, 'hT%d.%d' % (k, th)], writes=['ps%d' % (2 * n_ + 1)], sig=(k == 7))
                        p.op('scalar', lambda e, pa=pa: e.activation(sg[:], pa[:], AF.Silu), reads=['ps%d' % (2 * n_)], writes=['sg'])
                        p.op('vector', lambda e, pbb=pbb, j=j, tsl=tsl: e.tensor_tensor(uT[:, j, tsl], pbb[:], sg[:], ALU.mult),
                             reads=['ps%d' % (2 * n_ + 1), 'sg'], writes=id_uT(j, th * 512, (th + 1) * 512))
            for oc in range(8):
                Wt, wid = next_block('ffo', l, oc)
                for th in range(2):
                    tsl = slice(th * 512, (th + 1) * 512)
                    pb = ps[4 + th]
                    pid = 'ps%d' % (4 + th)
                    for j in range(NFF):
                        p.op('tensor', lambda e, pb=pb, j=j, tsl=tsl: e.matmul(
                            pb[:], Wt[:, j * 128:(j + 1) * 128], uT[:, j, tsl], start=(j == 0), stop=(j == NFF - 1)),
                            reads=[wid] + id_uT(j, th * 512, (th + 1) * 512), writes=[pid], sig=(j == NFF - 1))
                    p.op('vector', lambda e, pb=pb, oc=oc, tsl=tsl: e.scalar_tensor_tensor(
                        xT[:, oc, tsl], pb[:], gg2[:, oc:oc + 1], xT[:, oc, tsl], ALU.mult, ALU.add),
                        reads=[pid, 'modT', 'xT%d.%d' % (oc, th)], writes=['xT%d.%d' % (oc, th)])

          except _Stop:
            pass
        norm_phase(gfinT, zero8, ['gfinT', 'zero8'], y_out)
        p.emit()
    return nc


def _feat_major(v):
    return np.ascontiguousarray(v.reshape(8, 128).T)


def make_inputs(inp, depth=DEPTH):
    f32 = np.float32
    Wp = pack_weights(inp, depth)
    rows = np.repeat(np.arange(16), 64).astype(f32)
    cols = np.tile(np.arange(64), 16).astype(f32)
    inv = (10000.0 ** (-np.arange(16, dtype=f32) / 16)).astype(f32)
    ang = np.concatenate([rows[:, None] * inv, cols[:, None] * inv], axis=-1).astype(f32)
    cos_s, sin_s = np.cos(ang).astype(f32), np.sin(ang).astype(f32)
    cos_p, sin_p = np.ones((T, 32), f32), np.zeros((T, 32), f32)
    mb_p = np.full((12, 4), NEGBIG, f32)
    for t in range(8):
        mb_p[4 + t, t // 2] = 0.0
    mb_s = np.zeros((12, 4), f32)
    m01_s = np.ones((8, 4), f32)
    m01_p = np.zeros((8, 4), f32)
    for t in range(8):
        m01_p[t, t // 2] = 1.0
    kl = np.arange(128)[:, None]
    ql = np.arange(128)[None, :]
    cm_s = np.zeros((128, 8, 2, 128), f32)
    cm_p = np.zeros((128, 8, 2, 128), f32)
    for jt in range(8):
        cm_s[:, jt, 0, :] = (kl <= ql)
        cm_s[:, jt, 1, :] = (kl >= ql)
        cm_p[:, jt, 0, :] = 1.0 if (jt % 2 == 1) else 0.0
        cm_p[:, jt, 1, :] = 1.0 if (jt % 2 == 0) else 0.0
    rep = lambda v: np.ascontiguousarray(np.broadcast_to(v.reshape(1, -1), (128, v.size))).astype(f32)
    common = {
        'W': Wp,
        'ident': np.eye(128, dtype=f32),
        'bmodT': np.concatenate([np.ascontiguousarray(inp['b_mod'][l].reshape(48, 128).T) for l in range(depth)], axis=1),
        'g1T': np.concatenate([_feat_major(inp['norm1_g'][l]) for l in range(depth)], axis=1),
        'g2T': np.concatenate([_feat_major(inp['norm2_g'][l]) for l in range(depth)], axis=1),
        'gfinT': _feat_major(inp['final_g']),
        'sublnT': np.ascontiguousarray(inp['a_subln_g'][:depth].T),
        'gqB': rep(inp['b_qnorm_g'][:depth]),
        'gkB': rep(inp['b_knorm_g'][:depth]),
        'lamB': rep(np.concatenate([inp['a_lam_q1'][:depth].ravel(), inp['a_lam_k1'][:depth].ravel(),
                                    inp['a_lam_q2'][:depth].ravel(), inp['a_lam_k2'][:depth].ravel()])),
        'sinkB': rep(inp['c_sink'][:depth]),
    }
    maps = []
    for core in range(8):
        m = dict(common)
        if core < 4:
            xt = inp['x_prompt'][core * 4:(core + 1) * 4].reshape(T, D_MODEL)
            m['cvec'] = _feat_major(inp['c_ctx'])
            m['cKT'] = np.zeros((depth, 768, PAST), f32)
            m['cV'] = np.zeros((depth, PAST, VW), f32)
            m['m01'] = rep(m01_p)
            m['cos_t'], m['sin_t'] = cos_p, sin_p
            m['maskb'] = rep(mb_p)
            m['cmask'] = cm_p.reshape(128, -1)
        else:
            b = core - 4
            xt = inp['x_sample'][b]
            m['cvec'] = _feat_major(inp['c'][b])
            ck = np.concatenate([inp['cache_a_k'][b, :depth].reshape(depth, PAST, 512),
                                 inp['cache_b_k'][b, :depth].reshape(depth, PAST, 128),
                                 inp['cache_c_k'][b, :depth].reshape(depth, PAST, 128)], axis=-1)
            m['cKT'] = np.ascontiguousarray(ck.transpose(0, 2, 1))
            cv = np.ones((depth, PAST, VW), f32)
            cv[:, :, 0:512] = inp['cache_a_v'][b, :depth].reshape(depth, PAST, 512)
            for kv in range(2):
                cv[:, :, 512 + kv * 128:512 + kv * 128 + 64] = inp['cache_b_v'][b, :depth, :, kv, :]
                cv[:, :, 768 + kv * 128:768 + kv * 128 + 64] = inp['cache_c_v'][b, :depth, :, kv, :]
            m['cV'] = cv
            m['m01'] = rep(m01_s)
            m['cos_t'], m['sin_t'] = cos_s, sin_s
            m['maskb'] = rep(mb_s)
            m['cmask'] = cm_s.reshape(128, -1)
        m['xT_in'] = np.ascontiguousarray(xt.T)
        maps.append(m)
    return maps


_NC_CACHE = {}


def run(inp, depth=DEPTH):
    inp = {k: np.asarray(v) for k, v in inp.items()}
    if depth not in _NC_CACHE:
        _NC_CACHE[depth] = build_program(depth)
    nc = _NC_CACHE[depth]
    maps = make_inputs(inp, depth)
    res = run_bass_kernel_spmd(nc, maps, core_ids=list(range(8)))
    r = res.results
    f32 = np.float32
    y_prompt = np.concatenate([r[c]['yT'].T.reshape(4, 256, D_MODEL) for c in range(4)], axis=0).astype(f32)
    y_sample = np.stack([r[4 + b]['yT'].T for b in range(4)], axis=0).astype(f32)

    def gather(name, hd, dd):
        outs = []
        for c in range(4):
            a = r[c][name].reshape(depth, 4, 256, hd, dd).transpose(1, 0, 2, 3, 4)
            outs.append(a)
        return np.ascontiguousarray(np.concatenate(outs, axis=0)).astype(f32)
    return (y_prompt, y_sample,
            gather('nk_a', 4, 128), gather('nv_a', 4, 128),
            gather('nk_b', 2, 64), gather('nv_b', 2, 64),
            gather('nk_c', 2, 64), gather('nv_c', 2, 64))


def kernel(**inputs):
    return run(inputs, DEPTH)
```
